# Optimizing a Trainium2 kernel written in Bass

```python
import jax, jax.numpy as jnp
from jax import lax
import numpy as np

D_MODEL = 1024
BATCH = 2
SEQ = 8192
DEPTH = 4

GRID_W = 64
CTX_LEN = 256
N_MIXERS = 3
ROPE_THETA = 10000.0
NORM_EPS = 1e-6
Q_BLOCK = 128

GQA_HEAD_DIM = 128
GQA_Q_HEADS = D_MODEL // 64
GQA_KV_HEADS = GQA_Q_HEADS // 2
GQA_GROUP = GQA_Q_HEADS // GQA_KV_HEADS

HGRN_DK = 128
HGRN_DV = 128
HGRN_HEADS = D_MODEL // HGRN_DK
HGRN_WIDTH = HGRN_HEADS * HGRN_DK
HGRN_CHUNK = 64

MLA_HEADS = D_MODEL // 64
MLA_NOPE = 64
MLA_ROPE = 32
MLA_V = 64
MLA_Q_LORA = 3 * D_MODEL // 4
MLA_KV_LORA = D_MODEL // 4

D_FF = 3 * D_MODEL
CONV_W = 3

N_GQA = (DEPTH + 2) // 3
N_HGRN = (DEPTH + 1) // 3
N_MLA = DEPTH // 3

kernel_name = 'hybrid_gqa_hgrn2_mla_convffn_prefix_dit'


def rms_norm(x, gain):
    x32 = x.astype(jnp.float32)
    y = x32 * lax.rsqrt(jnp.mean(x32 * x32, axis=-1, keepdims=True) + NORM_EPS)
    return (y * gain.astype(jnp.float32)).astype(x.dtype)


def modulate(x, shift, scale):
    return x * (1.0 + scale) + shift


def axial_rope_tables(rows, rot_dim):
    row = jnp.repeat(jnp.arange(rows, dtype=jnp.float32), GRID_W)
    col = jnp.tile(jnp.arange(GRID_W, dtype=jnp.float32), rows)
    axis_dim = rot_dim // 2
    inv_freq = jnp.power(ROPE_THETA, -jnp.arange(0, axis_dim, 2, dtype=jnp.float32) / axis_dim)
    ang = jnp.concatenate([row[:, None] * inv_freq, col[:, None] * inv_freq], axis=-1)
    return jnp.cos(ang), jnp.sin(ang)


def apply_rope(x, cos, sin):
    half = x.shape[-1] // 2
    x32 = x.astype(jnp.float32)
    x1, x2 = x32[..., :half], x32[..., half:]
    return jnp.concatenate([x1 * cos - x2 * sin, x1 * sin + x2 * cos], axis=-1).astype(x.dtype)


def block_attention(q, k, v, scale):
    B, Nq, Hk, G, Dk = q.shape
    nb = Nq // Q_BLOCK
    qb = q.reshape(B, nb, Q_BLOCK, Hk, G, Dk).swapaxes(0, 1)

    def one_block(q_blk):
        s = jnp.einsum('bqhgd,bkhd->bhgqk', q_blk, k).astype(jnp.float32) * scale
        p = jax.nn.softmax(s, axis=-1).astype(v.dtype)
        return jnp.einsum('bhgqk,bkhd->bqhgd', p, v)

    o = lax.map(one_block, qb)
    return o.swapaxes(0, 1).reshape(B, Nq, Hk, G, v.shape[-1])


def dwconv_centered(u, w, b):
    L = u.shape[1]
    pad = CONV_W // 2
    up = jnp.pad(u, ((0, 0), (pad, CONV_W - 1 - pad), (0, 0)))
    out = b
    for j in range(CONV_W):
        out = out + up[:, j:j + L] * w[j]
    return out


def conv_ffn(h, w_in, w_conv, b_conv, w_out):
    u = dwconv_centered(h @ w_in, w_conv, b_conv)
    a, val = jnp.split(u, 2, axis=-1)
    return (jax.nn.silu(a) * val) @ w_out


def gqa_mixer(h, hc, w_in, q_gain, k_gain, w_out, cos, sin, ctx_out):
    qd = GQA_Q_HEADS * GQA_HEAD_DIM
    kd = GQA_KV_HEADS * GQA_HEAD_DIM

    def project(t, rotate):
        B, L, _ = t.shape
        p = t @ w_in
        q = rms_norm(p[..., :qd].reshape(B, L, GQA_Q_HEADS, GQA_HEAD_DIM), q_gain)
        k = rms_norm(p[..., qd:qd + kd].reshape(B, L, GQA_KV_HEADS, GQA_HEAD_DIM), k_gain)
        v = p[..., qd + kd:].reshape(B, L, GQA_KV_HEADS, GQA_HEAD_DIM)
        if rotate:
            q = apply_rope(q, cos[:, None, :], sin[:, None, :])
            k = apply_rope(k, cos[:, None, :], sin[:, None, :])
        return q.reshape(B, L, GQA_KV_HEADS, GQA_GROUP, GQA_HEAD_DIM), k, v

    def merge(o):
        B, L = o.shape[:2]
        return o.reshape(B, L, qd) @ w_out

    scale = GQA_HEAD_DIM ** -0.5
    q, k, v = project(h, True)
    qc, kc, vc = project(hc, False)
    y = merge(block_attention(q, jnp.concatenate([k, kc], axis=1), jnp.concatenate([v, vc], axis=1), scale))
    yc = merge(block_attention(qc, kc, vc, scale)) if ctx_out else None
    return y, yc


def mla_mixer(h, hc, w_in, q_gain, kv_gain, w_qb, w_kvb, w_out, cos, sin, ctx_out):
    def project(t, rotate):
        B, L, _ = t.shape
        p = t @ w_in
        cq = rms_norm(p[..., :MLA_Q_LORA], q_gain)
        ckv = rms_norm(p[..., MLA_Q_LORA:MLA_Q_LORA + MLA_KV_LORA], kv_gain)
        k_rope = p[..., MLA_Q_LORA + MLA_KV_LORA:]
        q = (cq @ w_qb).reshape(B, L, MLA_HEADS, MLA_NOPE + MLA_ROPE)
        kv = (ckv @ w_kvb).reshape(B, L, MLA_HEADS, MLA_NOPE + MLA_V)
        q_nope, q_rope = q[..., :MLA_NOPE], q[..., MLA_NOPE:]
        k_nope, v = kv[..., :MLA_NOPE], kv[..., MLA_NOPE:]
        if rotate:
            q_rope = apply_rope(q_rope, cos[:, None, :], sin[:, None, :])
            k_rope = apply_rope(k_rope, cos, sin)
        q = jnp.concatenate([q_nope, q_rope], axis=-1)[:, :, :, None, :]
        k = jnp.concatenate([k_nope, jnp.broadcast_to(k_rope[:, :, None, :], (B, L, MLA_HEADS, MLA_ROPE))], axis=-1)
        return q, k, v

    def merge(o):
        B, L = o.shape[:2]
        return o.reshape(B, L, MLA_HEADS * MLA_V) @ w_out

    scale = (MLA_NOPE + MLA_ROPE) ** -0.5
    q, k, v = project(h, True)
    qc, kc, vc = project(hc, False)
    y = merge(block_attention(q, jnp.concatenate([k, kc], axis=1), jnp.concatenate([v, vc], axis=1), scale))
    yc = merge(block_attention(qc, kc, vc, scale)) if ctx_out else None
    return y, yc


def gla_chunk_scan(q, k, v, log_f, state0):
    B, L, H, _ = q.shape
    dv = v.shape[-1]
    n = L // HGRN_CHUNK

    def to_chunks(a):
        return a.reshape(B, n, HGRN_CHUNK, H, a.shape[-1]).transpose(1, 0, 3, 2, 4)

    incl = jnp.tril(jnp.ones((HGRN_CHUNK, HGRN_CHUNK), dtype=bool))[:, :, None]

    def step(state, inp):
        qc, kc, vc, gc = inp
        cum = jnp.cumsum(gc, axis=2)
        diff = cum[:, :, :, None, :] - cum[:, :, None, :, :]
        decay = jnp.where(incl, jnp.exp(jnp.where(incl, diff, 0.0)), 0.0)
        scores = jnp.einsum('bhtd,bhsd,bhtsd->bhts', qc, kc, decay)
        out = jnp.einsum('bhts,bhsv->bhtv', scores, vc) + jnp.einsum('bhtd,bhdv->bhtv', qc * jnp.exp(cum), state)
        cum_end = cum[:, :, -1, :]
        state = jnp.exp(cum_end)[..., None] * state + jnp.einsum(
            'bhsd,bhsv->bhdv', kc * jnp.exp(cum_end[:, :, None, :] - cum), vc)
        return state, out

    state, out = lax.scan(step, state0, (to_chunks(q), to_chunks(k), to_chunks(v), to_chunks(log_f)))
    return out.transpose(1, 0, 3, 2, 4).reshape(B, L, H, dv), state


def hgrn_mixer(h, hc, w_in, o_gain, w_out, lower_bound, ctx_out):
    lb = lower_bound.reshape(HGRN_HEADS, HGRN_DK)
    scale = HGRN_DK ** -0.5

    def project(t):
        B, L, _ = t.shape
        p = (t @ w_in).astype(jnp.float32).reshape(B, L, 5, HGRN_HEADS, HGRN_DK)
        q, inp, f_fwd_pre, f_bwd_pre, gate = p[:, :, 0], p[:, :, 1], p[:, :, 2], p[:, :, 3], p[:, :, 4]
        f_fwd = lb + (1.0 - lb) * jax.nn.sigmoid(f_fwd_pre)
        f_bwd = lb + (1.0 - lb) * jax.nn.sigmoid(f_bwd_pre)
        return q * scale, inp, 1.0 - f_fwd, jnp.log(f_fwd), 1.0 - f_bwd, jnp.log(f_bwd), gate

    def flip(a):
        return jnp.flip(a, axis=1)

    def readout(o, gate, dtype):
        B, L = o.shape[:2]
        return (rms_norm(o, o_gain) * jax.nn.silu(gate)).reshape(B, L, HGRN_WIDTH).astype(dtype) @ w_out

    q, i_, kf, gf, kb, gb, g = project(h)
    qc, ic, kfc, gfc, kbc, gbc, gc = project(hc)
    state0 = jnp.zeros((h.shape[0], HGRN_HEADS, HGRN_DK, HGRN_DV), jnp.float32)
    oc_f, s_f = gla_chunk_scan(qc, kfc, ic, gfc, state0)
    oc_b, s_b = gla_chunk_scan(flip(qc), flip(kbc), flip(ic), flip(gbc), state0)
    o_f, _ = gla_chunk_scan(q, kf, i_, gf, s_f)
    o_b, _ = gla_chunk_scan(flip(q), flip(kb), flip(i_), flip(gb), s_b)
    y = readout(o_f + flip(o_b), g, h.dtype)
    yc = readout(oc_f + flip(oc_b), gc, hc.dtype) if ctx_out else None
    return y, yc


def setup_inputs(seed: int = 0) -> dict:
    key = jax.random.key(seed)
    ks = jax.random.split(key, 28)

    def nrm(k, shape, s):
        return jax.random.normal(k, shape, jnp.float32) * s

    D = D_MODEL
    return {
        'x': nrm(ks[0], (BATCH, SEQ, D), 1.0),
        'c': nrm(ks[1], (BATCH, D), 1.0),
        'ctx': nrm(ks[2], (BATCH, CTX_LEN, D), 1.0),
        'c_ctx': nrm(ks[3], (D,), 1.0),
        'w_ada': nrm(ks[4], (DEPTH, D, 6 * D), 0.5 * D ** -0.5),
        'b_ada': nrm(ks[5], (DEPTH, 6 * D), 0.01),
        'norm_mix': 1.0 + nrm(ks[6], (DEPTH, D), 0.02),
        'norm_ffn': 1.0 + nrm(ks[7], (DEPTH, D), 0.02),
        'ffn_w_in': nrm(ks[8], (DEPTH, D, 2 * D_FF), D ** -0.5),
        'ffn_conv_w': nrm(ks[9], (DEPTH, CONV_W, 2 * D_FF), CONV_W ** -0.5),
        'ffn_conv_b': nrm(ks[10], (DEPTH, 2 * D_FF), 0.01),
        'ffn_w_out': nrm(ks[11], (DEPTH, D_FF, D), D_FF ** -0.5),
        'gqa_w_in': nrm(ks[12], (N_GQA, D, (GQA_Q_HEADS + 2 * GQA_KV_HEADS) * GQA_HEAD_DIM), D ** -0.5),
        'gqa_q_norm': 1.0 + nrm(ks[13], (N_GQA, GQA_HEAD_DIM), 0.02),
        'gqa_k_norm': 1.0 + nrm(ks[14], (N_GQA, GQA_HEAD_DIM), 0.02),
        'gqa_w_out': nrm(ks[15], (N_GQA, GQA_Q_HEADS * GQA_HEAD_DIM, D), (GQA_Q_HEADS * GQA_HEAD_DIM) ** -0.5),
        'hgrn_w_in': nrm(ks[16], (N_HGRN, D, 5 * HGRN_WIDTH), D ** -0.5),
        'hgrn_out_norm': 1.0 + nrm(ks[17], (N_HGRN, HGRN_DV), 0.02),
        'hgrn_w_out': nrm(ks[18], (N_HGRN, HGRN_WIDTH, D), HGRN_WIDTH ** -0.5),
        'hgrn_lower_bounds': nrm(ks[19], (DEPTH, HGRN_WIDTH), 0.5),
        'mla_w_in': nrm(ks[20], (N_MLA, D, MLA_Q_LORA + MLA_KV_LORA + MLA_ROPE), D ** -0.5),
        'mla_q_norm': 1.0 + nrm(ks[21], (N_MLA, MLA_Q_LORA), 0.02),
        'mla_kv_norm': 1.0 + nrm(ks[22], (N_MLA, MLA_KV_LORA), 0.02),
        'mla_w_qb': nrm(ks[23], (N_MLA, MLA_Q_LORA, MLA_HEADS * (MLA_NOPE + MLA_ROPE)), MLA_Q_LORA ** -0.5),
        'mla_w_kvb': nrm(ks[24], (N_MLA, MLA_KV_LORA, MLA_HEADS * (MLA_NOPE + MLA_V)), MLA_KV_LORA ** -0.5),
        'mla_w_out': nrm(ks[25], (N_MLA, MLA_HEADS * MLA_V, D), (MLA_HEADS * MLA_V) ** -0.5),
        'final_norm': 1.0 + nrm(ks[26], (D,), 0.02),
    }


def reference(x, c, ctx, c_ctx, w_ada, b_ada, norm_mix, norm_ffn, ffn_w_in, ffn_conv_w, ffn_conv_b, ffn_w_out,
              gqa_w_in, gqa_q_norm, gqa_k_norm, gqa_w_out, hgrn_w_in, hgrn_out_norm, hgrn_w_out, hgrn_lower_bounds,
              mla_w_in, mla_q_norm, mla_kv_norm, mla_w_qb, mla_w_kvb, mla_w_out, final_norm):
    rows = x.shape[1] // GRID_W
    cos_a, sin_a = axial_rope_tables(rows, GQA_HEAD_DIM)
    cos_m, sin_m = axial_rope_tables(rows, MLA_ROPE)
    lb_all = jnp.cumsum(jax.nn.softmax(hgrn_lower_bounds.astype(jnp.float32), axis=0), axis=0)
    lb_all = lb_all - lb_all[0]
    silu_c = jax.nn.silu(c)
    silu_cc = jax.nn.silu(c_ctx)

    for i in range(DEPTH):
        last = i == DEPTH - 1
        kind = i % N_MIXERS
        j = i // N_MIXERS
        mod = (silu_c @ w_ada[i] + b_ada[i])[:, None, :]
        mod_c = silu_cc @ w_ada[i] + b_ada[i]
        sh_a, sc_a, g_a, sh_f, sc_f, g_f = jnp.split(mod, 6, axis=-1)
        csh_a, csc_a, cg_a, csh_f, csc_f, cg_f = jnp.split(mod_c, 6, axis=-1)

        h = modulate(rms_norm(x, norm_mix[i]), sh_a, sc_a)
        hc = modulate(rms_norm(ctx, norm_mix[i]), csh_a, csc_a)
        if kind == 0:
            y, yc = gqa_mixer(h, hc, gqa_w_in[j], gqa_q_norm[j], gqa_k_norm[j], gqa_w_out[j],
                              cos_a, sin_a, not last)
        elif kind == 1:
            y, yc = hgrn_mixer(h, hc, hgrn_w_in[j], hgrn_out_norm[j], hgrn_w_out[j], lb_all[i], not last)
        else:
            y, yc = mla_mixer(h, hc, mla_w_in[j], mla_q_norm[j], mla_kv_norm[j], mla_w_qb[j], mla_w_kvb[j],
                              mla_w_out[j], cos_m, sin_m, not last)
        x = x + g_a * y
        x = x + g_f * conv_ffn(modulate(rms_norm(x, norm_ffn[i]), sh_f, sc_f),
                               ffn_w_in[i], ffn_conv_w[i], ffn_conv_b[i], ffn_w_out[i])
        if not last:
            ctx = ctx + cg_a * yc
            ctx = ctx + cg_f * conv_ffn(modulate(rms_norm(ctx, norm_ffn[i]), csh_f, csc_f),
                                        ffn_w_in[i], ffn_conv_w[i], ffn_conv_b[i], ffn_w_out[i])

    return rms_norm(x, final_norm)
```

```python
import numpy as np
from contextlib import ExitStack
import concourse.bass as bass
import concourse.mybir as mybir
from concourse.bass_utils import run_bass_kernel_spmd

F32 = mybir.dt.float32
BF16 = mybir.dt.bfloat16
AF = mybir.ActivationFunctionType
ALU = mybir.AluOpType
AX = mybir.AxisListType

ENGS = ('pe', 'act', 'dve', 'pool', 'sp')
NDSEM = 24


class Buf:
    __slots__ = ('t', 'name', 'w', 'r')

    def __init__(self, t, name=''):
        self.t = t
        self.name = name
        self.w = None
        self.r = {}

    def __getitem__(self, k):
        return self.t[k]


class Sched:
    def __init__(self, nc, es):
        self.nc = nc
        self.es = es
        self.sem = {}
        self.cnt = {}
        for e in ENGS:
            self.sem[e] = es.enter_context(nc.semaphore('s_' + e))
            self.cnt[e] = 0
        for i in range(NDSEM):
            k = ('d', i)
            self.sem[k] = es.enter_context(nc.semaphore('sd%d' % i))
            self.cnt[k] = 0
        self.rr = 0
        self.rr2 = 0
        self.q = {e: [] for e in ENGS}
        self.seen = {e: {} for e in ENGS}
        self.nops = 0

    def sbuf(self, es, name, shape, dtype):
        self.uid = getattr(self, 'uid', 0) + 1
        t = es.enter_context(self.nc.sbuf_tensor("sb%d_%s" % (self.uid, name), list(shape), dtype))
        return Buf(t, name)

    def psum(self, es, name, shape=(128, 512), dtype=F32):
        t = es.enter_context(self.nc.psum_tensor("pp_" + name, list(shape), dtype))
        return Buf(t, name)

    def _waits(self, eng, reads, writes):
        deps = {}
        for b in reads:
            if b.w is not None and deps.get(b.w[0], 0) < b.w[1]:
                deps[b.w[0]] = b.w[1]
        for b in writes:
            if b.w is not None and deps.get(b.w[0], 0) < b.w[1]:
                deps[b.w[0]] = b.w[1]
            for k, v in b.r.items():
                if deps.get(k, 0) < v:
                    deps[k] = v
        waits = []
        seen = self.seen[eng]
        for k, v in deps.items():
            if k == 'pe' and eng == 'pe':
                continue
            if seen.get(k, 0) < v:
                seen[k] = v
                waits.append((k, v))
        return waits

    def op(self, eng, fn, reads=(), writes=(), inc=True):
        waits = self._waits(eng, reads, writes)
        n = self.cnt[eng] + 1
        if inc:
            self.cnt[eng] = n
        self.q[eng].append((waits, fn, (eng, 1) if inc else None))
        for b in reads:
            b.r[eng] = n
        for b in writes:
            b.w = (eng, n)
            b.r = {}
        self.nops += 1

    def dma(self, eng, out, in_, reads=(), writes=()):
        waits = self._waits(eng, reads, writes)
        if eng == 'sp':
            k = ('d', self.rr % 16)
            self.rr += 1
        else:
            k = ('d', 16 + self.rr2 % (NDSEM - 16))
            self.rr2 += 1
        if self.cnt[k] > 0 and self.seen[eng].get(k, 0) < self.cnt[k]:
            self.seen[eng][k] = self.cnt[k]
            waits.append((k, self.cnt[k]))
        self.cnt[k] += 16
        n = self.cnt[k]
        self.q[eng].append((waits, lambda e: e.dma_start(out=out, in_=in_), (k, 16)))
        for b in reads:
            b.r[k] = n
        for b in writes:
            b.w = (k, n)
            b.r = {}
        self.nops += 1

    def barrier(self):
        for e in ENGS:
            waits = []
            for k, v in self.cnt.items():
                if v > 0 and self.seen[e].get(k, 0) < v and k != e:
                    self.seen[e][k] = v
                    waits.append((k, v))
            if waits:
                self.q[e].append((waits, None, None))

    def flush(self):
        nc = self.nc
        sem = self.sem
        with nc.Block() as block:
            for e, deco in (('pe', block.tensor), ('act', block.scalar), ('dve', block.vector),
                            ('pool', block.gpsimd), ('sp', block.sync)):
                items = self.q[e]

                def body(eng, items=items):
                    for waits, fn, inc in items:
                        for k, v in waits:
                            eng.wait_ge(sem[k], v)
                        if fn is not None:
                            ins = fn(eng)
                            if inc is not None:
                                ins.then_inc(sem[inc[0]], inc[1])
                deco(body)
        self.q = {e: [] for e in ENGS}

    def mm(self, out, lhsT, rhs, start, stop, reads=(), writes=(), inc=None):
        if inc is None:
            inc = stop
        self.op('pe', lambda e: e.matmul(out, lhsT, rhs, start=start, stop=stop), reads, writes, inc=inc)

    def act(self, out, in_, func, reads=(), writes=(), bias=None, scale=None, eng='act'):
        kw = {}
        if bias is not None:
            kw['bias'] = bias
        if scale is not None:
            kw['scale'] = scale
        self.op('act', lambda e: e.activation(out, in_, func, **kw), reads, writes)

    def ts(self, eng, out, in0, s1, s2, op0, op1=None, reads=(), writes=()):
        if op1 is None:
            self.op(eng, lambda e: e.tensor_scalar(out, in0, s1, None, op0), reads, writes)
        else:
            self.op(eng, lambda e: e.tensor_scalar(out, in0, s1, s2, op0, op1), reads, writes)

    def tt(self, eng, out, in0, in1, op, reads=(), writes=()):
        self.op(eng, lambda e: e.tensor_tensor(out, in0, in1, op), reads, writes)

    def stt(self, eng, out, in0, scalar, in1, op0, op1, reads=(), writes=()):
        self.op(eng, lambda e: e.scalar_tensor_tensor(out, in0, scalar, in1, op0, op1), reads, writes)

    def copy(self, eng, out, in_, reads=(), writes=()):
        if eng == 'act':
            self.op('act', lambda e: e.activation(out, in_, AF.Copy), reads, writes)
        else:
            self.op(eng, lambda e: e.tensor_copy(out, in_), reads, writes)

    def memset(self, eng, out, val, writes=()):
        self.op(eng, lambda e: e.memset(out, val), (), writes)


class Ring:
    def __init__(self, bufs):
        self.bufs = bufs
        self.i = 0

    def next(self):
        b = self.bufs[self.i % len(self.bufs)]
        self.i += 1
        return b


D = 1024
SEQ = 8192
CTX = 256
NB = 2
DEPTH = 4
TL = 2048
T = TL + CTX
NK = SEQ + CTX
EPS = 1e-6
DFF = 3072
TILES = [(0, 512, 0), (512, 512, 0), (1024, 512, 0), (1536, 512, 0), (2048, 256, 1)]


def new_nc():
    return bass.Bass("TRN2", target_bir_lowering=False)


def din(nc, name, shape, dtype=F32):
    return nc.dram_tensor(name, list(shape), dtype, kind="ExternalInput").ap()


def dout(nc, name, shape, dtype=F32):
    return nc.dram_tensor(name, list(shape), dtype, kind="ExternalOutput").ap()


class Ctx:
    def __init__(self, nc, es, npsum=8):
        self.nc = nc
        self.es = es
        self.S = Sched(nc, es)
        S = self.S
        self.ps = Ring([S.psum(es, "ps%d" % i) for i in range(npsum)])
        self.ones = S.sbuf(es, "ones_f", [128, 128], F32)
        S.memset('dve', self.ones[:], 1.0, writes=[self.ones])
        self.ones_bf = S.sbuf(es, "ones_b", [128, 128], BF16)
        S.memset('pool', self.ones_bf[:], 1.0, writes=[self.ones_bf])
        self.eps = S.sbuf(es, "eps_t", [128, 1], F32)
        S.memset('dve', self.eps[:], EPS, writes=[self.eps])
        self.cast_i = 0

    def finish(self):
        self.S.barrier()
        self.S.flush()


def emit_mod(cx, es_outer, cvec_d, w_d, b_d, ns, ncol):
    S = cx.S
    mod = S.sbuf(es_outer, "mod", [128, ns, 8, ncol], F32)
    with ExitStack() as es:
        craw = S.sbuf(es, "craw", [128, 8, ncol], F32)
        sc = S.sbuf(es, "silu_c", [128, 8, ncol], F32)
        bada = S.sbuf(es, "bada", [128, ns * 8], F32)
        wb = Ring([S.sbuf(es, "wada%d" % i, [128, 8, 1024], F32) for i in range(2)])
        S.dma('sp', craw[:], cvec_d[:, :, :], writes=[craw])
        S.dma('sp', bada[:], b_d[:, :], writes=[bada])
        S.act(sc[:], craw[:], AF.Silu, reads=[craw], writes=[sc])
        for s in range(ns):
            w = wb.next()
            src = w_d[:, s * 1024:(s + 1) * 1024].rearrange("(c p) o -> p c o", p=128)
            S.dma('sp', w[:, 0:4, :], src[:, 0:4, :], writes=[w])
            S.dma('sp', w[:, 4:8, :], src[:, 4:8, :], writes=[w])
            ps = cx.ps.next()
            for o in range(8):
                for kc in range(8):
                    last = (o == 7 and kc == 7)
                    S.mm(ps[:, o * ncol:(o + 1) * ncol], w[:, kc, o * 128:(o + 1) * 128], sc[:, kc, :], kc == 0, kc == 7,
                         reads=[w, sc], writes=[ps], inc=last)
            S.tt('dve', mod[:, s, :, :], ps[:, 0:8 * ncol].rearrange("p (c j) -> p c j", j=ncol),
                 bada[:, s * 8:(s + 1) * 8].unsqueeze(2).broadcast_to([128, 8, ncol]), ALU.add,
                 reads=[ps, bada], writes=[mod])
        S.barrier()
        S.flush()
    return mod


def build_mod():
    nc = new_nc()
    cvec = din(nc, "cvec", [128, 8, 3])
    w_d = din(nc, "w", [D, 3 * D])
    b_d = din(nc, "b", [128, 24])
    lbin = din(nc, "lbin", [128, 8, 4])
    mod_o = dout(nc, "mod", [128, 3, 8, 3])
    lb_o = dout(nc, "lb", [128, 8])
    with ExitStack() as es:
        cx = Ctx(nc, es)
        S = cx.S
        mod = emit_mod(cx, es, cvec, w_d, b_d, 3, 3)
        S.dma('pool', mod_o[:, :, :, :], mod[:], reads=[mod])
        l0 = S.sbuf(es, "l0", [128, 8, 4], F32)
        l1 = S.sbuf(es, "l1", [128, 8, 4], F32)
        l2 = S.sbuf(es, "l2", [128, 8], F32)
        l3 = S.sbuf(es, "l3", [128, 8], F32)
        l4 = S.sbuf(es, "l4", [128, 8], F32)
        S.dma('sp', l0[:], lbin[:, :, :], writes=[l0])
        S.act(l1[:], l0[:], AF.Exp, reads=[l0], writes=[l1])
        S.op('dve', lambda e: e.tensor_reduce(l2[:], l1[:], AX.X, ALU.add), reads=[l1], writes=[l2])
        S.op('dve', lambda e: e.reciprocal(l3[:], l2[:]), reads=[l2], writes=[l3])
        S.tt('dve', l4[:], l1[:, :, 1], l3[:], ALU.mult, reads=[l1, l3], writes=[l4])
        S.dma('pool', lb_o[:, :], l4[:], reads=[l4])
        cx.finish()
    return nc


def launch_mod(inp):
    nc = build_mod()
    c, c_ctx = inp['c'], inp['c_ctx']
    cvec = np.ascontiguousarray(np.stack([fm(c[0]), fm(c[1]), fm(c_ctx)], axis=-1))
    lbin = np.ascontiguousarray(inp['hgrn_lower_bounds'].T.reshape(8, 128, 4).transpose(1, 0, 2))
    maps = []
    for r in range(NCORES):
        layer, hf = r // 2, r % 2
        w = np.ascontiguousarray(inp['w_ada'][layer][:, hf * 3 * D:(hf + 1) * 3 * D])
        b = fm(inp['b_ada'][layer][hf * 3 * D:(hf + 1) * 3 * D])
        maps.append(dict(cvec=cvec, w=w, b=b, lbin=lbin))
    res = run(nc, maps)
    mods = []
    for layer in range(DEPTH):
        full = np.concatenate([res[2 * layer]['mod'], res[2 * layer + 1]['mod']], axis=1)
        per_b = [np.ascontiguousarray(full[:, :, :, [b, 2]]) for b in range(NB)]
        mods.append([per_b[r // 4] for r in range(NCORES)])
    return mods, np.ascontiguousarray(res[0]['lb'])


def emit_affine(cx, es, mod, s_shift, s_scale, gain_d, name):
    S = cx.S
    gain = S.sbuf(es, name + "_g", [128, 8], F32)
    A = S.sbuf(es, name + "_A", [128, 8, 2], F32)
    S.dma('sp', gain[:], gain_d[:, :], writes=[gain])
    S.stt('dve', A[:], mod[:, s_scale, :, :], 1.0, gain[:, :].unsqueeze(2).broadcast_to([128, 8, 2]), ALU.add, ALU.mult,
          reads=[mod, gain], writes=[A])
    return A


def emit_norm_mod(cx, es, x_src, A, mod, s_shift, h, tiles=TILES, xkeep=None, tag="nm"):
    S = cx.S
    sq = Ring([S.sbuf(es, tag + "_sq%d" % i, [128, 8, 512], F32) for i in range(2)])
    rs = Ring([S.sbuf(es, tag + "_rs%d" % i, [128, 512], F32) for i in range(2)])
    rstd = Ring([S.sbuf(es, tag + "_rstd%d" % i, [128, 512], F32) for i in range(2)])
    for ti, (c0, n, j) in enumerate(tiles):
        xb, xap = x_src(ti)
        q = sq.next()
        S.tt('pool', q[:, :, 0:n], xap, xap, ALU.mult, reads=[xb], writes=[q])
        ps = cx.ps.next()
        for c in range(8):
            S.mm(ps[:, 0:n], cx.ones[:, :], q[:, c, 0:n], c == 0, c == 7, reads=[cx.ones, q], writes=[ps])
        r = rs.next()
        S.act(r[:, 0:n], ps[:, 0:n], AF.Sqrt, reads=[ps, cx.eps], writes=[r], bias=cx.eps[:, 0:1], scale=1.0 / D)
        rd = rstd.next()
        S.op('dve', lambda e, rd=rd, r=r, n=n: e.reciprocal(rd[:, 0:n], r[:, 0:n]), reads=[r], writes=[rd])
        S.tt('dve', q[:, :, 0:n], xap, rd[:, 0:n].unsqueeze(1).broadcast_to([128, 8, n]), ALU.mult,
             reads=[xb, rd], writes=[q])
        for c in range(8):
            S.act(h[:, c, c0:c0 + n], q[:, c, 0:n], AF.Identity, reads=[q, A, mod], writes=[h],
                  bias=mod[:, s_shift, c, j:j + 1], scale=A[:, c, j:j + 1])


def linear(cx, es, W_d, K, kp, ochunks, xb, tiles, evac, gcols=512, tag="lin", cast_engs=('dve', 'pool')):
    S = cx.S
    KC = K // kp
    stg = Ring([S.sbuf(es, tag + "_stg%d" % i, [kp, KC, gcols], F32) for i in range(2)])
    wbf = Ring([S.sbuf(es, tag + "_wbf%d" % i, [kp, KC, gcols], BF16) for i in range(2)])
    groups = []
    cur = []
    for oi, (c0, m) in enumerate(ochunks):
        if cur and (c0 + m - cur[0][1] > gcols or c0 != cur[-1][1] + cur[-1][2]):
            groups.append(cur)
            cur = []
        cur.append((oi, c0, m))
    if cur:
        groups.append(cur)
    Wv = W_d.rearrange("(c p) o -> p c o", p=kp)
    for g in groups:
        ga = g[0][1]
        gb = g[-1][1] + g[-1][2]
        w = gb - ga
        st = stg.next()
        half = max(1, KC // 2)
        S.dma('sp', st[:, 0:half, 0:w], Wv[:, 0:half, ga:gb], writes=[st])
        if half < KC:
            S.dma('sp', st[:, half:KC, 0:w], Wv[:, half:KC, ga:gb], writes=[st])
        wb = wbf.next()
        ce = cast_engs[cx.cast_i % len(cast_engs)]
        cx.cast_i += 1
        S.copy(ce, wb[:, :, 0:w], st[:, :, 0:w], reads=[st], writes=[wb])
        for (oi, c0, m) in g:
            oc = c0 - ga
            for ti, (t0, n, j) in enumerate(tiles):
                ps = cx.ps.next()
                for kc in range(KC):
                    S.mm(ps[0:m, 0:n], wb[:, kc, oc:oc + m], xb[:, kc, t0:t0 + n], kc == 0, kc == KC - 1,
                         reads=[wb, xb], writes=[ps])
                evac(oi, ti, ps, m, t0, n, j)


def emit_headnorm_rope(cx, pools, ps, m, n, gain_ap, gain_buf, cs, sn, t0, out_ap, out_buf, hd, rope_lo, half, inv_dim):
    S = cx.S
    sqp, rsp, rdp, qnp, t1p, t2p = pools
    if gain_ap is not None:
        sq = sqp.next()
        S.act(sq[0:m, 0:n], ps[0:m, 0:n], AF.Square, reads=[ps], writes=[sq])
        ps2 = cx.ps.next()
        S.mm(ps2[0:m, 0:n], cx.ones[0:m, 0:m], sq[0:m, 0:n], True, True, reads=[cx.ones, sq], writes=[ps2])
        r = rsp.next()
        S.act(r[0:m, 0:n], ps2[0:m, 0:n], AF.Sqrt, reads=[ps2, cx.eps], writes=[r], bias=cx.eps[0:m, 0:1], scale=inv_dim)
        rd = rdp.next()
        S.op('dve', lambda e: e.reciprocal(rd[0:m, 0:n], r[0:m, 0:n]), reads=[r], writes=[rd])
        qn = qnp.next()
        S.stt('dve', qn[0:m, 0:n], ps[0:m, 0:n], gain_ap, rd[0:m, 0:n], ALU.mult, ALU.mult,
              reads=[ps, gain_buf, rd], writes=[qn])
    else:
        qn = qnp.next()
        S.copy('act', qn[0:m, 0:n], ps[0:m, 0:n], reads=[ps], writes=[qn])
    t1 = t1p.next()
    S.tt('pool', t1[0:m, 0:n], qn[0:m, 0:n], cs[0:m, t0:t0 + n], ALU.mult, reads=[qn, cs], writes=[t1])
    t2 = t2p.next()
    x1, x2 = rope_lo
    if 2 * half != m:
        S.memset('pool', t2[0:m, 0:n], 0.0, writes=[t2])
    S.tt('dve', t2[x1:x1 + half, 0:n], qn[x2:x2 + half, 0:n], sn[x2:x2 + half, t0:t0 + n], ALU.mult,
         reads=[qn, sn], writes=[t2])
    S.tt('dve', t2[x2:x2 + half, 0:n], qn[x1:x1 + half, 0:n], sn[x1:x1 + half, t0:t0 + n], ALU.mult,
         reads=[qn, sn], writes=[t2])
    S.tt('pool', out_ap, t1[0:m, 0:n], t2[0:m, 0:n], ALU.add, reads=[t1, t2], writes=[out_buf])


def fm(v):
    v = np.asarray(v)
    return np.ascontiguousarray(v.reshape(-1, 128).T)


def build_gqa_pre():
    nc = new_nc()
    xT = din(nc, "xT", [D, T])
    mod_d = din(nc, "mod", [128, 6, 8, 2])
    gain = din(nc, "gain", [128, 8])
    w_in = din(nc, "w_in", [D, 4096])
    qkg = din(nc, "qkg", [128, 2])
    cs_d = din(nc, "cs", [128, T])
    sn_d = din(nc, "sn", [128, T])
    qkv = dout(nc, "qkv", [4096, T], BF16)
    with ExitStack() as es:
        cx = Ctx(nc, es)
        S = cx.S
        mod = S.sbuf(es, "mod", [128, 6, 8, 2], F32)
        S.dma('sp', mod[:], mod_d[:, :, :, :], writes=[mod])
        h = S.sbuf(es, "h", [128, 8, T], BF16)
        cs = S.sbuf(es, "cs_s", [128, T], F32)
        sn = S.sbuf(es, "sn_s", [128, T], F32)
        g2 = S.sbuf(es, "qkg_s", [128, 2], F32)
        S.dma('sp', cs[:], cs_d[:, :], writes=[cs])
        S.dma('sp', sn[:], sn_d[:, :], writes=[sn])
        S.dma('sp', g2[:], qkg[:, :], writes=[g2])
        with ExitStack() as es2:
            A = emit_affine(cx, es2, mod, 0, 1, gain, "afa")
            xt = Ring([S.sbuf(es2, "xt%d" % i, [128, 8, 512], F32) for i in range(2)])
            xv = xT.rearrange("(c p) t -> p c t", p=128)

            def x_src(ti):
                c0, n, j = TILES[ti]
                b = xt.next()
                S.dma('sp', b[:, :, 0:n], xv[:, :, c0:c0 + n], writes=[b])
                return b, b[:, :, 0:n]
            emit_norm_mod(cx, es2, x_src, A, mod, 0, h)
            S.barrier()
            S.flush()
        with ExitStack() as es3:
            def mk(name, dt):
                return Ring([S.sbuf(es3, "%s%d" % (name, i), [128, 512], dt) for i in range(2)])
            pools = (mk("sq", F32), mk("rs", F32), mk("rd", F32), mk("qn", F32), mk("t1", F32), mk("t2", F32))
            ob = mk("ob", BF16)
            ochunks = [(o * 128, 128) for o in range(32)]

            def evac(oi, ti, ps, m, t0, n, j):
                o = ob.next()
                if oi < 24:
                    gi = 0 if oi < 16 else 1
                    emit_headnorm_rope(cx, pools, ps, 128, n, g2[:, gi:gi + 1], g2, cs, sn, t0, o[:, 0:n], o, 128, (0, 64), 64,
                                       1.0 / 128)
                else:
                    S.copy('act', o[:, 0:n], ps[:, 0:n], reads=[ps], writes=[o])
                S.dma('pool', qkv[oi * 128:(oi + 1) * 128, t0:t0 + n], o[:, 0:n], reads=[o])
            linear(cx, es3, w_in, D, 128, ochunks, h, TILES, evac)
            cx.finish()
    return nc


def rope_tables_axial(pos, rot_dim):
    GRID_W = 64
    row = (pos // GRID_W).astype(np.float32)
    col = (pos % GRID_W).astype(np.float32)
    axis_dim = rot_dim // 2
    inv_freq = np.power(np.float32(10000.0), -np.arange(0, axis_dim, 2, dtype=np.float32) / np.float32(axis_dim)).astype(np.float32)
    ang = np.concatenate([row[:, None] * inv_freq, col[:, None] * inv_freq], axis=-1).astype(np.float32)
    return np.cos(ang).astype(np.float32), np.sin(ang).astype(np.float32)


NCORES = 8


def core_tok(r):
    b = r // 4
    q = r % 4
    return b, q * TL, (q + 1) * TL


def host_xT(x, ctx):
    out = []
    for r in range(NCORES):
        b, t0, t1 = core_tok(r)
        out.append(np.ascontiguousarray(np.concatenate([x[b, t0:t1].T, ctx[b].T], axis=1)))
    return out


def host_cvec(c, c_ctx):
    return [np.ascontiguousarray(np.stack([fm(c[r // 4]), fm(c_ctx)], axis=-1)) for r in range(NCORES)]


def host_rope_fm(rot_dim, x1, x2, m):
    half = rot_dim // 2
    res = []
    for r in range(NCORES):
        b, t0, t1 = core_tok(r)
        cos, sin = rope_tables_axial(np.arange(t0, t1), rot_dim)
        cs = np.ones((m, T), np.float32)
        sn = np.zeros((m, T), np.float32)
        cs[x1:x1 + half, :TL] = cos.T
        cs[x2:x2 + half, :TL] = cos.T
        sn[x1:x1 + half, :TL] = sin.T
        sn[x2:x2 + half, :TL] = -sin.T
        res.append((cs, sn))
    return res


def run(nc, in_maps):
    res = run_bass_kernel_spmd(nc, in_maps, core_ids=list(range(NCORES)))
    return res.results


def launch_gqa_pre(xT, mods, inp, i, j):
    nc = build_gqa_pre()
    tabs = host_rope_fm(128, 0, 64, 128)
    gain = fm(inp['norm_mix'][i])
    w_in = np.ascontiguousarray(inp['gqa_w_in'][j])
    qkg = np.ascontiguousarray(np.stack([inp['gqa_q_norm'][j], inp['gqa_k_norm'][j]], axis=1))
    maps = [dict(xT=xT[r], mod=mods[i][r], gain=gain, w_in=w_in, qkg=qkg,
                 cs=tabs[r][0], sn=tabs[r][1]) for r in range(NCORES)]
    return [o['qkv'] for o in run(nc, maps)]


def emit_attention(cx, es_outer, q_d, kT_d, v_d, nheads, group, dk, dv, scale, nkc, AO):
    S = cx.S
    with ExitStack() as es:
        kt = Ring([S.sbuf(es, "kt%d" % i, [dk, NK], BF16) for i in range(1)])
        vv = Ring([S.sbuf(es, "vv%d" % i, [128, nkc, dv], BF16) for i in range(1)])
        qt = Ring([S.sbuf(es, "qt%d" % i, [dk, T], BF16) for i in range(2)])
        pT = Ring([S.sbuf(es, "pT%d" % i, [128, 512], BF16) for i in range(4)])
        rdp = Ring([S.sbuf(es, "ard%d" % i, [128, 512], F32) for i in range(2)])
        psO = Ring([S.psum(es, "psO%d" % i) for i in range(2)])
        psD = Ring([S.psum(es, "psD%d" % i) for i in range(2)])
        nctx = CTX // 128
        for g in range(nheads // group):
            k = kt.next()
            S.dma('sp', k[:, :], kT_d[g * dk:(g + 1) * dk, :], writes=[k])
            v = vv.next()
            S.dma('sp', v[:, :, :], v_d[g].rearrange("p (c d) -> p c d", d=dv), writes=[v])
            for hh in range(group):
                h = g * group + hh
                q = qt.next()
                S.dma('sp', q[:, :], q_d[h * dk:(h + 1) * dk, :], writes=[q])
                for (c0, n, j) in TILES:
                    kcs = list(range(nkc)) if j == 0 else list(range(nkc - nctx, nkc))
                    po = psO.next()
                    pd = psD.next()
                    for idx, kc in enumerate(kcs):
                        ps = cx.ps.next()
                        S.mm(ps[:, 0:n], k[:, kc * 128:(kc + 1) * 128], q[:, c0:c0 + n], True, True,
                             reads=[k, q], writes=[ps])
                        p = pT.next()
                        S.act(p[:, 0:n], ps[:, 0:n], AF.Exp, reads=[ps], writes=[p], scale=scale)
                        last = idx == len(kcs) - 1
                        S.mm(po[0:dv, 0:n], v[:, kc, :], p[:, 0:n], idx == 0, last, reads=[v, p], writes=[po], inc=False)
                        S.mm(pd[0:dv, 0:n], cx.ones_bf[:, 0:dv], p[:, 0:n], idx == 0, last, reads=[cx.ones_bf, p],
                             writes=[pd], inc=last)
                    rd = rdp.next()
                    S.op('dve', lambda e, rd=rd, pd=pd, n=n: e.reciprocal(rd[0:dv, 0:n], pd[0:dv, 0:n]),
                         reads=[pd], writes=[rd])
                    S.tt('dve', AO[:, h, c0:c0 + n], po[0:dv, 0:n], rd[0:dv, 0:n], ALU.mult, reads=[po, rd], writes=[AO])
        S.barrier()
        S.flush()


def emit_post(cx, es_outer, AO, kp, K, w_out_d, xT_d, x1_d, h2_d, mod, gain_d, hres=None):
    S = cx.S
    x1buf = Buf(None, "x1_dram")
    with ExitStack() as es:
        xt = Ring([S.sbuf(es, "pxt%d" % i, [128, 512], F32) for i in range(3)])
        xo = Ring([S.sbuf(es, "pxo%d" % i, [128, 512], F32) for i in range(3)])

        def evac(oi, ti, ps, m, t0, n, j):
            a = xt.next()
            S.dma('sp', a[:, 0:n], xT_d[oi * 128:(oi + 1) * 128, t0:t0 + n], writes=[a])
            o = xo.next()
            S.stt('dve', o[:, 0:n], ps[:, 0:n], mod[:, 2, oi, j:j + 1], a[:, 0:n], ALU.mult, ALU.add,
                  reads=[ps, mod, a], writes=[o])
            S.dma('pool', x1_d[oi * 128:(oi + 1) * 128, t0:t0 + n], o[:, 0:n], reads=[o], writes=[x1buf])
        linear(cx, es, w_out_d, K, kp, [(o * 128, 128) for o in range(8)], AO, TILES, evac, gcols=256, tag="wo")
        S.barrier()
        S.flush()
    with ExitStack() as es:
        h2 = S.sbuf(es, "h2", [128, 8, T], BF16)
        A = emit_affine(cx, es, mod, 3, 4, gain_d, "aff")
        xt = Ring([S.sbuf(es, "nxt%d" % i, [128, 8, 512], F32) for i in range(2)])
        xv = x1_d.rearrange("(c p) t -> p c t", p=128)

        def x_src(ti):
            c0, n, j = TILES[ti]
            b = xt.next()
            S.dma('sp', b[:, :, 0:n], xv[:, :, c0:c0 + n], reads=[x1buf], writes=[b])
            return b, b[:, :, 0:n]
        emit_norm_mod(cx, es, x_src, A, mod, 3, h2, tag="n2")
        S.dma('pool', h2_d.rearrange("(c p) t -> p c t", p=128), h2[:, :, :], reads=[h2])
        S.barrier()
        S.flush()


def build_attn_post(nheads, group, dk, dv, scale):
    nc = new_nc()
    nkv = nheads // group
    nkc = NK // 128
    q_d = din(nc, "q", [nheads * dk, T], BF16)
    kT_d = din(nc, "kT", [nkv * dk, NK], BF16)
    v_d = din(nc, "v", [nkv, 128, nkc * dv], BF16)
    xT_d = din(nc, "xT", [D, T])
    mod_d = din(nc, "mod", [128, 6, 8, 2])
    gain_d = din(nc, "gain", [128, 8])
    w_out_d = din(nc, "w_out", [nheads * dv, D])
    x1_d = dout(nc, "x1", [D, T])
    h2_d = dout(nc, "h2", [D, T], BF16)
    with ExitStack() as es:
        cx = Ctx(nc, es, npsum=4)
        S = cx.S
        mod = S.sbuf(es, "mod", [128, 6, 8, 2], F32)
        S.dma('sp', mod[:], mod_d[:, :, :, :], writes=[mod])
        with ExitStack() as es2:
            AO = S.sbuf(es2, "AO", [dv, nheads, T], BF16)
            emit_attention(cx, es2, q_d, kT_d, v_d, nheads, group, dk, dv, scale, nkc, AO)
            emit_post(cx, es2, AO, dv, nheads * dv, w_out_d, xT_d, x1_d, h2_d, mod, gain_d)
        cx.finish()
    return nc


def host_kv_gqa(qkv):
    kTs, vs = [], []
    for b in range(NB):
        k = np.concatenate([qkv[4 * b + q][2048:3072, :TL] for q in range(4)] + [qkv[4 * b][2048:3072, TL:]], axis=1)
        v = np.concatenate([qkv[4 * b + q][3072:4096, :TL] for q in range(4)] + [qkv[4 * b][3072:4096, TL:]], axis=1)
        kTs.append(np.ascontiguousarray(k))
        v4 = v.reshape(8, 128, NK // 128, 128)
        vs.append(np.ascontiguousarray(v4.transpose(0, 3, 2, 1)).reshape(8, 128, (NK // 128) * 128))
    return kTs, vs


def launch_gqa_attn(qkv, xT, mods, inp, i, j):
    nc = build_attn_post(16, 2, 128, 128, 128 ** -0.5)
    kTs, vs = host_kv_gqa(qkv)
    gain = fm(inp['norm_ffn'][i])
    w_out = np.ascontiguousarray(inp['gqa_w_out'][j])
    maps = [dict(q=np.ascontiguousarray(qkv[r][0:2048]), kT=kTs[r // 4], v=vs[r // 4], xT=xT[r], mod=mods[i][r],
                 gain=gain, w_out=w_out) for r in range(NCORES)]
    res = run(nc, maps)
    return [o['x1'] for o in res], [o['h2'] for o in res]


TP = TL + 2 + CTX + 2
UT = [(0, 512), (510, 512), (1020, 512), (1530, 512), (2040, 268)]
OT = [(1, 512, 0), (513, 512, 0), (1025, 512, 0), (1537, 512, 0), (2051, 256, 1)]


def tcol(p):
    return p - 1 if p < 2050 else p - 3


def build_ffn(final):
    nc = new_nc()
    h2_d = din(nc, "h2p", [D, TP], BF16)
    x1_d = din(nc, "x1", [D, T])
    mod_d = din(nc, "mod", [128, 6, 8, 2])
    w_in_d = din(nc, "w_in", [D, 2 * DFF])
    cw_d = din(nc, "cw", [128, 48, 3])
    cb_d = din(nc, "cb", [128, 48])
    w_out_d = din(nc, "w_out", [DFF, D])
    x2_d = dout(nc, "x2", [D, T])
    if final:
        fg_d = din(nc, "fgain", [128, 8])
        y_d = dout(nc, "y", [D, TL])
    with ExitStack() as es:
        cx = Ctx(nc, es)
        S = cx.S
        mod = S.sbuf(es, "mod", [128, 6, 8, 2], F32)
        S.dma('sp', mod[:], mod_d[:, :, :, :], writes=[mod])
        cw = S.sbuf(es, "cw", [128, 48, 3], F32)
        cb = S.sbuf(es, "cb", [128, 48], F32)
        S.dma('sp', cw[:], cw_d[:, :, :], writes=[cw])
        S.dma('sp', cb[:], cb_d[:, :], writes=[cb])
        x2buf = Buf(None, "x2_dram")
        with ExitStack() as esG:
            G = S.sbuf(esG, "G", [128, 24, TP], BF16)
            with ExitStack() as es1:
                h2 = S.sbuf(es1, "h2", [128, 8, TP], BF16)
                S.dma('sp', h2[:, 0:4, :], h2_d.rearrange("(c p) t -> p c t", p=128)[:, 0:4, :], writes=[h2])
                S.dma('sp', h2[:, 4:8, :], h2_d.rearrange("(c p) t -> p c t", p=128)[:, 4:8, :], writes=[h2])
                ta = Ring([S.sbuf(es1, "ta%d" % i, [128, 512], F32) for i in range(3)])
                sa = [S.sbuf(es1, "sa%d" % i, [128, 512], F32) for i in range(5)]
                ochunks = []
                for j in range(24):
                    ochunks += [(j * 128, 128), (DFF + j * 128, 128)]

                def evac(oi, ti, ps, m, u0, un, jj):
                    jch = oi // 2
                    isval = oi % 2
                    ch = jch + 24 * isval
                    on = un - 2
                    t = ta.next()
                    S.act(t[:, 0:on], ps[:, 1:1 + on], AF.Identity, reads=[ps, cw, cb], writes=[t],
                          bias=cb[:, ch:ch + 1], scale=cw[:, ch, 1:2])
                    S.stt('dve', t[:, 0:on], ps[:, 0:on], cw[:, ch, 0:1], t[:, 0:on], ALU.mult, ALU.add,
                          reads=[ps, cw, t], writes=[t])
                    S.stt('dve', t[:, 0:on], ps[:, 2:2 + on], cw[:, ch, 2:3], t[:, 0:on], ALU.mult, ALU.add,
                          reads=[ps, cw, t], writes=[t])
                    if not isval:
                        S.act(sa[ti][:, 0:on], t[:, 0:on], AF.Silu, reads=[t], writes=[sa[ti]])
                    else:
                        S.tt('pool', G[:, jch, u0 + 1:u0 + 1 + on], sa[ti][:, 0:on], t[:, 0:on], ALU.mult,
                             reads=[sa[ti], t], writes=[G])
                linear(cx, es1, w_in_d, D, 128, ochunks, h2, [(u0, un, 0) for (u0, un) in UT], evac, gcols=128, tag="fi",
                       cast_engs=('pool', 'dve'))
                S.barrier()
                S.flush()
            with ExitStack() as es2:
                xt = Ring([S.sbuf(es2, "fxt%d" % i, [128, 512], F32) for i in range(3)])
                xo = Ring([S.sbuf(es2, "fxo%d" % i, [128, 512], F32) for i in range(3)])

                def evac2(oi, ti, ps, m, p0, n, j):
                    t0 = tcol(p0)
                    a = xt.next()
                    S.dma('sp', a[:, 0:n], x1_d[oi * 128:(oi + 1) * 128, t0:t0 + n], writes=[a])
                    o = xo.next()
                    S.stt('dve', o[:, 0:n], ps[:, 0:n], mod[:, 5, oi, j:j + 1], a[:, 0:n], ALU.mult, ALU.add,
                          reads=[ps, mod, a], writes=[o])
                    S.dma('pool', x2_d[oi * 128:(oi + 1) * 128, t0:t0 + n], o[:, 0:n], reads=[o], writes=[x2buf])
                linear(cx, es2, w_out_d, DFF, 128, [(o * 128, 128) for o in range(8)], G, OT, evac2, gcols=128, tag="fo")
                S.barrier()
                S.flush()
        if final:
            with ExitStack() as es3:
                fg = S.sbuf(es3, "fg", [128, 8], F32)
                S.dma('sp', fg[:], fg_d[:, :], writes=[fg])
                xt = Ring([S.sbuf(es3, "yxt%d" % i, [128, 8, 512], F32) for i in range(2)])
                sq = Ring([S.sbuf(es3, "ysq%d" % i, [128, 8, 512], F32) for i in range(2)])
                rs = Ring([S.sbuf(es3, "yrs%d" % i, [128, 512], F32) for i in range(2)])
                rdp = Ring([S.sbuf(es3, "yrd%d" % i, [128, 512], F32) for i in range(2)])
                xv = x2_d.rearrange("(c p) t -> p c t", p=128)
                yv = y_d.rearrange("(c p) t -> p c t", p=128)
                for (c0, n, j) in TILES[0:4]:
                    b = xt.next()
                    S.dma('sp', b[:, :, 0:n], xv[:, :, c0:c0 + n], reads=[x2buf], writes=[b])
                    q = sq.next()
                    S.tt('pool', q[:, :, 0:n], b[:, :, 0:n], b[:, :, 0:n], ALU.mult, reads=[b], writes=[q])
                    ps = cx.ps.next()
                    for c in range(8):
                        S.mm(ps[:, 0:n], cx.ones[:, :], q[:, c, 0:n], c == 0, c == 7, reads=[cx.ones, q], writes=[ps])
                    r = rs.next()
                    S.act(r[:, 0:n], ps[:, 0:n], AF.Sqrt, reads=[ps, cx.eps], writes=[r], bias=cx.eps[:, 0:1], scale=1.0 / D)
                    rd = rdp.next()
                    S.op('dve', lambda e, rd=rd, r=r, n=n: e.reciprocal(rd[:, 0:n], r[:, 0:n]), reads=[r], writes=[rd])
                    S.tt('dve', q[:, :, 0:n], b[:, :, 0:n], rd[:, 0:n].unsqueeze(1).broadcast_to([128, 8, n]), ALU.mult,
                         reads=[b, rd], writes=[q])
                    S.tt('pool', b[:, :, 0:n], q[:, :, 0:n], fg[:, :].unsqueeze(2).broadcast_to([128, 8, n]), ALU.mult,
                         reads=[q, fg], writes=[b])
                    S.dma('pool', yv[:, :, c0:c0 + n], b[:, :, 0:n], reads=[b])
                S.barrier()
                S.flush()
        cx.finish()
    return nc


def host_h2p(h2):
    out = []
    for r in range(NCORES):
        q = r % 4
        a = np.asarray(h2[r])
        z1 = np.zeros((D, 1), a.dtype)
        left = np.asarray(h2[r - 1])[:, TL - 1:TL] if q > 0 else z1
        right = np.asarray(h2[r + 1])[:, 0:1] if q < 3 else z1
        out.append(np.ascontiguousarray(np.concatenate([left, a[:, :TL], right, z1, a[:, TL:], z1], axis=1)))
    return out


def launch_ffn(h2, x1, mods, inp, i, final):
    nc = build_ffn(final)
    h2p = host_h2p(h2)
    w_in = np.ascontiguousarray(inp['ffn_w_in'][i])
    w_out = np.ascontiguousarray(inp['ffn_w_out'][i])
    cw = np.ascontiguousarray(inp['ffn_conv_w'][i].T.reshape(48, 128, 3).transpose(1, 0, 2))
    cb = fm(inp['ffn_conv_b'][i])
    maps = []
    for r in range(NCORES):
        m = dict(h2p=h2p[r], x1=x1[r], mod=mods[i][r], w_in=w_in, cw=cw, cb=cb, w_out=w_out)
        if final:
            m['fgain'] = fm(inp['final_norm'])
        maps.append(m)
    res = run(nc, maps)
    if final:
        return [o['x2'] for o in res], [o['y'] for o in res]
    return [o['x2'] for o in res], None


def build_mla_pre():
    nc = new_nc()
    xT = din(nc, "xT", [D, T])
    mod_d = din(nc, "mod", [128, 6, 8, 2])
    gain = din(nc, "gain", [128, 8])
    w_in = din(nc, "w_in", [D, 1024 + 64])
    qg_d = din(nc, "qg", [128, 8])
    w_qb = din(nc, "w_qb", [768, 2048])
    w_kvb = din(nc, "w_kvb", [256, 2048])
    cs_d = din(nc, "cs", [128, T])
    sn_d = din(nc, "sn", [128, T])
    q_o = dout(nc, "q", [2048, T], BF16)
    kv_o = dout(nc, "kv", [2048, T], BF16)
    kr_o = dout(nc, "kr", [64, T], BF16)
    with ExitStack() as es:
        cx = Ctx(nc, es)
        S = cx.S
        mod = S.sbuf(es, "mod", [128, 6, 8, 2], F32)
        S.dma('sp', mod[:], mod_d[:, :, :, :], writes=[mod])
        cs = S.sbuf(es, "cs_s", [128, T], F32)
        sn = S.sbuf(es, "sn_s", [128, T], F32)
        qg = S.sbuf(es, "qg_s", [128, 8], F32)
        S.dma('sp', cs[:], cs_d[:, :], writes=[cs])
        S.dma('sp', sn[:], sn_d[:, :], writes=[sn])
        S.dma('sp', qg[:], qg_d[:, :], writes=[qg])
        cqn = S.sbuf(es, "cqn", [128, 8, T], BF16)
        cq_d = nc.dram_tensor("cq_scr", [D, T], F32, kind="Internal").ap()
        cqbuf = Buf(None, "cq_dram")
        with ExitStack() as es_h:
            h = S.sbuf(es_h, "h", [128, 8, T], BF16)
            with ExitStack() as es2:
                A = emit_affine(cx, es2, mod, 0, 1, gain, "afa")
                xt = Ring([S.sbuf(es2, "xt%d" % i, [128, 8, 512], F32) for i in range(2)])
                xv = xT.rearrange("(c p) t -> p c t", p=128)

                def x_src(ti):
                    c0, n, j = TILES[ti]
                    b = xt.next()
                    S.dma('sp', b[:, :, 0:n], xv[:, :, c0:c0 + n], writes=[b])
                    return b, b[:, :, 0:n]
                emit_norm_mod(cx, es2, x_src, A, mod, 0, h)
                S.barrier()
                S.flush()
            with ExitStack() as es3:
                def mk(name, dt):
                    return Ring([S.sbuf(es3, "%s%d" % (name, i), [128, 512], dt) for i in range(2)])
                pools = (mk("sq", F32), mk("rs", F32), mk("rd", F32), mk("qn", F32), mk("t1", F32), mk("t2", F32))
                ob = mk("ob", BF16)
                cqs = Ring([S.sbuf(es3, "cqs%d" % i, [128, 512], F32) for i in range(3)])
                ochunks = [(o * 128, 128) for o in range(8)] + [(1024, 64)]

                def evac1(oi, ti, ps, m, t0, n, j):
                    if oi < 8:
                        o = cqs.next()
                        S.copy('act', o[:, 0:n], ps[:, 0:n], reads=[ps], writes=[o])
                        S.dma('pool', cq_d[oi * 128:(oi + 1) * 128, t0:t0 + n], o[:, 0:n], reads=[o], writes=[cqbuf])
                    else:
                        o = ob.next()
                        emit_headnorm_rope(cx, pools, ps, 64, n, None, None, cs, sn, t0, o[0:64, 0:n], o, 64, (0, 32), 16, 0.0)
                        S.dma('pool', kr_o[:, t0:t0 + n], o[0:64, 0:n], reads=[o])
                linear(cx, es3, w_in, D, 128, ochunks, h, TILES, evac1, gcols=256, tag="l1")
                S.barrier()
                S.flush()
        with ExitStack() as es4:
            cqt = Ring([S.sbuf(es4, "cqt%d" % i, [128, 8, 512], F32) for i in range(2)])
            sq = Ring([S.sbuf(es4, "lsq%d" % i, [128, 8, 512], F32) for i in range(2)])
            rs = Ring([S.sbuf(es4, "lrs%d" % i, [128, 512], F32) for i in range(2)])
            rdp = Ring([S.sbuf(es4, "lrd%d" % i, [128, 512], F32) for i in range(4)])
            cqv = cq_d.rearrange("(c p) t -> p c t", p=128)
            for (c0, n, j) in TILES:
                cq = cqt.next()
                S.dma('sp', cq[:, :, 0:n], cqv[:, :, c0:c0 + n], reads=[cqbuf], writes=[cq])
                q = sq.next()
                S.tt('pool', q[:, :, 0:n], cq[:, :, 0:n], cq[:, :, 0:n], ALU.mult, reads=[cq], writes=[q])
                for (ca, cb_, inv) in ((0, 6, 1.0 / 768), (6, 8, 1.0 / 256)):
                    ps = cx.ps.next()
                    for c in range(ca, cb_):
                        S.mm(ps[:, 0:n], cx.ones[:, :], q[:, c, 0:n], c == ca, c == cb_ - 1, reads=[cx.ones, q], writes=[ps])
                    r = rs.next()
                    S.act(r[:, 0:n], ps[:, 0:n], AF.Sqrt, reads=[ps, cx.eps], writes=[r], bias=cx.eps[:, 0:1], scale=inv)
                    rd = rdp.next()
                    S.op('dve', lambda e, rd=rd, r=r, n=n: e.reciprocal(rd[:, 0:n], r[:, 0:n]), reads=[r], writes=[rd])
                    for c in range(ca, cb_):
                        S.stt('dve', cqn[:, c, c0:c0 + n], cq[:, c, 0:n], qg[:, c:c + 1], rd[:, 0:n], ALU.mult, ALU.mult,
                              reads=[cq, qg, rd], writes=[cqn])
            S.barrier()
            S.flush()
        with ExitStack() as es5:
            def mk(name, dt):
                return Ring([S.sbuf(es5, "%s%d" % (name, i), [128, 512], dt) for i in range(2)])
            pools = (mk("sq", F32), mk("rs", F32), mk("rd", F32), mk("qn", F32), mk("t1", F32), mk("t2", F32))
            ob = mk("ob", BF16)

            def evac2(oi, ti, ps, m, t0, n, j):
                o = ob.next()
                emit_headnorm_rope(cx, pools, ps, 128, n, None, None, cs, sn, t0, o[:, 0:n], o, 128, (0, 32), 16, 0.0)
                S.dma('pool', q_o[oi * 128:(oi + 1) * 128, t0:t0 + n], o[:, 0:n], reads=[o])
            linear(cx, es5, w_qb, 768, 128, [(o * 128, 128) for o in range(16)], cqn, TILES, evac2, tag="l2")
            kvsrc = Buf(cqn.t[:, 6:8, :], "kvsrc")
            kvsrc.w, kvsrc.r = cqn.w, cqn.r

            def evac3(oi, ti, ps, m, t0, n, j):
                o = ob.next()
                S.copy('act', o[:, 0:n], ps[:, 0:n], reads=[ps], writes=[o])
                S.dma('pool', kv_o[oi * 128:(oi + 1) * 128, t0:t0 + n], o[:, 0:n], reads=[o])
            linear(cx, es5, w_kvb, 256, 128, [(o * 128, 128) for o in range(16)], kvsrc, TILES, evac3, tag="l3")
            cx.finish()
    return nc


def host_mla_weights(inp, j):
    w_in = inp['mla_w_in'][j]
    w_in2 = np.zeros((D, 1024 + 64), np.float32)
    w_in2[:, :1024] = w_in[:, :1024]
    w_in2[:, 1024:1040] = w_in[:, 1024:1040]
    w_in2[:, 1056:1072] = w_in[:, 1040:1056]
    w_qb = inp['mla_w_qb'][j].reshape(768, 16, 96)
    w_qb2 = np.zeros((768, 16, 128), np.float32)
    w_qb2[:, :, 0:16] = w_qb[:, :, 64:80]
    w_qb2[:, :, 32:48] = w_qb[:, :, 80:96]
    w_qb2[:, :, 64:128] = w_qb[:, :, 0:64]
    return w_in2, np.ascontiguousarray(w_qb2.reshape(768, 2048)), np.ascontiguousarray(inp['mla_w_kvb'][j])


def launch_mla_pre(xT, mods, inp, i, j):
    nc = build_mla_pre()
    tabs = host_rope_fm(32, 0, 32, 128)
    gain = fm(inp['norm_mix'][i])
    w_in2, w_qb2, w_kvb = host_mla_weights(inp, j)
    qg = np.ascontiguousarray(np.concatenate([fm(inp['mla_q_norm'][j]), fm(inp['mla_kv_norm'][j])], axis=1))
    maps = [dict(xT=xT[r], mod=mods[i][r], gain=gain, w_in=w_in2, qg=qg, w_qb=w_qb2, w_kvb=w_kvb,
                 cs=tabs[r][0], sn=tabs[r][1]) for r in range(NCORES)]
    res = run(nc, maps)
    return [o['q'] for o in res], [o['kv'] for o in res], [o['kr'] for o in res]


def host_kv_mla(kv, kr):
    kTs, vs = [], []
    for b in range(NB):
        kvb = np.concatenate([np.asarray(kv[4 * b + q])[:, :TL] for q in range(4)] + [np.asarray(kv[4 * b])[:, TL:]], axis=1)
        krb = np.concatenate([np.asarray(kr[4 * b + q])[:, :TL] for q in range(4)] + [np.asarray(kr[4 * b])[:, TL:]], axis=1)
        kvh = kvb.reshape(16, 128, NK)
        kT = np.zeros((16, 128, NK), kvb.dtype)
        kT[:, 0:64, :] = krb[None, :, :]
        kT[:, 64:128, :] = kvh[:, 0:64, :]
        kTs.append(np.ascontiguousarray(kT.reshape(16 * 128, NK)))
        v = kvh[:, 64:128, :]
        v4 = v.reshape(16, 64, NK // 128, 128)
        vs.append(np.ascontiguousarray(v4.transpose(0, 3, 2, 1)).reshape(16, 128, (NK // 128) * 64))
    return kTs, vs


def launch_mla_attn(q, kv, kr, xT, mods, inp, i, j):
    nc = build_attn_post(16, 1, 128, 64, 96 ** -0.5)
    kTs, vs = host_kv_mla(kv, kr)
    gain = fm(inp['norm_ffn'][i])
    w_out = np.ascontiguousarray(inp['mla_w_out'][j])
    maps = [dict(q=q[r], kT=kTs[r // 4], v=vs[r // 4], xT=xT[r], mod=mods[i][r],
                 gain=gain, w_out=w_out) for r in range(NCORES)]
    res = run(nc, maps)
    return [o['x1'] for o in res], [o['h2'] for o in res]


NS = CTX + SEQ
HC = 64
SUP = [(0, 256)] + [(256 + 512 * i, 512) for i in range(16)]


def build_hgrn_pre():
    nc = new_nc()
    xT = din(nc, "xT", [D, T])
    mod_d = din(nc, "mod", [128, 6, 8, 2])
    gain = din(nc, "gain", [128, 8])
    w_in = din(nc, "w_in", [D, 5120])
    p_o = dout(nc, "p", [5120, T])
    with ExitStack() as es:
        cx = Ctx(nc, es)
        S = cx.S
        mod = S.sbuf(es, "mod", [128, 6, 8, 2], F32)
        S.dma('sp', mod[:], mod_d[:, :, :, :], writes=[mod])
        h = S.sbuf(es, "h", [128, 8, T], BF16)
        with ExitStack() as es2:
            A = emit_affine(cx, es2, mod, 0, 1, gain, "afa")
            xt = Ring([S.sbuf(es2, "xt%d" % i, [128, 8, 512], F32) for i in range(2)])
            xv = xT.rearrange("(c p) t -> p c t", p=128)

            def x_src(ti):
                c0, n, j = TILES[ti]
                b = xt.next()
                S.dma('sp', b[:, :, 0:n], xv[:, :, c0:c0 + n], writes=[b])
                return b, b[:, :, 0:n]
            emit_norm_mod(cx, es2, x_src, A, mod, 0, h)
            S.barrier()
            S.flush()
        with ExitStack() as es3:
            ob = Ring([S.sbuf(es3, "ob%d" % i, [128, 512], F32) for i in range(3)])

            def evac(oi, ti, ps, m, t0, n, j):
                o = ob.next()
                if oi % 2 == 0:
                    S.copy('act', o[:, 0:n], ps[:, 0:n], reads=[ps], writes=[o])
                else:
                    S.copy('dve', o[:, 0:n], ps[:, 0:n], reads=[ps], writes=[o])
                S.dma('pool', p_o[oi * 128:(oi + 1) * 128, t0:t0 + n], o[:, 0:n], reads=[o])
            linear(cx, es3, w_in, D, 128, [(o * 128, 128) for o in range(40)], h, TILES, evac, tag="hl", cast_engs=('pool',))
            cx.finish()
    return nc


def build_hgrn_scan():
    nc = new_nc()
    q_d = din(nc, "q", [4, 128, NS])
    f_d = din(nc, "f", [4, 128, NS])
    v_d = din(nc, "v", [4, HC, (NS // HC) * 128])
    lb_d = din(nc, "lb", [128, 4])
    rm_d = din(nc, "rmask", [128, 512])
    ut_d = din(nc, "ut", [HC, HC])
    id_d = din(nc, "ident", [128, 128])
    o_d = dout(nc, "o", [4, 128, NS])
    scale = 128 ** -0.5
    with ExitStack() as es:
        cx = Ctx(nc, es)
        S = cx.S
        lb = S.sbuf(es, "lb", [128, 4], F32)
        oml = S.sbuf(es, "oml", [128, 4], F32)
        rm = S.sbuf(es, "rm", [128, 512], F32)
        ut = S.sbuf(es, "ut", [HC, HC], F32)
        idf = S.sbuf(es, "idf", [128, 128], F32)
        idb = S.sbuf(es, "idb", [128, 128], BF16)
        S.dma('sp', lb[:], lb_d[:, :], writes=[lb])
        S.dma('sp', rm[:], rm_d[:, :], writes=[rm])
        S.dma('sp', ut[:], ut_d[:, :], writes=[ut])
        S.dma('sp', idf[:], id_d[:, :], writes=[idf])
        S.copy('dve', idb[:], idf[:], reads=[idf], writes=[idb])
        S.ts('dve', oml[:], lb[:], -1.0, 1.0, ALU.mult, ALU.add, reads=[lb], writes=[oml])

        def mk(name, shape, dt, k=2):
            return Ring([S.sbuf(es, "%s%d" % (name, i), shape, dt) for i in range(k)])
        qin = mk("qin", [128, 512], F32)
        fin = mk("fin", [128, 512], F32)
        vin = mk("vin", [HC, 8, 128], F32)
        vbf = mk("vbf", [HC, 8, 128], BF16)
        sg = mk("sg", [128, 512], F32)
        ff = mk("ff", [128, 512], F32)
        lg = mk("lg", [128, 512], F32)
        k0 = mk("k0", [128, 512], F32)
        cum = mk("cum", [128, 512], F32)
        e1 = mk("e1", [128, 512], F32)
        e2 = mk("e2", [128, 512], F32)
        kkf = mk("kkf", [128, 512], F32)
        dcy = mk("dcy", [128, 8], F32)
        qq = mk("qq", [128, 512], BF16)
        kk = mk("kk", [128, 512], BF16)
        kk2 = mk("kk2", [128, 512], BF16)
        am = mk("am", [HC, HC], BF16, 3)
        k2t = mk("k2t", [HC, 128], BF16, 3)
        ost = mk("ost", [128, 512], F32)
        Sf = mk("Sf", [128, 128], F32, 2)
        Sb = mk("Sb", [128, 128], BF16, 2)
        for sc in range(4):
            s_cur = Sf.next()
            S.memset('dve', s_cur[:], 0.0, writes=[s_cur])
            sb_cur = Sb.next()
            S.memset('pool', sb_cur[:], 0.0, writes=[sb_cur])
            for (s0, n) in SUP:
                ncnk = n // HC
                a_q = qin.next()
                S.dma('sp', a_q[:, 0:n], q_d[sc, :, s0:s0 + n], writes=[a_q])
                a_f = fin.next()
                S.dma('sp', a_f[:, 0:n], f_d[sc, :, s0:s0 + n], writes=[a_f])
                a_v = vin.next()
                c0 = s0 // HC
                S.dma('sp', a_v[:, 0:ncnk, :], v_d[sc, :, c0 * 128:(c0 + ncnk) * 128].rearrange("p (c d) -> p c d", d=128),
                      writes=[a_v])
                b_v = vbf.next()
                S.copy('pool', b_v[:, 0:ncnk, :], a_v[:, 0:ncnk, :], reads=[a_v], writes=[b_v])
                t_sg = sg.next()
                S.act(t_sg[:, 0:n], a_f[:, 0:n], AF.Sigmoid, reads=[a_f], writes=[t_sg])
                t_f = ff.next()
                S.ts('dve', t_f[:, 0:n], t_sg[:, 0:n], oml[:, sc:sc + 1], lb[:, sc:sc + 1], ALU.mult, ALU.add,
                     reads=[t_sg, oml, lb], writes=[t_f])
                t_lg = lg.next()
                S.act(t_lg[:, 0:n], t_f[:, 0:n], AF.Ln, reads=[t_f], writes=[t_lg])
                t_k0 = k0.next()
                S.ts('pool', t_k0[:, 0:n], t_f[:, 0:n], -1.0, 1.0, ALU.mult, ALU.add, reads=[t_f], writes=[t_k0])
                t_cum = cum.next()
                S.op('dve', lambda e, t_cum=t_cum, t_lg=t_lg, n=n: e.tensor_tensor_scan(
                    t_cum[:, 0:n], rm[:, 0:n], t_lg[:, 0:n], 0.0, ALU.mult, ALU.add), reads=[rm, t_lg], writes=[t_cum])
                t_e1 = e1.next()
                S.act(t_e1[:, 0:n], t_cum[:, 0:n], AF.Exp, reads=[t_cum], writes=[t_e1])
                t_e2 = e2.next()
                S.act(t_e2[:, 0:n], t_cum[:, 0:n], AF.Exp, reads=[t_cum], writes=[t_e2], scale=-1.0)
                t_d = dcy.next()
                S.copy('dve', t_d[:, 0:ncnk], t_e1[:, HC - 1:n:HC], reads=[t_e1], writes=[t_d])
                t_qq = qq.next()
                S.stt('dve', t_qq[:, 0:n], a_q[:, 0:n], scale, t_e1[:, 0:n], ALU.mult, ALU.mult, reads=[a_q, t_e1], writes=[t_qq])
                t_kkf = kkf.next()
                S.tt('pool', t_kkf[:, 0:n], t_k0[:, 0:n], t_e2[:, 0:n], ALU.mult, reads=[t_k0, t_e2], writes=[t_kkf])
                t_kk = kk.next()
                S.copy('pool', t_kk[:, 0:n], t_kkf[:, 0:n], reads=[t_kkf], writes=[t_kk])
                t_kk2 = kk2.next()
                S.tt('dve', t_kk2[:, 0:n].rearrange("p (c s) -> p c s", s=HC), t_kkf[:, 0:n].rearrange("p (c s) -> p c s", s=HC),
                     t_d[:, 0:ncnk].unsqueeze(2).broadcast_to([128, ncnk, HC]), ALU.mult, reads=[t_kkf, t_d], writes=[t_kk2])
                t_o = ost.next()
                for c in range(ncnk):
                    cs_ = slice(c * HC, (c + 1) * HC)
                    psA = cx.ps.next()
                    S.mm(psA[0:HC, 0:HC], t_kk[:, cs_], t_qq[:, cs_], True, True, reads=[t_kk, t_qq], writes=[psA])
                    t_am = am.next()
                    S.tt('dve', t_am[:, :], psA[0:HC, 0:HC], ut[:, :], ALU.mult, reads=[psA, ut], writes=[t_am])
                    psT = cx.ps.next()
                    S.mm(psT[0:HC, 0:128], t_kk2[:, cs_], idb[:, :], True, True, reads=[t_kk2, idb], writes=[psT])
                    t_k2t = k2t.next()
                    S.copy('act', t_k2t[:, :], psT[0:HC, 0:128], reads=[psT], writes=[t_k2t])
                    psO = cx.ps.next()
                    S.mm(psO[:, 0:HC], b_v[:, c, :], t_am[:, :], True, False, reads=[b_v, t_am], writes=[psO], inc=False)
                    S.mm(psO[:, 0:HC], sb_cur[:, :], t_qq[:, cs_], False, True, reads=[sb_cur, t_qq], writes=[psO], inc=True)
                    S.copy('act', t_o[:, cs_], psO[:, 0:HC], reads=[psO], writes=[t_o])
                    psS = cx.ps.next()
                    S.mm(psS[:, 0:128], t_k2t[:, :], b_v[:, c, :], True, True, reads=[t_k2t, b_v], writes=[psS])
                    s_new = Sf.next()
                    S.stt('dve', s_new[:, :], s_cur[:, :], t_d[:, c:c + 1], psS[:, 0:128], ALU.mult, ALU.add,
                          reads=[s_cur, t_d, psS], writes=[s_new])
                    s_cur = s_new
                    sb_cur = Sb.next()
                    S.copy('pool', sb_cur[:, :], s_cur[:, :], reads=[s_cur], writes=[sb_cur])
                S.dma('pool', o_d[sc, :, s0:s0 + n], t_o[:, 0:n], reads=[t_o])
        cx.finish()
    return nc


def build_hgrn_post():
    nc = new_nc()
    of_d = din(nc, "of", [D, T])
    ob_d = din(nc, "ob", [D, T])
    g_d = din(nc, "gate", [D, T])
    xT_d = din(nc, "xT", [D, T])
    mod_d = din(nc, "mod", [128, 6, 8, 2])
    gain_d = din(nc, "gain", [128, 8])
    og_d = din(nc, "ogain", [128, 1])
    w_out_d = din(nc, "w_out", [D, D])
    x1_d = dout(nc, "x1", [D, T])
    h2_d = dout(nc, "h2", [D, T], BF16)
    with ExitStack() as es:
        cx = Ctx(nc, es)
        S = cx.S
        mod = S.sbuf(es, "mod", [128, 6, 8, 2], F32)
        S.dma('sp', mod[:], mod_d[:, :, :, :], writes=[mod])
        og = S.sbuf(es, "og", [128, 1], F32)
        S.dma('sp', og[:], og_d[:, :], writes=[og])
        with ExitStack() as es2:
            Z = S.sbuf(es2, "Z", [128, 8, T], BF16)
            with ExitStack() as es3:
                def mk(name, dt, k=2):
                    return Ring([S.sbuf(es3, "%s%d" % (name, i), [128, 512], dt) for i in range(k)])
                a_of, a_ob, a_g = mk("iof", F32), mk("iob", F32), mk("ig", F32)
                t_o, t_sq, t_rs, t_rd, t_zn, t_sg = mk("to", F32), mk("tsq", F32), mk("trs", F32), mk("trd", F32), mk("tzn", F32), mk("tsg", F32)
                for c in range(8):
                    for (c0, n, j) in TILES:
                        x_of = a_of.next()
                        S.dma('sp', x_of[:, 0:n], of_d[c * 128:(c + 1) * 128, c0:c0 + n], writes=[x_of])
                        x_ob = a_ob.next()
                        S.dma('sp', x_ob[:, 0:n], ob_d[c * 128:(c + 1) * 128, c0:c0 + n], writes=[x_ob])
                        x_g = a_g.next()
                        S.dma('sp', x_g[:, 0:n], g_d[c * 128:(c + 1) * 128, c0:c0 + n], writes=[x_g])
                        o = t_o.next()
                        S.tt('pool', o[:, 0:n], x_of[:, 0:n], x_ob[:, 0:n], ALU.add, reads=[x_of, x_ob], writes=[o])
                        sq = t_sq.next()
                        S.act(sq[:, 0:n], o[:, 0:n], AF.Square, reads=[o], writes=[sq])
                        ps = cx.ps.next()
                        S.mm(ps[:, 0:n], cx.ones[:, :], sq[:, 0:n], True, True, reads=[cx.ones, sq], writes=[ps])
                        r = t_rs.next()
                        S.act(r[:, 0:n], ps[:, 0:n], AF.Sqrt, reads=[ps, cx.eps], writes=[r], bias=cx.eps[:, 0:1], scale=1.0 / 128)
                        rd = t_rd.next()
                        S.op('dve', lambda e, rd=rd, r=r, n=n: e.reciprocal(rd[:, 0:n], r[:, 0:n]), reads=[r], writes=[rd])
                        zn = t_zn.next()
                        S.stt('dve', zn[:, 0:n], o[:, 0:n], og[:, 0:1], rd[:, 0:n], ALU.mult, ALU.mult, reads=[o, og, rd], writes=[zn])
                        sg = t_sg.next()
                        S.act(sg[:, 0:n], x_g[:, 0:n], AF.Silu, reads=[x_g], writes=[sg])
                        S.tt('pool', Z[:, c, c0:c0 + n], zn[:, 0:n], sg[:, 0:n], ALU.mult, reads=[zn, sg], writes=[Z])
                S.barrier()
                S.flush()
            emit_post(cx, es2, Z, 128, D, w_out_d, xT_d, x1_d, h2_d, mod, gain_d)
        cx.finish()
    return nc


def launch_hgrn(xT, mods, lbv, inp, i, j):
    nc = build_hgrn_pre()
    gain = fm(inp['norm_mix'][i])
    w_in = np.ascontiguousarray(inp['hgrn_w_in'][j])
    res = run(nc, [dict(xT=xT[r], mod=mods[i][r], gain=gain, w_in=w_in) for r in range(NCORES)])
    P = [np.asarray(o['p']) for o in res]
    Pfull = [np.concatenate([P[4 * b][:, TL:]] + [P[4 * b + q][:, :TL] for q in range(4)], axis=1) for b in range(NB)]
    idx_b = np.concatenate([np.arange(CTX - 1, -1, -1), CTX + np.arange(SEQ - 1, -1, -1)])
    rmask = np.ones((128, 512), np.float32)
    rmask[:, ::HC] = 0.0
    ut = np.triu(np.ones((HC, HC), np.float32))
    ident = np.eye(128, dtype=np.float32)
    maps = []
    for r in range(NCORES):
        qs, fs, vs, lbs = [], [], [], []
        for pp in range(2):
            p = 2 * r + pp
            b, hd = p // 8, p % 8
            rows = lambda k: slice(k * 1024 + hd * 128, k * 1024 + (hd + 1) * 128)
            qf = Pfull[b][rows(0)]
            vf = Pfull[b][rows(1)].T
            for d in range(2):
                fpre = Pfull[b][rows(2 + d)]
                if d == 1:
                    qd, fd, vd = qf[:, idx_b], fpre[:, idx_b], vf[idx_b]
                else:
                    qd, fd, vd = qf, fpre, vf
                qs.append(qd)
                fs.append(fd)
                vs.append(vd.reshape(NS // HC, HC, 128).transpose(1, 0, 2).reshape(HC, (NS // HC) * 128))
                lbs.append(lbv[:, hd])
        maps.append(dict(q=np.ascontiguousarray(np.stack(qs)), f=np.ascontiguousarray(np.stack(fs)),
                         v=np.ascontiguousarray(np.stack(vs)), lb=np.ascontiguousarray(np.stack(lbs, axis=1)),
                         rmask=rmask, ut=ut, ident=ident))
    nc = build_hgrn_scan()
    res = run(nc, maps)
    O = [np.asarray(o['o']) for o in res]
    of_full = [np.zeros((D, NS), np.float32) for _ in range(NB)]
    ob_full = [np.zeros((D, NS), np.float32) for _ in range(NB)]
    for r in range(NCORES):
        for pp in range(2):
            p = 2 * r + pp
            b, hd = p // 8, p % 8
            of_full[b][hd * 128:(hd + 1) * 128] = O[r][2 * pp]
            ob_full[b][hd * 128:(hd + 1) * 128] = O[r][2 * pp + 1][:, idx_b]
    nc = build_hgrn_post()
    gain2 = fm(inp['norm_ffn'][i])
    og = np.ascontiguousarray(inp['hgrn_out_norm'][j].reshape(128, 1))
    w_out = np.ascontiguousarray(inp['hgrn_w_out'][j])
    maps = []
    for r in range(NCORES):
        b, t0, t1 = core_tok(r)
        sel = lambda a: np.ascontiguousarray(np.concatenate([a[:, CTX + t0:CTX + t1], a[:, :CTX]], axis=1))
        maps.append(dict(of=sel(of_full[b]), ob=sel(ob_full[b]), gate=np.ascontiguousarray(P[r][4096:5120]), xT=xT[r],
                         mod=mods[i][r], gain=gain2, ogain=og, w_out=w_out))
    res = run(nc, maps)
    return [o['x1'] for o in res], [o['h2'] for o in res]


def kernel(**inputs):
    inp = {k: np.asarray(v) for k, v in inputs.items()}
    mods, lbv = launch_mod(inp)
    xT = host_xT(inp['x'], inp['ctx'])
    y = None
    for i in range(DEPTH):
        kind, j = i % 3, i // 3
        if kind == 0:
            qkv = launch_gqa_pre(xT, mods, inp, i, j)
            x1, h2 = launch_gqa_attn(qkv, xT, mods, inp, i, j)
        elif kind == 1:
            x1, h2 = launch_hgrn(xT, mods, lbv, inp, i, j)
        else:
            q, kv, kr = launch_mla_pre(xT, mods, inp, i, j)
            x1, h2 = launch_mla_attn(q, kv, kr, xT, mods, inp, i, j)
        xT, y = launch_ffn(h2, x1, mods, inp, i, i == DEPTH - 1)
    out = np.empty((NB, SEQ, D), np.float32)
    for r in range(NCORES):
        b, t0, t1 = core_tok(r)
        out[b, t0:t1, :] = np.asarray(y[r]).T
    return out
```

```python
import numpy as np
from contextlib import ExitStack
import concourse.bass as bass
import concourse.mybir as mybir
from concourse.bass_utils import run_bass_kernel_spmd

F32 = mybir.dt.float32
BF16 = mybir.dt.bfloat16
AF = mybir.ActivationFunctionType
ALU = mybir.AluOpType
AX = mybir.AxisListType

ENGS = ('pe', 'act', 'dve', 'pool', 'sp')
NDSEM = 24


class Buf:
    __slots__ = ('t', 'name', 'w', 'r')

    def __init__(self, t, name=''):
        self.t = t
        self.name = name
        self.w = None
        self.r = {}

    def __getitem__(self, k):
        return self.t[k]


class Sched:
    def __init__(self, nc, es):
        self.nc = nc
        self.es = es
        self.sem = {}
        self.cnt = {}
        for e in ENGS:
            self.sem[e] = es.enter_context(nc.semaphore('s_' + e))
            self.cnt[e] = 0
        for i in range(NDSEM):
            k = ('d', i)
            self.sem[k] = es.enter_context(nc.semaphore('sd%d' % i))
            self.cnt[k] = 0
        self.rr = 0
        self.rr2 = 0
        self.q = {e: [] for e in ENGS}
        self.seen = {e: {} for e in ENGS}
        self.nops = 0

    def sbuf(self, es, name, shape, dtype):
        self.uid = getattr(self, 'uid', 0) + 1
        t = es.enter_context(self.nc.sbuf_tensor("sb%d_%s" % (self.uid, name), list(shape), dtype))
        return Buf(t, name)

    def psum(self, es, name, shape=(128, 512), dtype=F32):
        t = es.enter_context(self.nc.psum_tensor("pp_" + name, list(shape), dtype))
        return Buf(t, name)

    def _waits(self, eng, reads, writes):
        deps = {}
        for b in reads:
            if b.w is not None and deps.get(b.w[0], 0) < b.w[1]:
                deps[b.w[0]] = b.w[1]
        for b in writes:
            if b.w is not None and deps.get(b.w[0], 0) < b.w[1]:
                deps[b.w[0]] = b.w[1]
            for k, v in b.r.items():
                if deps.get(k, 0) < v:
                    deps[k] = v
        waits = []
        seen = self.seen[eng]
        for k, v in deps.items():
            if k == 'pe' and eng == 'pe':
                continue
            if seen.get(k, 0) < v:
                seen[k] = v
                waits.append((k, v))
        return waits

    def op(self, eng, fn, reads=(), writes=(), inc=True):
        waits = self._waits(eng, reads, writes)
        n = self.cnt[eng] + 1
        if inc:
            self.cnt[eng] = n
        self.q[eng].append((waits, fn, (eng, 1) if inc else None))
        for b in reads:
            b.r[eng] = n
        for b in writes:
            b.w = (eng, n)
            b.r = {}
        self.nops += 1

    def dma(self, eng, out, in_, reads=(), writes=()):
        waits = self._waits(eng, reads, writes)
        if eng == 'sp':
            k = ('d', self.rr % 16)
            self.rr += 1
        else:
            k = ('d', 16 + self.rr2 % (NDSEM - 16))
            self.rr2 += 1
        if self.cnt[k] > 0 and self.seen[eng].get(k, 0) < self.cnt[k]:
            self.seen[eng][k] = self.cnt[k]
            waits.append((k, self.cnt[k]))
        self.cnt[k] += 16
        n = self.cnt[k]
        self.q[eng].append((waits, lambda e: e.dma_start(out=out, in_=in_), (k, 16)))
        for b in reads:
            b.r[k] = n
        for b in writes:
            b.w = (k, n)
            b.r = {}
        self.nops += 1

    def barrier(self):
        for e in ENGS:
            waits = []
            for k, v in self.cnt.items():
                if v > 0 and self.seen[e].get(k, 0) < v and k != e:
                    self.seen[e][k] = v
                    waits.append((k, v))
            if waits:
                self.q[e].append((waits, None, None))

    def flush(self):
        nc = self.nc
        sem = self.sem
        with nc.Block() as block:
            for e, deco in (('pe', block.tensor), ('act', block.scalar), ('dve', block.vector),
                            ('pool', block.gpsimd), ('sp', block.sync)):
                items = self.q[e]

                def body(eng, items=items):
                    for waits, fn, inc in items:
                        for k, v in waits:
                            eng.wait_ge(sem[k], v)
                        if fn is not None:
                            ins = fn(eng)
                            if inc is not None:
                                ins.then_inc(sem[inc[0]], inc[1])
                deco(body)
        self.q = {e: [] for e in ENGS}

    def mm(self, out, lhsT, rhs, start, stop, reads=(), writes=(), inc=None):
        if inc is None:
            inc = stop
        self.op('pe', lambda e: e.matmul(out, lhsT, rhs, start=start, stop=stop), reads, writes, inc=inc)

    def act(self, out, in_, func, reads=(), writes=(), bias=None, scale=None, eng='act'):
        kw = {}
        if bias is not None:
            kw['bias'] = bias
        if scale is not None:
            kw['scale'] = scale
        self.op('act', lambda e: e.activation(out, in_, func, **kw), reads, writes)

    def ts(self, eng, out, in0, s1, s2, op0, op1=None, reads=(), writes=()):
        if op1 is None:
            self.op(eng, lambda e: e.tensor_scalar(out, in0, s1, None, op0), reads, writes)
        else:
            self.op(eng, lambda e: e.tensor_scalar(out, in0, s1, s2, op0, op1), reads, writes)

    def tt(self, eng, out, in0, in1, op, reads=(), writes=()):
        self.op(eng, lambda e: e.tensor_tensor(out, in0, in1, op), reads, writes)

    def stt(self, eng, out, in0, scalar, in1, op0, op1, reads=(), writes=()):
        self.op(eng, lambda e: e.scalar_tensor_tensor(out, in0, scalar, in1, op0, op1), reads, writes)

    def copy(self, eng, out, in_, reads=(), writes=()):
        if eng == 'act':
            self.op('act', lambda e: e.activation(out, in_, AF.Copy), reads, writes)
        else:
            self.op(eng, lambda e: e.tensor_copy(out, in_), reads, writes)

    def memset(self, eng, out, val, writes=()):
        self.op(eng, lambda e: e.memset(out, val), (), writes)


class Ring:
    def __init__(self, bufs):
        self.bufs = bufs
        self.i = 0

    def next(self):
        b = self.bufs[self.i % len(self.bufs)]
        self.i += 1
        return b


D = 1024
SEQ = 8192
CTX = 256
NB = 2
DEPTH = 4
TL = 2048
T = TL + CTX
NK = SEQ + CTX
EPS = 1e-6
DFF = 3072
TILES = [(0, 512, 0), (512, 512, 0), (1024, 512, 0), (1536, 512, 0), (2048, 256, 1)]


def new_nc():
    return bass.Bass("TRN2", target_bir_lowering=False)


def din(nc, name, shape, dtype=F32):
    return nc.dram_tensor(name, list(shape), dtype, kind="ExternalInput").ap()


def dout(nc, name, shape, dtype=F32):
    return nc.dram_tensor(name, list(shape), dtype, kind="ExternalOutput").ap()


class Ctx:
    def __init__(self, nc, es, npsum=8):
        self.nc = nc
        self.es = es
        self.S = Sched(nc, es)
        S = self.S
        self.ps = Ring([S.psum(es, "ps%d" % i) for i in range(npsum)])
        self.ones = S.sbuf(es, "ones_f", [128, 128], F32)
        S.memset('dve', self.ones[:], 1.0, writes=[self.ones])
        self.ones_bf = S.sbuf(es, "ones_b", [128, 128], BF16)
        S.memset('pool', self.ones_bf[:], 1.0, writes=[self.ones_bf])
        self.eps = S.sbuf(es, "eps_t", [128, 1], F32)
        S.memset('dve', self.eps[:], EPS, writes=[self.eps])
        self.cast_i = 0

    def finish(self):
        self.S.barrier()
        self.S.flush()


def emit_mod(cx, es_outer, cvec_d, w_d, b_d, ns, ncol):
    S = cx.S
    mod = S.sbuf(es_outer, "mod", [128, ns, 8, ncol], F32)
    with ExitStack() as es:
        craw = S.sbuf(es, "craw", [128, 8, ncol], F32)
        sc = S.sbuf(es, "silu_c", [128, 8, ncol], F32)
        bada = S.sbuf(es, "bada", [128, ns * 8], F32)
        wb = Ring([S.sbuf(es, "wada%d" % i, [128, 8, 1024], F32) for i in range(2)])
        S.dma('sp', craw[:], cvec_d[:, :, :], writes=[craw])
        S.dma('sp', bada[:], b_d[:, :], writes=[bada])
        S.act(sc[:], craw[:], AF.Silu, reads=[craw], writes=[sc])
        for s in range(ns):
            w = wb.next()
            src = w_d[:, s * 1024:(s + 1) * 1024].rearrange("(c p) o -> p c o", p=128)
            S.dma('sp', w[:, 0:4, :], src[:, 0:4, :], writes=[w])
            S.dma('sp', w[:, 4:8, :], src[:, 4:8, :], writes=[w])
            ps = cx.ps.next()
            for o in range(8):
                for kc in range(8):
                    last = (o == 7 and kc == 7)
                    S.mm(ps[:, o * ncol:(o + 1) * ncol], w[:, kc, o * 128:(o + 1) * 128], sc[:, kc, :], kc == 0, kc == 7,
                         reads=[w, sc], writes=[ps], inc=last)
            S.tt('dve', mod[:, s, :, :], ps[:, 0:8 * ncol].rearrange("p (c j) -> p c j", j=ncol),
                 bada[:, s * 8:(s + 1) * 8].unsqueeze(2).broadcast_to([128, 8, ncol]), ALU.add,
                 reads=[ps, bada], writes=[mod])
        S.barrier()
        S.flush()
    return mod


def build_mod():
    nc = new_nc()
    cvec = din(nc, "cvec", [128, 8, 3])
    w_d = din(nc, "w", [D, 3 * D])
    b_d = din(nc, "b", [128, 24])
    lbin = din(nc, "lbin", [128, 8, 4])
    mod_o = dout(nc, "mod", [128, 3, 8, 3])
    lb_o = dout(nc, "lb", [128, 8])
    with ExitStack() as es:
        cx = Ctx(nc, es)
        S = cx.S
        mod = emit_mod(cx, es, cvec, w_d, b_d, 3, 3)
        S.dma('pool', mod_o[:, :, :, :], mod[:], reads=[mod])
        l0 = S.sbuf(es, "l0", [128, 8, 4], F32)
        l1 = S.sbuf(es, "l1", [128, 8, 4], F32)
        l2 = S.sbuf(es, "l2", [128, 8], F32)
        l3 = S.sbuf(es, "l3", [128, 8], F32)
        l4 = S.sbuf(es, "l4", [128, 8], F32)
        S.dma('sp', l0[:], lbin[:, :, :], writes=[l0])
        S.act(l1[:], l0[:], AF.Exp, reads=[l0], writes=[l1])
        S.op('dve', lambda e: e.tensor_reduce(l2[:], l1[:], AX.X, ALU.add), reads=[l1], writes=[l2])
        S.op('dve', lambda e: e.reciprocal(l3[:], l2[:]), reads=[l2], writes=[l3])
        S.tt('dve', l4[:], l1[:, :, 1], l3[:], ALU.mult, reads=[l1, l3], writes=[l4])
        S.dma('pool', lb_o[:, :], l4[:], reads=[l4])
        cx.finish()
    return nc


def launch_mod(inp):
    nc = build_mod()
    c, c_ctx = inp['c'], inp['c_ctx']
    cvec = np.ascontiguousarray(np.stack([fm(c[0]), fm(c[1]), fm(c_ctx)], axis=-1))
    lbin = np.ascontiguousarray(inp['hgrn_lower_bounds'].T.reshape(8, 128, 4).transpose(1, 0, 2))
    maps = []
    for r in range(NCORES):
        layer, hf = r // 2, r % 2
        w = np.ascontiguousarray(inp['w_ada'][layer][:, hf * 3 * D:(hf + 1) * 3 * D])
        b = fm(inp['b_ada'][layer][hf * 3 * D:(hf + 1) * 3 * D])
        maps.append(dict(cvec=cvec, w=w, b=b, lbin=lbin))
    res = run(nc, maps)
    mods = []
    for layer in range(DEPTH):
        full = np.concatenate([res[2 * layer]['mod'], res[2 * layer + 1]['mod']], axis=1)
        per_b = [np.ascontiguousarray(full[:, :, :, [b, 2]]) for b in range(NB)]
        mods.append([per_b[r // 4] for r in range(NCORES)])
    return mods, np.ascontiguousarray(res[0]['lb'])


def emit_affine(cx, es, mod, s_shift, s_scale, gain_d, name):
    S = cx.S
    gain = S.sbuf(es, name + "_g", [128, 8], F32)
    A = S.sbuf(es, name + "_A", [128, 8, 2], F32)
    S.dma('sp', gain[:], gain_d[:, :], writes=[gain])
    S.stt('dve', A[:], mod[:, s_scale, :, :], 1.0, gain[:, :].unsqueeze(2).broadcast_to([128, 8, 2]), ALU.add, ALU.mult,
          reads=[mod, gain], writes=[A])
    return A


def emit_norm_mod(cx, es, x_src, A, mod, s_shift, h, tiles=TILES, xkeep=None, tag="nm"):
    S = cx.S
    sq = Ring([S.sbuf(es, tag + "_sq%d" % i, [128, 8, 512], F32) for i in range(2)])
    rs = Ring([S.sbuf(es, tag + "_rs%d" % i, [128, 512], F32) for i in range(2)])
    rstd = Ring([S.sbuf(es, tag + "_rstd%d" % i, [128, 512], F32) for i in range(2)])
    for ti, (c0, n, j) in enumerate(tiles):
        xb, xap = x_src(ti)
        q = sq.next()
        S.tt('pool', q[:, :, 0:n], xap, xap, ALU.mult, reads=[xb], writes=[q])
        ps = cx.ps.next()
        for c in range(8):
            S.mm(ps[:, 0:n], cx.ones[:, :], q[:, c, 0:n], c == 0, c == 7, reads=[cx.ones, q], writes=[ps])
        r = rs.next()
        S.act(r[:, 0:n], ps[:, 0:n], AF.Sqrt, reads=[ps, cx.eps], writes=[r], bias=cx.eps[:, 0:1], scale=1.0 / D)
        rd = rstd.next()
        S.op('dve', lambda e, rd=rd, r=r, n=n: e.reciprocal(rd[:, 0:n], r[:, 0:n]), reads=[r], writes=[rd])
        S.tt('dve', q[:, :, 0:n], xap, rd[:, 0:n].unsqueeze(1).broadcast_to([128, 8, n]), ALU.mult,
             reads=[xb, rd], writes=[q])
        for c in range(8):
            S.act(h[:, c, c0:c0 + n], q[:, c, 0:n], AF.Identity, reads=[q, A, mod], writes=[h],
                  bias=mod[:, s_shift, c, j:j + 1], scale=A[:, c, j:j + 1])


def linear(cx, es, W_d, K, kp, ochunks, xb, tiles, evac, gcols=512, tag="lin", cast_engs=('dve', 'pool')):
    S = cx.S
    KC = K // kp
    stg = Ring([S.sbuf(es, tag + "_stg%d" % i, [kp, KC, gcols], F32) for i in range(2)])
    wbf = Ring([S.sbuf(es, tag + "_wbf%d" % i, [kp, KC, gcols], BF16) for i in range(2)])
    groups = []
    cur = []
    for oi, (c0, m) in enumerate(ochunks):
        if cur and (c0 + m - cur[0][1] > gcols or c0 != cur[-1][1] + cur[-1][2]):
            groups.append(cur)
            cur = []
        cur.append((oi, c0, m))
    if cur:
        groups.append(cur)
    Wv = W_d.rearrange("(c p) o -> p c o", p=kp)
    for g in groups:
        ga = g[0][1]
        gb = g[-1][1] + g[-1][2]
        w = gb - ga
        st = stg.next()
        half = max(1, KC // 2)
        S.dma('sp', st[:, 0:half, 0:w], Wv[:, 0:half, ga:gb], writes=[st])
        if half < KC:
            S.dma('sp', st[:, half:KC, 0:w], Wv[:, half:KC, ga:gb], writes=[st])
        wb = wbf.next()
        ce = cast_engs[cx.cast_i % len(cast_engs)]
        cx.cast_i += 1
        S.copy(ce, wb[:, :, 0:w], st[:, :, 0:w], reads=[st], writes=[wb])
        for (oi, c0, m) in g:
            oc = c0 - ga
            for ti, (t0, n, j) in enumerate(tiles):
                ps = cx.ps.next()
                for kc in range(KC):
                    S.mm(ps[0:m, 0:n], wb[:, kc, oc:oc + m], xb[:, kc, t0:t0 + n], kc == 0, kc == KC - 1,
                         reads=[wb, xb], writes=[ps])
                evac(oi, ti, ps, m, t0, n, j)


def emit_headnorm_rope(cx, pools, ps, m, n, gain_ap, gain_buf, cs, sn, t0, out_ap, out_buf, hd, rope_lo, half, inv_dim):
    S = cx.S
    sqp, rsp, rdp, qnp, t1p, t2p = pools
    if gain_ap is not None:
        sq = sqp.next()
        S.act(sq[0:m, 0:n], ps[0:m, 0:n], AF.Square, reads=[ps], writes=[sq])
        ps2 = cx.ps.next()
        S.mm(ps2[0:m, 0:n], cx.ones[0:m, 0:m], sq[0:m, 0:n], True, True, reads=[cx.ones, sq], writes=[ps2])
        r = rsp.next()
        S.act(r[0:m, 0:n], ps2[0:m, 0:n], AF.Sqrt, reads=[ps2, cx.eps], writes=[r], bias=cx.eps[0:m, 0:1], scale=inv_dim)
        rd = rdp.next()
        S.op('dve', lambda e: e.reciprocal(rd[0:m, 0:n], r[0:m, 0:n]), reads=[r], writes=[rd])
        qn = qnp.next()
        S.stt('dve', qn[0:m, 0:n], ps[0:m, 0:n], gain_ap, rd[0:m, 0:n], ALU.mult, ALU.mult,
              reads=[ps, gain_buf, rd], writes=[qn])
    else:
        qn = qnp.next()
        S.copy('act', qn[0:m, 0:n], ps[0:m, 0:n], reads=[ps], writes=[qn])
    t1 = t1p.next()
    S.tt('pool', t1[0:m, 0:n], qn[0:m, 0:n], cs[0:m, t0:t0 + n], ALU.mult, reads=[qn, cs], writes=[t1])
    t2 = t2p.next()
    x1, x2 = rope_lo
    if 2 * half != m:
        S.memset('pool', t2[0:m, 0:n], 0.0, writes=[t2])
    S.tt('dve', t2[x1:x1 + half, 0:n], qn[x2:x2 + half, 0:n], sn[x2:x2 + half, t0:t0 + n], ALU.mult,
         reads=[qn, sn], writes=[t2])
    S.tt('dve', t2[x2:x2 + half, 0:n], qn[x1:x1 + half, 0:n], sn[x1:x1 + half, t0:t0 + n], ALU.mult,
         reads=[qn, sn], writes=[t2])
    S.tt('pool', out_ap, t1[0:m, 0:n], t2[0:m, 0:n], ALU.add, reads=[t1, t2], writes=[out_buf])


def fm(v):
    v = np.asarray(v)
    return np.ascontiguousarray(v.reshape(-1, 128).T)


def build_gqa_pre():
    nc = new_nc()
    xT = din(nc, "xT", [D, T])
    mod_d = din(nc, "mod", [128, 6, 8, 2])
    gain = din(nc, "gain", [128, 8])
    w_in = din(nc, "w_in", [D, 4096])
    qkg = din(nc, "qkg", [128, 2])
    cs_d = din(nc, "cs", [128, T])
    sn_d = din(nc, "sn", [128, T])
    qkv = dout(nc, "qkv", [4096, T], BF16)
    with ExitStack() as es:
        cx = Ctx(nc, es)
        S = cx.S
        mod = S.sbuf(es, "mod", [128, 6, 8, 2], F32)
        S.dma('sp', mod[:], mod_d[:, :, :, :], writes=[mod])
        h = S.sbuf(es, "h", [128, 8, T], BF16)
        cs = S.sbuf(es, "cs_s", [128, T], F32)
        sn = S.sbuf(es, "sn_s", [128, T], F32)
        g2 = S.sbuf(es, "qkg_s", [128, 2], F32)
        S.dma('sp', cs[:], cs_d[:, :], writes=[cs])
        S.dma('sp', sn[:], sn_d[:, :], writes=[sn])
        S.dma('sp', g2[:], qkg[:, :], writes=[g2])
        with ExitStack() as es2:
            A = emit_affine(cx, es2, mod, 0, 1, gain, "afa")
            xt = Ring([S.sbuf(es2, "xt%d" % i, [128, 8, 512], F32) for i in range(2)])
            xv = xT.rearrange("(c p) t -> p c t", p=128)

            def x_src(ti):
                c0, n, j = TILES[ti]
                b = xt.next()
                S.dma('sp', b[:, :, 0:n], xv[:, :, c0:c0 + n], writes=[b])
                return b, b[:, :, 0:n]
            emit_norm_mod(cx, es2, x_src, A, mod, 0, h)
            S.barrier()
            S.flush()
        with ExitStack() as es3:
            def mk(name, dt):
                return Ring([S.sbuf(es3, "%s%d" % (name, i), [128, 512], dt) for i in range(2)])
            pools = (mk("sq", F32), mk("rs", F32), mk("rd", F32), mk("qn", F32), mk("t1", F32), mk("t2", F32))
            ob = mk("ob", BF16)
            ochunks = [(o * 128, 128) for o in range(32)]

            def evac(oi, ti, ps, m, t0, n, j):
                o = ob.next()
                if oi < 24:
                    gi = 0 if oi < 16 else 1
                    emit_headnorm_rope(cx, pools, ps, 128, n, g2[:, gi:gi + 1], g2, cs, sn, t0, o[:, 0:n], o, 128, (0, 64), 64,
                                       1.0 / 128)
                else:
                    S.copy('act', o[:, 0:n], ps[:, 0:n], reads=[ps], writes=[o])
                S.dma('pool', qkv[oi * 128:(oi + 1) * 128, t0:t0 + n], o[:, 0:n], reads=[o])
            linear(cx, es3, w_in, D, 128, ochunks, h, TILES, evac)
            cx.finish()
    return nc


def rope_tables_axial(pos, rot_dim):
    GRID_W = 64
    row = (pos // GRID_W).astype(np.float32)
    col = (pos % GRID_W).astype(np.float32)
    axis_dim = rot_dim // 2
    inv_freq = np.power(np.float32(10000.0), -np.arange(0, axis_dim, 2, dtype=np.float32) / np.float32(axis_dim)).astype(np.float32)
    ang = np.concatenate([row[:, None] * inv_freq, col[:, None] * inv_freq], axis=-1).astype(np.float32)
    return np.cos(ang).astype(np.float32), np.sin(ang).astype(np.float32)


NCORES = 8


def core_tok(r):
    b = r // 4
    q = r % 4
    return b, q * TL, (q + 1) * TL


def host_xT(x, ctx):
    out = []
    for r in range(NCORES):
        b, t0, t1 = core_tok(r)
        out.append(np.ascontiguousarray(np.concatenate([x[b, t0:t1].T, ctx[b].T], axis=1)))
    return out


def host_cvec(c, c_ctx):
    return [np.ascontiguousarray(np.stack([fm(c[r // 4]), fm(c_ctx)], axis=-1)) for r in range(NCORES)]


def host_rope_fm(rot_dim, x1, x2, m):
    half = rot_dim // 2
    res = []
    for r in range(NCORES):
        b, t0, t1 = core_tok(r)
        cos, sin = rope_tables_axial(np.arange(t0, t1), rot_dim)
        cs = np.ones((m, T), np.float32)
        sn = np.zeros((m, T), np.float32)
        cs[x1:x1 + half, :TL] = cos.T
        cs[x2:x2 + half, :TL] = cos.T
        sn[x1:x1 + half, :TL] = sin.T
        sn[x2:x2 + half, :TL] = -sin.T
        res.append((cs, sn))
    return res


TRACE = False


def run(nc, in_maps):
    if TRACE:
        res = run_bass_kernel_spmd(nc, in_maps, core_ids=list(range(NCORES)), trace=True)
        print("EXEC_TIME_NS", res.exec_time_ns)
        return res.results
    res = run_bass_kernel_spmd(nc, in_maps, core_ids=list(range(NCORES)))
    t = getattr(res, 'exec_time_ns', None)
    if t is not None:
        print("EXEC_TIME_NS", t)
    return res.results


def launch_gqa_pre(xT, mods, inp, i, j):
    nc = build_gqa_pre()
    tabs = host_rope_fm(128, 0, 64, 128)
    gain = fm(inp['norm_mix'][i])
    w_in = np.ascontiguousarray(inp['gqa_w_in'][j])
    qkg = np.ascontiguousarray(np.stack([inp['gqa_q_norm'][j], inp['gqa_k_norm'][j]], axis=1))
    maps = [dict(xT=xT[r], mod=mods[i][r], gain=gain, w_in=w_in, qkg=qkg,
                 cs=tabs[r][0], sn=tabs[r][1]) for r in range(NCORES)]
    return [o['qkv'] for o in run(nc, maps)]


def emit_attention(cx, es_outer, q_d, kT_d, v_d, nheads, group, dk, dv, scale, nkc, AO, LOOK=3):
    S = cx.S
    with ExitStack() as es:
        kt = Ring([S.sbuf(es, "kt%d" % i, [dk, NK], BF16) for i in range(2)])
        vv = Ring([S.sbuf(es, "vv%d" % i, [128, nkc, dv], BF16) for i in range(2)])
        qt = Ring([S.sbuf(es, "qt%d" % i, [dk, T], BF16) for i in range(2)])
        pT = Ring([S.sbuf(es, "pT%d" % i, [128, 512], BF16) for i in range(LOOK + 2)])
        rdp = Ring([S.sbuf(es, "ard%d" % i, [128, 512], F32) for i in range(2)])
        psO = Ring([S.psum(es, "psO%d" % i) for i in range(2)])
        psD = Ring([S.psum(es, "psD%d" % i) for i in range(2)])
        nctx = CTX // 128
        pending = []

        def second_half(it):
            (v, p, po, pd, kc, n, first, last, h, c0) = it
            S.mm(po[0:dv, 0:n], v[:, kc, :], p[:, 0:n], first, last, reads=[v, p], writes=[po], inc=False)
            S.mm(pd[0:dv, 0:n], cx.ones_bf[:, 0:dv], p[:, 0:n], first, last, reads=[cx.ones_bf, p],
                 writes=[pd], inc=True)
            if last:
                rd = rdp.next()
                S.op('dve', lambda e, rd=rd, pd=pd, n=n: e.reciprocal(rd[0:dv, 0:n], pd[0:dv, 0:n]),
                     reads=[pd], writes=[rd])
                S.tt('dve', AO[:, h, c0:c0 + n], po[0:dv, 0:n], rd[0:dv, 0:n], ALU.mult, reads=[po, rd], writes=[AO])

        for g in range(nheads // group):
            k = kt.next()
            S.dma('sp', k[:, :], kT_d[g * dk:(g + 1) * dk, :], writes=[k])
            v = vv.next()
            S.dma('sp', v[:, :, :], v_d[g].rearrange("p (c d) -> p c d", d=dv), writes=[v])
            for hh in range(group):
                h = g * group + hh
                q = qt.next()
                S.dma('sp', q[:, :], q_d[h * dk:(h + 1) * dk, :], writes=[q])
                for (c0, n, j) in TILES:
                    kcs = list(range(nkc)) if j == 0 else list(range(nkc - nctx, nkc))
                    po = psO.next()
                    pd = psD.next()
                    for idx, kc in enumerate(kcs):
                        ps = cx.ps.next()
                        S.mm(ps[:, 0:n], k[:, kc * 128:(kc + 1) * 128], q[:, c0:c0 + n], True, True,
                             reads=[k, q], writes=[ps])
                        p = pT.next()
                        S.act(p[:, 0:n], ps[:, 0:n], AF.Exp, reads=[ps], writes=[p], scale=scale)
                        pending.append((v, p, po, pd, kc, n, idx == 0, idx == len(kcs) - 1, h, c0))
                        if len(pending) > LOOK:
                            second_half(pending.pop(0))
        while pending:
            second_half(pending.pop(0))
        S.barrier()
        S.flush()


def emit_post(cx, es_outer, AO, kp, K, w_out_d, xT_d, x1_d, h2_d, mod, gain_d, hres=None):
    S = cx.S
    x1buf = Buf(None, "x1_dram")
    with ExitStack() as es:
        xt = Ring([S.sbuf(es, "pxt%d" % i, [128, 512], F32) for i in range(3)])
        xo = Ring([S.sbuf(es, "pxo%d" % i, [128, 512], F32) for i in range(3)])

        def evac(oi, ti, ps, m, t0, n, j):
            a = xt.next()
            S.dma('sp', a[:, 0:n], xT_d[oi * 128:(oi + 1) * 128, t0:t0 + n], writes=[a])
            o = xo.next()
            S.stt('dve', o[:, 0:n], ps[:, 0:n], mod[:, 2, oi, j:j + 1], a[:, 0:n], ALU.mult, ALU.add,
                  reads=[ps, mod, a], writes=[o])
            S.dma('pool', x1_d[oi * 128:(oi + 1) * 128, t0:t0 + n], o[:, 0:n], reads=[o], writes=[x1buf])
        linear(cx, es, w_out_d, K, kp, [(o * 128, 128) for o in range(8)], AO, TILES, evac, gcols=256, tag="wo")
        S.barrier()
        S.flush()
    with ExitStack() as es:
        h2 = S.sbuf(es, "h2", [128, 8, T], BF16)
        A = emit_affine(cx, es, mod, 3, 4, gain_d, "aff")
        xt = Ring([S.sbuf(es, "nxt%d" % i, [128, 8, 512], F32) for i in range(2)])
        xv = x1_d.rearrange("(c p) t -> p c t", p=128)

        def x_src(ti):
            c0, n, j = TILES[ti]
            b = xt.next()
            S.dma('sp', b[:, :, 0:n], xv[:, :, c0:c0 + n], reads=[x1buf], writes=[b])
            return b, b[:, :, 0:n]
        emit_norm_mod(cx, es, x_src, A, mod, 3, h2, tag="n2")
        S.dma('pool', h2_d.rearrange("(c p) t -> p c t", p=128), h2[:, :, :], reads=[h2])
        S.barrier()
        S.flush()


def build_attn_post(nheads, group, dk, dv, scale):
    nc = new_nc()
    nkv = nheads // group
    nkc = NK // 128
    q_d = din(nc, "q", [nheads * dk, T], BF16)
    kT_d = din(nc, "kT", [nkv * dk, NK], BF16)
    v_d = din(nc, "v", [nkv, 128, nkc * dv], BF16)
    xT_d = din(nc, "xT", [D, T])
    mod_d = din(nc, "mod", [128, 6, 8, 2])
    gain_d = din(nc, "gain", [128, 8])
    w_out_d = din(nc, "w_out", [nheads * dv, D])
    x1_d = dout(nc, "x1", [D, T])
    h2_d = dout(nc, "h2", [D, T], BF16)
    with ExitStack() as es:
        cx = Ctx(nc, es, npsum=4)
        S = cx.S
        mod = S.sbuf(es, "mod", [128, 6, 8, 2], F32)
        S.dma('sp', mod[:], mod_d[:, :, :, :], writes=[mod])
        with ExitStack() as es2:
            AO = S.sbuf(es2, "AO", [dv, nheads, T], BF16)
            emit_attention(cx, es2, q_d, kT_d, v_d, nheads, group, dk, dv, scale, nkc, AO)
            emit_post(cx, es2, AO, dv, nheads * dv, w_out_d, xT_d, x1_d, h2_d, mod, gain_d)
        cx.finish()
    return nc


def host_kv_gqa(qkv):
    kTs, vs = [], []
    for b in range(NB):
        k = np.concatenate([qkv[4 * b + q][2048:3072, :TL] for q in range(4)] + [qkv[4 * b][2048:3072, TL:]], axis=1)
        v = np.concatenate([qkv[4 * b + q][3072:4096, :TL] for q in range(4)] + [qkv[4 * b][3072:4096, TL:]], axis=1)
        kTs.append(np.ascontiguousarray(k))
        v4 = v.reshape(8, 128, NK // 128, 128)
        vs.append(np.ascontiguousarray(v4.transpose(0, 3, 2, 1)).reshape(8, 128, (NK // 128) * 128))
    return kTs, vs


def launch_gqa_attn(qkv, xT, mods, inp, i, j):
    nc = build_attn_post(16, 2, 128, 128, 128 ** -0.5)
    kTs, vs = host_kv_gqa(qkv)
    gain = fm(inp['norm_ffn'][i])
    w_out = np.ascontiguousarray(inp['gqa_w_out'][j])
    maps = [dict(q=np.ascontiguousarray(qkv[r][0:2048]), kT=kTs[r // 4], v=vs[r // 4], xT=xT[r], mod=mods[i][r],
                 gain=gain, w_out=w_out) for r in range(NCORES)]
    res = run(nc, maps)
    return [o['x1'] for o in res], [o['h2'] for o in res]


TP = TL + 2 + CTX + 2
UT = [(0, 512), (510, 512), (1020, 512), (1530, 512), (2040, 268)]
OT = [(1, 512, 0), (513, 512, 0), (1025, 512, 0), (1537, 512, 0), (2051, 256, 1)]


def tcol(p):
    return p - 1 if p < 2050 else p - 3


def build_ffn(final):
    nc = new_nc()
    h2_d = din(nc, "h2p", [D, TP], BF16)
    x1_d = din(nc, "x1", [D, T])
    mod_d = din(nc, "mod", [128, 6, 8, 2])
    w_in_d = din(nc, "w_in", [D, 2 * DFF])
    cw_d = din(nc, "cw", [128, 48, 3])
    cb_d = din(nc, "cb", [128, 48])
    w_out_d = din(nc, "w_out", [DFF, D])
    x2_d = dout(nc, "x2", [D, T])
    if final:
        fg_d = din(nc, "fgain", [128, 8])
        y_d = dout(nc, "y", [D, TL])
    with ExitStack() as es:
        cx = Ctx(nc, es)
        S = cx.S
        mod = S.sbuf(es, "mod", [128, 6, 8, 2], F32)
        S.dma('sp', mod[:], mod_d[:, :, :, :], writes=[mod])
        cw = S.sbuf(es, "cw", [128, 48, 3], F32)
        cb = S.sbuf(es, "cb", [128, 48], F32)
        S.dma('sp', cw[:], cw_d[:, :, :], writes=[cw])
        S.dma('sp', cb[:], cb_d[:, :], writes=[cb])
        x2buf = Buf(None, "x2_dram")
        with ExitStack() as esG:
            G = S.sbuf(esG, "G", [128, 24, TP], BF16)
            with ExitStack() as es1:
                h2 = S.sbuf(es1, "h2", [128, 8, TP], BF16)
                S.dma('sp', h2[:, 0:4, :], h2_d.rearrange("(c p) t -> p c t", p=128)[:, 0:4, :], writes=[h2])
                S.dma('sp', h2[:, 4:8, :], h2_d.rearrange("(c p) t -> p c t", p=128)[:, 4:8, :], writes=[h2])
                ta = Ring([S.sbuf(es1, "ta%d" % i, [128, 512], F32) for i in range(3)])
                sa = [S.sbuf(es1, "sa%d" % i, [128, 512], F32) for i in range(5)]
                ochunks = []
                for j in range(24):
                    ochunks += [(j * 128, 128), (DFF + j * 128, 128)]

                def evac(oi, ti, ps, m, u0, un, jj):
                    jch = oi // 2
                    isval = oi % 2
                    ch = jch + 24 * isval
                    on = un - 2
                    t = ta.next()
                    S.act(t[:, 0:on], ps[:, 1:1 + on], AF.Identity, reads=[ps, cw, cb], writes=[t],
                          bias=cb[:, ch:ch + 1], scale=cw[:, ch, 1:2])
                    S.stt('dve', t[:, 0:on], ps[:, 0:on], cw[:, ch, 0:1], t[:, 0:on], ALU.mult, ALU.add,
                          reads=[ps, cw, t], writes=[t])
                    S.stt('dve', t[:, 0:on], ps[:, 2:2 + on], cw[:, ch, 2:3], t[:, 0:on], ALU.mult, ALU.add,
                          reads=[ps, cw, t], writes=[t])
                    if not isval:
                        S.act(sa[ti][:, 0:on], t[:, 0:on], AF.Silu, reads=[t], writes=[sa[ti]])
                    else:
                        S.tt('pool', G[:, jch, u0 + 1:u0 + 1 + on], sa[ti][:, 0:on], t[:, 0:on], ALU.mult,
                             reads=[sa[ti], t], writes=[G])
                linear(cx, es1, w_in_d, D, 128, ochunks, h2, [(u0, un, 0) for (u0, un) in UT], evac, gcols=128, tag="fi",
                       cast_engs=('pool', 'dve'))
                S.barrier()
                S.flush()
            with ExitStack() as es2:
                xt = Ring([S.sbuf(es2, "fxt%d" % i, [128, 512], F32) for i in range(3)])
                xo = Ring([S.sbuf(es2, "fxo%d" % i, [128, 512], F32) for i in range(3)])

                def evac2(oi, ti, ps, m, p0, n, j):
                    t0 = tcol(p0)
                    a = xt.next()
                    S.dma('sp', a[:, 0:n], x1_d[oi * 128:(oi + 1) * 128, t0:t0 + n], writes=[a])
                    o = xo.next()
                    S.stt('dve', o[:, 0:n], ps[:, 0:n], mod[:, 5, oi, j:j + 1], a[:, 0:n], ALU.mult, ALU.add,
                          reads=[ps, mod, a], writes=[o])
                    S.dma('pool', x2_d[oi * 128:(oi + 1) * 128, t0:t0 + n], o[:, 0:n], reads=[o], writes=[x2buf])
                linear(cx, es2, w_out_d, DFF, 128, [(o * 128, 128) for o in range(8)], G, OT, evac2, gcols=128, tag="fo")
                S.barrier()
                S.flush()
        if final:
            with ExitStack() as es3:
                fg = S.sbuf(es3, "fg", [128, 8], F32)
                S.dma('sp', fg[:], fg_d[:, :], writes=[fg])
                xt = Ring([S.sbuf(es3, "yxt%d" % i, [128, 8, 512], F32) for i in range(2)])
                sq = Ring([S.sbuf(es3, "ysq%d" % i, [128, 8, 512], F32) for i in range(2)])
                rs = Ring([S.sbuf(es3, "yrs%d" % i, [128, 512], F32) for i in range(2)])
                rdp = Ring([S.sbuf(es3, "yrd%d" % i, [128, 512], F32) for i in range(2)])
                xv = x2_d.rearrange("(c p) t -> p c t", p=128)
                yv = y_d.rearrange("(c p) t -> p c t", p=128)
                for (c0, n, j) in TILES[0:4]:
                    b = xt.next()
                    S.dma('sp', b[:, :, 0:n], xv[:, :, c0:c0 + n], reads=[x2buf], writes=[b])
                    q = sq.next()
                    S.tt('pool', q[:, :, 0:n], b[:, :, 0:n], b[:, :, 0:n], ALU.mult, reads=[b], writes=[q])
                    ps = cx.ps.next()
                    for c in range(8):
                        S.mm(ps[:, 0:n], cx.ones[:, :], q[:, c, 0:n], c == 0, c == 7, reads=[cx.ones, q], writes=[ps])
                    r = rs.next()
                    S.act(r[:, 0:n], ps[:, 0:n], AF.Sqrt, reads=[ps, cx.eps], writes=[r], bias=cx.eps[:, 0:1], scale=1.0 / D)
                    rd = rdp.next()
                    S.op('dve', lambda e, rd=rd, r=r, n=n: e.reciprocal(rd[:, 0:n], r[:, 0:n]), reads=[r], writes=[rd])
                    S.tt('dve', q[:, :, 0:n], b[:, :, 0:n], rd[:, 0:n].unsqueeze(1).broadcast_to([128, 8, n]), ALU.mult,
                         reads=[b, rd], writes=[q])
                    S.tt('pool', b[:, :, 0:n], q[:, :, 0:n], fg[:, :].unsqueeze(2).broadcast_to([128, 8, n]), ALU.mult,
                         reads=[q, fg], writes=[b])
                    S.dma('pool', yv[:, :, c0:c0 + n], b[:, :, 0:n], reads=[b])
                S.barrier()
                S.flush()
        cx.finish()
    return nc


def host_h2p(h2):
    out = []
    for r in range(NCORES):
        q = r % 4
        a = np.asarray(h2[r])
        z1 = np.zeros((D, 1), a.dtype)
        left = np.asarray(h2[r - 1])[:, TL - 1:TL] if q > 0 else z1
        right = np.asarray(h2[r + 1])[:, 0:1] if q < 3 else z1
        out.append(np.ascontiguousarray(np.concatenate([left, a[:, :TL], right, z1, a[:, TL:], z1], axis=1)))
    return out


def launch_ffn(h2, x1, mods, inp, i, final):
    nc = build_ffn(final)
    h2p = host_h2p(h2)
    w_in = np.ascontiguousarray(inp['ffn_w_in'][i])
    w_out = np.ascontiguousarray(inp['ffn_w_out'][i])
    cw = np.ascontiguousarray(inp['ffn_conv_w'][i].T.reshape(48, 128, 3).transpose(1, 0, 2))
    cb = fm(inp['ffn_conv_b'][i])
    maps = []
    for r in range(NCORES):
        m = dict(h2p=h2p[r], x1=x1[r], mod=mods[i][r], w_in=w_in, cw=cw, cb=cb, w_out=w_out)
        if final:
            m['fgain'] = fm(inp['final_norm'])
        maps.append(m)
    res = run(nc, maps)
    if final:
        return [o['x2'] for o in res], [o['y'] for o in res]
    return [o['x2'] for o in res], None


def build_mla_pre():
    nc = new_nc()
    xT = din(nc, "xT", [D, T])
    mod_d = din(nc, "mod", [128, 6, 8, 2])
    gain = din(nc, "gain", [128, 8])
    w_in = din(nc, "w_in", [D, 1024 + 64])
    qg_d = din(nc, "qg", [128, 8])
    w_qb = din(nc, "w_qb", [768, 2048])
    w_kvb = din(nc, "w_kvb", [256, 2048])
    cs_d = din(nc, "cs", [128, T])
    sn_d = din(nc, "sn", [128, T])
    q_o = dout(nc, "q", [2048, T], BF16)
    kv_o = dout(nc, "kv", [2048, T], BF16)
    kr_o = dout(nc, "kr", [64, T], BF16)
    with ExitStack() as es:
        cx = Ctx(nc, es)
        S = cx.S
        mod = S.sbuf(es, "mod", [128, 6, 8, 2], F32)
        S.dma('sp', mod[:], mod_d[:, :, :, :], writes=[mod])
        cs = S.sbuf(es, "cs_s", [128, T], F32)
        sn = S.sbuf(es, "sn_s", [128, T], F32)
        qg = S.sbuf(es, "qg_s", [128, 8], F32)
        S.dma('sp', cs[:], cs_d[:, :], writes=[cs])
        S.dma('sp', sn[:], sn_d[:, :], writes=[sn])
        S.dma('sp', qg[:], qg_d[:, :], writes=[qg])
        cqn = S.sbuf(es, "cqn", [128, 8, T], BF16)
        cq_d = nc.dram_tensor("cq_scr", [D, T], F32, kind="Internal").ap()
        cqbuf = Buf(None, "cq_dram")
        with ExitStack() as es_h:
            h = S.sbuf(es_h, "h", [128, 8, T], BF16)
            with ExitStack() as es2:
                A = emit_affine(cx, es2, mod, 0, 1, gain, "afa")
                xt = Ring([S.sbuf(es2, "xt%d" % i, [128, 8, 512], F32) for i in range(2)])
                xv = xT.rearrange("(c p) t -> p c t", p=128)

                def x_src(ti):
                    c0, n, j = TILES[ti]
                    b = xt.next()
                    S.dma('sp', b[:, :, 0:n], xv[:, :, c0:c0 + n], writes=[b])
                    return b, b[:, :, 0:n]
                emit_norm_mod(cx, es2, x_src, A, mod, 0, h)
                S.barrier()
                S.flush()
            with ExitStack() as es3:
                def mk(name, dt):
                    return Ring([S.sbuf(es3, "%s%d" % (name, i), [128, 512], dt) for i in range(2)])
                pools = (mk("sq", F32), mk("rs", F32), mk("rd", F32), mk("qn", F32), mk("t1", F32), mk("t2", F32))
                ob = mk("ob", BF16)
                cqs = Ring([S.sbuf(es3, "cqs%d" % i, [128, 512], F32) for i in range(3)])
                ochunks = [(o * 128, 128) for o in range(8)] + [(1024, 64)]

                def evac1(oi, ti, ps, m, t0, n, j):
                    if oi < 8:
                        o = cqs.next()
                        S.copy('act', o[:, 0:n], ps[:, 0:n], reads=[ps], writes=[o])
                        S.dma('pool', cq_d[oi * 128:(oi + 1) * 128, t0:t0 + n], o[:, 0:n], reads=[o], writes=[cqbuf])
                    else:
                        o = ob.next()
                        emit_headnorm_rope(cx, pools, ps, 64, n, None, None, cs, sn, t0, o[0:64, 0:n], o, 64, (0, 32), 16, 0.0)
                        S.dma('pool', kr_o[:, t0:t0 + n], o[0:64, 0:n], reads=[o])
                linear(cx, es3, w_in, D, 128, ochunks, h, TILES, evac1, gcols=256, tag="l1")
                S.barrier()
                S.flush()
        with ExitStack() as es4:
            cqt = Ring([S.sbuf(es4, "cqt%d" % i, [128, 8, 512], F32) for i in range(2)])
            sq = Ring([S.sbuf(es4, "lsq%d" % i, [128, 8, 512], F32) for i in range(2)])
            rs = Ring([S.sbuf(es4, "lrs%d" % i, [128, 512], F32) for i in range(2)])
            rdp = Ring([S.sbuf(es4, "lrd%d" % i, [128, 512], F32) for i in range(4)])
            cqv = cq_d.rearrange("(c p) t -> p c t", p=128)
            for (c0, n, j) in TILES:
                cq = cqt.next()
                S.dma('sp', cq[:, :, 0:n], cqv[:, :, c0:c0 + n], reads=[cqbuf], writes=[cq])
                q = sq.next()
                S.tt('pool', q[:, :, 0:n], cq[:, :, 0:n], cq[:, :, 0:n], ALU.mult, reads=[cq], writes=[q])
                for (ca, cb_, inv) in ((0, 6, 1.0 / 768), (6, 8, 1.0 / 256)):
                    ps = cx.ps.next()
                    for c in range(ca, cb_):
                        S.mm(ps[:, 0:n], cx.ones[:, :], q[:, c, 0:n], c == ca, c == cb_ - 1, reads=[cx.ones, q], writes=[ps])
                    r = rs.next()
                    S.act(r[:, 0:n], ps[:, 0:n], AF.Sqrt, reads=[ps, cx.eps], writes=[r], bias=cx.eps[:, 0:1], scale=inv)
                    rd = rdp.next()
                    S.op('dve', lambda e, rd=rd, r=r, n=n: e.reciprocal(rd[:, 0:n], r[:, 0:n]), reads=[r], writes=[rd])
                    for c in range(ca, cb_):
                        S.stt('dve', cqn[:, c, c0:c0 + n], cq[:, c, 0:n], qg[:, c:c + 1], rd[:, 0:n], ALU.mult, ALU.mult,
                              reads=[cq, qg, rd], writes=[cqn])
            S.barrier()
            S.flush()
        with ExitStack() as es5:
            def mk(name, dt):
                return Ring([S.sbuf(es5, "%s%d" % (name, i), [128, 512], dt) for i in range(2)])
            pools = (mk("sq", F32), mk("rs", F32), mk("rd", F32), mk("qn", F32), mk("t1", F32), mk("t2", F32))
            ob = mk("ob", BF16)

            def evac2(oi, ti, ps, m, t0, n, j):
                o = ob.next()
                emit_headnorm_rope(cx, pools, ps, 128, n, None, None, cs, sn, t0, o[:, 0:n], o, 128, (0, 32), 16, 0.0)
                S.dma('pool', q_o[oi * 128:(oi + 1) * 128, t0:t0 + n], o[:, 0:n], reads=[o])
            linear(cx, es5, w_qb, 768, 128, [(o * 128, 128) for o in range(16)], cqn, TILES, evac2, tag="l2")
            kvsrc = Buf(cqn.t[:, 6:8, :], "kvsrc")
            kvsrc.w, kvsrc.r = cqn.w, cqn.r

            def evac3(oi, ti, ps, m, t0, n, j):
                o = ob.next()
                S.copy('act', o[:, 0:n], ps[:, 0:n], reads=[ps], writes=[o])
                S.dma('pool', kv_o[oi * 128:(oi + 1) * 128, t0:t0 + n], o[:, 0:n], reads=[o])
            linear(cx, es5, w_kvb, 256, 128, [(o * 128, 128) for o in range(16)], kvsrc, TILES, evac3, tag="l3")
            cx.finish()
    return nc


def host_mla_weights(inp, j):
    w_in = inp['mla_w_in'][j]
    w_in2 = np.zeros((D, 1024 + 64), np.float32)
    w_in2[:, :1024] = w_in[:, :1024]
    w_in2[:, 1024:1040] = w_in[:, 1024:1040]
    w_in2[:, 1056:1072] = w_in[:, 1040:1056]
    w_qb = inp['mla_w_qb'][j].reshape(768, 16, 96)
    w_qb2 = np.zeros((768, 16, 128), np.float32)
    w_qb2[:, :, 0:16] = w_qb[:, :, 64:80]
    w_qb2[:, :, 32:48] = w_qb[:, :, 80:96]
    w_qb2[:, :, 64:128] = w_qb[:, :, 0:64]
    return w_in2, np.ascontiguousarray(w_qb2.reshape(768, 2048)), np.ascontiguousarray(inp['mla_w_kvb'][j])


def launch_mla_pre(xT, mods, inp, i, j):
    nc = build_mla_pre()
    tabs = host_rope_fm(32, 0, 32, 128)
    gain = fm(inp['norm_mix'][i])
    w_in2, w_qb2, w_kvb = host_mla_weights(inp, j)
    qg = np.ascontiguousarray(np.concatenate([fm(inp['mla_q_norm'][j]), fm(inp['mla_kv_norm'][j])], axis=1))
    maps = [dict(xT=xT[r], mod=mods[i][r], gain=gain, w_in=w_in2, qg=qg, w_qb=w_qb2, w_kvb=w_kvb,
                 cs=tabs[r][0], sn=tabs[r][1]) for r in range(NCORES)]
    res = run(nc, maps)
    return [o['q'] for o in res], [o['kv'] for o in res], [o['kr'] for o in res]


def host_kv_mla(kv, kr):
    kTs, vs = [], []
    for b in range(NB):
        kvb = np.concatenate([np.asarray(kv[4 * b + q])[:, :TL] for q in range(4)] + [np.asarray(kv[4 * b])[:, TL:]], axis=1)
        krb = np.concatenate([np.asarray(kr[4 * b + q])[:, :TL] for q in range(4)] + [np.asarray(kr[4 * b])[:, TL:]], axis=1)
        kvh = kvb.reshape(16, 128, NK)
        kT = np.zeros((16, 128, NK), kvb.dtype)
        kT[:, 0:64, :] = krb[None, :, :]
        kT[:, 64:128, :] = kvh[:, 0:64, :]
        kTs.append(np.ascontiguousarray(kT.reshape(16 * 128, NK)))
        v = kvh[:, 64:128, :]
        v4 = v.reshape(16, 64, NK // 128, 128)
        vs.append(np.ascontiguousarray(v4.transpose(0, 3, 2, 1)).reshape(16, 128, (NK // 128) * 64))
    return kTs, vs


def launch_mla_attn(q, kv, kr, xT, mods, inp, i, j):
    nc = build_attn_post(16, 1, 128, 64, 96 ** -0.5)
    kTs, vs = host_kv_mla(kv, kr)
    gain = fm(inp['norm_ffn'][i])
    w_out = np.ascontiguousarray(inp['mla_w_out'][j])
    maps = [dict(q=q[r], kT=kTs[r // 4], v=vs[r // 4], xT=xT[r], mod=mods[i][r],
                 gain=gain, w_out=w_out) for r in range(NCORES)]
    res = run(nc, maps)
    return [o['x1'] for o in res], [o['h2'] for o in res]


NS = CTX + SEQ
HC = 64
SUP = [(0, 256)] + [(256 + 512 * i, 512) for i in range(16)]


def build_hgrn_pre():
    nc = new_nc()
    xT = din(nc, "xT", [D, T])
    mod_d = din(nc, "mod", [128, 6, 8, 2])
    gain = din(nc, "gain", [128, 8])
    w_in = din(nc, "w_in", [D, 5120])
    p_o = dout(nc, "p", [5120, T])
    with ExitStack() as es:
        cx = Ctx(nc, es)
        S = cx.S
        mod = S.sbuf(es, "mod", [128, 6, 8, 2], F32)
        S.dma('sp', mod[:], mod_d[:, :, :, :], writes=[mod])
        h = S.sbuf(es, "h", [128, 8, T], BF16)
        with ExitStack() as es2:
            A = emit_affine(cx, es2, mod, 0, 1, gain, "afa")
            xt = Ring([S.sbuf(es2, "xt%d" % i, [128, 8, 512], F32) for i in range(2)])
            xv = xT.rearrange("(c p) t -> p c t", p=128)

            def x_src(ti):
                c0, n, j = TILES[ti]
                b = xt.next()
                S.dma('sp', b[:, :, 0:n], xv[:, :, c0:c0 + n], writes=[b])
                return b, b[:, :, 0:n]
            emit_norm_mod(cx, es2, x_src, A, mod, 0, h)
            S.barrier()
            S.flush()
        with ExitStack() as es3:
            ob = Ring([S.sbuf(es3, "ob%d" % i, [128, 512], F32) for i in range(3)])

            def evac(oi, ti, ps, m, t0, n, j):
                o = ob.next()
                if oi % 2 == 0:
                    S.copy('act', o[:, 0:n], ps[:, 0:n], reads=[ps], writes=[o])
                else:
                    S.copy('dve', o[:, 0:n], ps[:, 0:n], reads=[ps], writes=[o])
                S.dma('pool', p_o[oi * 128:(oi + 1) * 128, t0:t0 + n], o[:, 0:n], reads=[o])
            linear(cx, es3, w_in, D, 128, [(o * 128, 128) for o in range(40)], h, TILES, evac, tag="hl", cast_engs=('pool',))
            cx.finish()
    return nc


def build_hgrn_scan():
    nc = new_nc()
    q_d = din(nc, "q", [4, 128, NS])
    f_d = din(nc, "f", [4, 128, NS])
    v_d = din(nc, "v", [4, HC, (NS // HC) * 128])
    lb_d = din(nc, "lb", [128, 4])
    rm_d = din(nc, "rmask", [128, 512])
    ut_d = din(nc, "ut", [HC, HC])
    id_d = din(nc, "ident", [128, 128])
    o_d = dout(nc, "o", [4, 128, NS])
    scale = 128 ** -0.5
    with ExitStack() as es:
        cx = Ctx(nc, es)
        S = cx.S
        lb = S.sbuf(es, "lb", [128, 4], F32)
        oml = S.sbuf(es, "oml", [128, 4], F32)
        rm = S.sbuf(es, "rm", [128, 512], F32)
        ut = S.sbuf(es, "ut", [HC, HC], F32)
        idf = S.sbuf(es, "idf", [128, 128], F32)
        idb = S.sbuf(es, "idb", [128, 128], BF16)
        S.dma('sp', lb[:], lb_d[:, :], writes=[lb])
        S.dma('sp', rm[:], rm_d[:, :], writes=[rm])
        S.dma('sp', ut[:], ut_d[:, :], writes=[ut])
        S.dma('sp', idf[:], id_d[:, :], writes=[idf])
        S.copy('dve', idb[:], idf[:], reads=[idf], writes=[idb])
        S.ts('dve', oml[:], lb[:], -1.0, 1.0, ALU.mult, ALU.add, reads=[lb], writes=[oml])

        def mk(name, shape, dt, k=2):
            return Ring([S.sbuf(es, "%s%d" % (name, i), shape, dt) for i in range(k)])
        qin = mk("qin", [128, 512], F32)
        fin = mk("fin", [128, 512], F32)
        vin = mk("vin", [HC, 8, 128], F32)
        vbf = mk("vbf", [HC, 8, 128], BF16)
        sg = mk("sg", [128, 512], F32)
        ff = mk("ff", [128, 512], F32)
        lg = mk("lg", [128, 512], F32)
        k0 = mk("k0", [128, 512], F32)
        cum = mk("cum", [128, 512], F32)
        e1 = mk("e1", [128, 512], F32)
        e2 = mk("e2", [128, 512], F32)
        kkf = mk("kkf", [128, 512], F32)
        dcy = mk("dcy", [128, 8], F32)
        qq = mk("qq", [128, 512], BF16)
        kk = mk("kk", [128, 512], BF16)
        kk2 = mk("kk2", [128, 512], BF16)
        am = mk("am", [HC, HC], BF16, 3)
        k2t = mk("k2t", [HC, 128], BF16, 3)
        ost = mk("ost", [128, 512], F32)
        Sf = mk("Sf", [128, 128], F32, 2)
        Sb = mk("Sb", [128, 128], BF16, 2)
        for sc in range(4):
            s_cur = Sf.next()
            S.memset('dve', s_cur[:], 0.0, writes=[s_cur])
            sb_cur = Sb.next()
            S.memset('pool', sb_cur[:], 0.0, writes=[sb_cur])
            for (s0, n) in SUP:
                ncnk = n // HC
                a_q = qin.next()
                S.dma('sp', a_q[:, 0:n], q_d[sc, :, s0:s0 + n], writes=[a_q])
                a_f = fin.next()
                S.dma('sp', a_f[:, 0:n], f_d[sc, :, s0:s0 + n], writes=[a_f])
                a_v = vin.next()
                c0 = s0 // HC
                S.dma('sp', a_v[:, 0:ncnk, :], v_d[sc, :, c0 * 128:(c0 + ncnk) * 128].rearrange("p (c d) -> p c d", d=128),
                      writes=[a_v])
                b_v = vbf.next()
                S.copy('pool', b_v[:, 0:ncnk, :], a_v[:, 0:ncnk, :], reads=[a_v], writes=[b_v])
                t_sg = sg.next()
                S.act(t_sg[:, 0:n], a_f[:, 0:n], AF.Sigmoid, reads=[a_f], writes=[t_sg])
                t_f = ff.next()
                S.ts('dve', t_f[:, 0:n], t_sg[:, 0:n], oml[:, sc:sc + 1], lb[:, sc:sc + 1], ALU.mult, ALU.add,
                     reads=[t_sg, oml, lb], writes=[t_f])
                t_lg = lg.next()
                S.act(t_lg[:, 0:n], t_f[:, 0:n], AF.Ln, reads=[t_f], writes=[t_lg])
                t_k0 = k0.next()
                S.ts('pool', t_k0[:, 0:n], t_f[:, 0:n], -1.0, 1.0, ALU.mult, ALU.add, reads=[t_f], writes=[t_k0])
                t_cum = cum.next()
                S.op('dve', lambda e, t_cum=t_cum, t_lg=t_lg, n=n: e.tensor_tensor_scan(
                    t_cum[:, 0:n], rm[:, 0:n], t_lg[:, 0:n], 0.0, ALU.mult, ALU.add), reads=[rm, t_lg], writes=[t_cum])
                t_e1 = e1.next()
                S.act(t_e1[:, 0:n], t_cum[:, 0:n], AF.Exp, reads=[t_cum], writes=[t_e1])
                t_e2 = e2.next()
                S.act(t_e2[:, 0:n], t_cum[:, 0:n], AF.Exp, reads=[t_cum], writes=[t_e2], scale=-1.0)
                t_d = dcy.next()
                S.copy('dve', t_d[:, 0:ncnk], t_e1[:, HC - 1:n:HC], reads=[t_e1], writes=[t_d])
                t_qq = qq.next()
                S.stt('dve', t_qq[:, 0:n], a_q[:, 0:n], scale, t_e1[:, 0:n], ALU.mult, ALU.mult, reads=[a_q, t_e1], writes=[t_qq])
                t_kkf = kkf.next()
                S.tt('pool', t_kkf[:, 0:n], t_k0[:, 0:n], t_e2[:, 0:n], ALU.mult, reads=[t_k0, t_e2], writes=[t_kkf])
                t_kk = kk.next()
                S.copy('pool', t_kk[:, 0:n], t_kkf[:, 0:n], reads=[t_kkf], writes=[t_kk])
                t_kk2 = kk2.next()
                S.tt('dve', t_kk2[:, 0:n].rearrange("p (c s) -> p c s", s=HC), t_kkf[:, 0:n].rearrange("p (c s) -> p c s", s=HC),
                     t_d[:, 0:ncnk].unsqueeze(2).broadcast_to([128, ncnk, HC]), ALU.mult, reads=[t_kkf, t_d], writes=[t_kk2])
                t_o = ost.next()
                for c in range(ncnk):
                    cs_ = slice(c * HC, (c + 1) * HC)
                    psA = cx.ps.next()
                    S.mm(psA[0:HC, 0:HC], t_kk[:, cs_], t_qq[:, cs_], True, True, reads=[t_kk, t_qq], writes=[psA])
                    t_am = am.next()
                    S.tt('dve', t_am[:, :], psA[0:HC, 0:HC], ut[:, :], ALU.mult, reads=[psA, ut], writes=[t_am])
                    psT = cx.ps.next()
                    S.mm(psT[0:HC, 0:128], t_kk2[:, cs_], idb[:, :], True, True, reads=[t_kk2, idb], writes=[psT])
                    t_k2t = k2t.next()
                    S.copy('act', t_k2t[:, :], psT[0:HC, 0:128], reads=[psT], writes=[t_k2t])
                    psO = cx.ps.next()
                    S.mm(psO[:, 0:HC], b_v[:, c, :], t_am[:, :], True, False, reads=[b_v, t_am], writes=[psO], inc=False)
                    S.mm(psO[:, 0:HC], sb_cur[:, :], t_qq[:, cs_], False, True, reads=[sb_cur, t_qq], writes=[psO], inc=True)
                    S.copy('act', t_o[:, cs_], psO[:, 0:HC], reads=[psO], writes=[t_o])
                    psS = cx.ps.next()
                    S.mm(psS[:, 0:128], t_k2t[:, :], b_v[:, c, :], True, True, reads=[t_k2t, b_v], writes=[psS])
                    s_new = Sf.next()
                    S.stt('dve', s_new[:, :], s_cur[:, :], t_d[:, c:c + 1], psS[:, 0:128], ALU.mult, ALU.add,
                          reads=[s_cur, t_d, psS], writes=[s_new])
                    s_cur = s_new
                    sb_cur = Sb.next()
                    S.copy('pool', sb_cur[:, :], s_cur[:, :], reads=[s_cur], writes=[sb_cur])
                S.dma('pool', o_d[sc, :, s0:s0 + n], t_o[:, 0:n], reads=[t_o])
        cx.finish()
    return nc


def build_hgrn_post():
    nc = new_nc()
    of_d = din(nc, "of", [D, T])
    ob_d = din(nc, "ob", [D, T])
    g_d = din(nc, "gate", [D, T])
    xT_d = din(nc, "xT", [D, T])
    mod_d = din(nc, "mod", [128, 6, 8, 2])
    gain_d = din(nc, "gain", [128, 8])
    og_d = din(nc, "ogain", [128, 1])
    w_out_d = din(nc, "w_out", [D, D])
    x1_d = dout(nc, "x1", [D, T])
    h2_d = dout(nc, "h2", [D, T], BF16)
    with ExitStack() as es:
        cx = Ctx(nc, es)
        S = cx.S
        mod = S.sbuf(es, "mod", [128, 6, 8, 2], F32)
        S.dma('sp', mod[:], mod_d[:, :, :, :], writes=[mod])
        og = S.sbuf(es, "og", [128, 1], F32)
        S.dma('sp', og[:], og_d[:, :], writes=[og])
        with ExitStack() as es2:
            Z = S.sbuf(es2, "Z", [128, 8, T], BF16)
            with ExitStack() as es3:
                def mk(name, dt, k=2):
                    return Ring([S.sbuf(es3, "%s%d" % (name, i), [128, 512], dt) for i in range(k)])
                a_of, a_ob, a_g = mk("iof", F32), mk("iob", F32), mk("ig", F32)
                t_o, t_sq, t_rs, t_rd, t_zn, t_sg = mk("to", F32), mk("tsq", F32), mk("trs", F32), mk("trd", F32), mk("tzn", F32), mk("tsg", F32)
                for c in range(8):
                    for (c0, n, j) in TILES:
                        x_of = a_of.next()
                        S.dma('sp', x_of[:, 0:n], of_d[c * 128:(c + 1) * 128, c0:c0 + n], writes=[x_of])
                        x_ob = a_ob.next()
                        S.dma('sp', x_ob[:, 0:n], ob_d[c * 128:(c + 1) * 128, c0:c0 + n], writes=[x_ob])
                        x_g = a_g.next()
                        S.dma('sp', x_g[:, 0:n], g_d[c * 128:(c + 1) * 128, c0:c0 + n], writes=[x_g])
                        o = t_o.next()
                        S.tt('pool', o[:, 0:n], x_of[:, 0:n], x_ob[:, 0:n], ALU.add, reads=[x_of, x_ob], writes=[o])
                        sq = t_sq.next()
                        S.act(sq[:, 0:n], o[:, 0:n], AF.Square, reads=[o], writes=[sq])
                        ps = cx.ps.next()
                        S.mm(ps[:, 0:n], cx.ones[:, :], sq[:, 0:n], True, True, reads=[cx.ones, sq], writes=[ps])
                        r = t_rs.next()
                        S.act(r[:, 0:n], ps[:, 0:n], AF.Sqrt, reads=[ps, cx.eps], writes=[r], bias=cx.eps[:, 0:1], scale=1.0 / 128)
                        rd = t_rd.next()
                        S.op('dve', lambda e, rd=rd, r=r, n=n: e.reciprocal(rd[:, 0:n], r[:, 0:n]), reads=[r], writes=[rd])
                        zn = t_zn.next()
                        S.stt('dve', zn[:, 0:n], o[:, 0:n], og[:, 0:1], rd[:, 0:n], ALU.mult, ALU.mult, reads=[o, og, rd], writes=[zn])
                        sg = t_sg.next()
                        S.act(sg[:, 0:n], x_g[:, 0:n], AF.Silu, reads=[x_g], writes=[sg])
                        S.tt('pool', Z[:, c, c0:c0 + n], zn[:, 0:n], sg[:, 0:n], ALU.mult, reads=[zn, sg], writes=[Z])
                S.barrier()
                S.flush()
            emit_post(cx, es2, Z, 128, D, w_out_d, xT_d, x1_d, h2_d, mod, gain_d)
        cx.finish()
    return nc


def launch_hgrn(xT, mods, lbv, inp, i, j):
    nc = build_hgrn_pre()
    gain = fm(inp['norm_mix'][i])
    w_in = np.ascontiguousarray(inp['hgrn_w_in'][j])
    res = run(nc, [dict(xT=xT[r], mod=mods[i][r], gain=gain, w_in=w_in) for r in range(NCORES)])
    P = [np.asarray(o['p']) for o in res]
    Pfull = [np.concatenate([P[4 * b][:, TL:]] + [P[4 * b + q][:, :TL] for q in range(4)], axis=1) for b in range(NB)]
    idx_b = np.concatenate([np.arange(CTX - 1, -1, -1), CTX + np.arange(SEQ - 1, -1, -1)])
    rmask = np.ones((128, 512), np.float32)
    rmask[:, ::HC] = 0.0
    ut = np.triu(np.ones((HC, HC), np.float32))
    ident = np.eye(128, dtype=np.float32)
    maps = []
    for r in range(NCORES):
        qs, fs, vs, lbs = [], [], [], []
        for pp in range(2):
            p = 2 * r + pp
            b, hd = p // 8, p % 8
            rows = lambda k: slice(k * 1024 + hd * 128, k * 1024 + (hd + 1) * 128)
            qf = Pfull[b][rows(0)]
            vf = Pfull[b][rows(1)].T
            for d in range(2):
                fpre = Pfull[b][rows(2 + d)]
                if d == 1:
                    qd, fd, vd = qf[:, idx_b], fpre[:, idx_b], vf[idx_b]
                else:
                    qd, fd, vd = qf, fpre, vf
                qs.append(qd)
                fs.append(fd)
                vs.append(vd.reshape(NS // HC, HC, 128).transpose(1, 0, 2).reshape(HC, (NS // HC) * 128))
                lbs.append(lbv[:, hd])
        maps.append(dict(q=np.ascontiguousarray(np.stack(qs)), f=np.ascontiguousarray(np.stack(fs)),
                         v=np.ascontiguousarray(np.stack(vs)), lb=np.ascontiguousarray(np.stack(lbs, axis=1)),
                         rmask=rmask, ut=ut, ident=ident))
    nc = build_hgrn_scan()
    res = run(nc, maps)
    O = [np.asarray(o['o']) for o in res]
    of_full = [np.zeros((D, NS), np.float32) for _ in range(NB)]
    ob_full = [np.zeros((D, NS), np.float32) for _ in range(NB)]
    for r in range(NCORES):
        for pp in range(2):
            p = 2 * r + pp
            b, hd = p // 8, p % 8
            of_full[b][hd * 128:(hd + 1) * 128] = O[r][2 * pp]
            ob_full[b][hd * 128:(hd + 1) * 128] = O[r][2 * pp + 1][:, idx_b]
    nc = build_hgrn_post()
    gain2 = fm(inp['norm_ffn'][i])
    og = np.ascontiguousarray(inp['hgrn_out_norm'][j].reshape(128, 1))
    w_out = np.ascontiguousarray(inp['hgrn_w_out'][j])
    maps = []
    for r in range(NCORES):
        b, t0, t1 = core_tok(r)
        sel = lambda a: np.ascontiguousarray(np.concatenate([a[:, CTX + t0:CTX + t1], a[:, :CTX]], axis=1))
        maps.append(dict(of=sel(of_full[b]), ob=sel(ob_full[b]), gate=np.ascontiguousarray(P[r][4096:5120]), xT=xT[r],
                         mod=mods[i][r], gain=gain2, ogain=og, w_out=w_out))
    res = run(nc, maps)
    return [o['x1'] for o in res], [o['h2'] for o in res]


def kernel(**inputs):
    inp = {k: np.asarray(v) for k, v in inputs.items()}
    mods, lbv = launch_mod(inp)
    xT = host_xT(inp['x'], inp['ctx'])
    y = None
    for i in range(DEPTH):
        kind, j = i % 3, i // 3
        if kind == 0:
            qkv = launch_gqa_pre(xT, mods, inp, i, j)
            x1, h2 = launch_gqa_attn(qkv, xT, mods, inp, i, j)
        elif kind == 1:
            x1, h2 = launch_hgrn(xT, mods, lbv, inp, i, j)
        else:
            q, kv, kr = launch_mla_pre(xT, mods, inp, i, j)
            x1, h2 = launch_mla_attn(q, kv, kr, xT, mods, inp, i, j)
        xT, y = launch_ffn(h2, x1, mods, inp, i, i == DEPTH - 1)
    out = np.empty((NB, SEQ, D), np.float32)
    for r in range(NCORES):
        b, t0, t1 = core_tok(r)
        out[b, t0:t1, :] = np.asarray(y[r]).T
    return out
```

```python
import numpy as np
from contextlib import ExitStack
import concourse.bass as bass
import concourse.mybir as mybir
from concourse.bass_utils import run_bass_kernel_spmd

F32 = mybir.dt.float32
BF16 = mybir.dt.bfloat16
AF = mybir.ActivationFunctionType
ALU = mybir.AluOpType
AX = mybir.AxisListType

ENGS = ('pe', 'act', 'dve', 'pool', 'sp')
NDSEM = 24


class Buf:
    __slots__ = ('t', 'name', 'w', 'r', 'pr')

    def __init__(self, t, name=''):
        self.t = t
        self.name = name
        self.w = {}
        self.r = {}
        self.pr = {}

    def __getitem__(self, k):
        return self.t[k]


class Sched:
    def __init__(self, nc, es):
        self.nc = nc
        self.es = es
        self.sem = {}
        self.cnt = {}
        for e in ENGS:
            self.sem[e] = es.enter_context(nc.semaphore('s_' + e))
            self.cnt[e] = 0
        for i in range(NDSEM):
            k = ('d', i)
            self.sem[k] = es.enter_context(nc.semaphore('sd%d' % i))
            self.cnt[k] = 0
        self.rr = 0
        self.rr2 = 0
        self.q = {e: [] for e in ENGS}
        self.seen = {e: {} for e in ENGS}
        self.nops = 0

    def sbuf(self, es, name, shape, dtype):
        self.uid = getattr(self, 'uid', 0) + 1
        t = es.enter_context(self.nc.sbuf_tensor("sb%d_%s" % (self.uid, name), list(shape), dtype))
        return Buf(t, name)

    def psum(self, es, name, shape=(128, 512), dtype=F32):
        t = es.enter_context(self.nc.psum_tensor("pp_" + name, list(shape), dtype))
        return Buf(t, name)

    def _waits(self, eng, reads, writes, skip_waw=False):
        deps = {}
        for b in reads:
            if b.w:
                for k, v in b.w.items():
                    if deps.get(k, 0) < v:
                        deps[k] = v
        for b in writes:
            if b.w and not skip_waw:
                for k, v in b.w.items():
                    if deps.get(k, 0) < v:
                        deps[k] = v
            if skip_waw:
                for k, v in b.pr.items():
                    if deps.get(k, 0) < v:
                        deps[k] = v
            for k, v in b.r.items():
                if deps.get(k, 0) < v:
                    deps[k] = v
        waits = []
        seen = self.seen[eng]
        for k, v in deps.items():
            if k == 'pe' and eng == 'pe':
                continue
            if seen.get(k, 0) < v:
                seen[k] = v
                waits.append((k, v))
        return waits

    def op(self, eng, fn, reads=(), writes=(), inc=True):
        waits = self._waits(eng, reads, writes)
        n = self.cnt[eng] + 1
        if inc:
            self.cnt[eng] = n
        self.q[eng].append((waits, fn, (eng, 1) if inc else None))
        for b in reads:
            b.r[eng] = n
        for b in writes:
            b.w = {eng: n}
            b.r = {}
        self.nops += 1

    def dma(self, eng, out, in_, reads=(), writes=(), join=False):
        waits = self._waits(eng, reads, writes, skip_waw=join)
        if eng == 'sp':
            k = ('d', self.rr % 16)
            self.rr += 1
        else:
            k = ('d', 16 + self.rr2 % (NDSEM - 16))
            self.rr2 += 1
        if self.cnt[k] > 0 and self.seen[eng].get(k, 0) < self.cnt[k]:
            self.seen[eng][k] = self.cnt[k]
            waits.append((k, self.cnt[k]))
        self.cnt[k] += 16
        n = self.cnt[k]
        self.q[eng].append((waits, lambda e: e.dma_start(out=out, in_=in_), (k, 16)))
        for b in reads:
            b.r[k] = n
        for b in writes:
            if join and b.w:
                b.w[k] = n
            else:
                pr = dict(b.w)
                for kk_, vv_ in b.r.items():
                    if pr.get(kk_, 0) < vv_:
                        pr[kk_] = vv_
                b.pr = pr
                b.w = {k: n}
            b.r = {}
        self.nops += 1

    def barrier(self):
        for e in ENGS:
            waits = []
            for k, v in self.cnt.items():
                if v > 0 and self.seen[e].get(k, 0) < v and k != e:
                    self.seen[e][k] = v
                    waits.append((k, v))
            if waits:
                self.q[e].append((waits, None, None))

    def flush(self):
        nc = self.nc
        sem = self.sem
        with nc.Block() as block:
            for e, deco in (('pe', block.tensor), ('act', block.scalar), ('dve', block.vector),
                            ('pool', block.gpsimd), ('sp', block.sync)):
                items = self.q[e]

                def body(eng, items=items):
                    for waits, fn, inc in items:
                        for k, v in waits:
                            eng.wait_ge(sem[k], v)
                        if fn is not None:
                            ins = fn(eng)
                            if inc is not None:
                                ins.then_inc(sem[inc[0]], inc[1])
                deco(body)
        self.q = {e: [] for e in ENGS}

    def mm(self, out, lhsT, rhs, start, stop, reads=(), writes=(), inc=None):
        if inc is None:
            inc = stop
        self.op('pe', lambda e: e.matmul(out, lhsT, rhs, start=start, stop=stop), reads, writes, inc=inc)

    def act(self, out, in_, func, reads=(), writes=(), bias=None, scale=None, eng='act'):
        kw = {}
        if bias is not None:
            kw['bias'] = bias
        if scale is not None:
            kw['scale'] = scale
        self.op('act', lambda e: e.activation(out, in_, func, **kw), reads, writes)

    def ts(self, eng, out, in0, s1, s2, op0, op1=None, reads=(), writes=()):
        if op1 is None:
            self.op(eng, lambda e: e.tensor_scalar(out, in0, s1, None, op0), reads, writes)
        else:
            self.op(eng, lambda e: e.tensor_scalar(out, in0, s1, s2, op0, op1), reads, writes)

    def tt(self, eng, out, in0, in1, op, reads=(), writes=()):
        self.op(eng, lambda e: e.tensor_tensor(out, in0, in1, op), reads, writes)

    def stt(self, eng, out, in0, scalar, in1, op0, op1, reads=(), writes=()):
        self.op(eng, lambda e: e.scalar_tensor_tensor(out, in0, scalar, in1, op0, op1), reads, writes)

    def copy(self, eng, out, in_, reads=(), writes=()):
        if eng == 'act':
            self.op('act', lambda e: e.activation(out, in_, AF.Copy), reads, writes)
        else:
            self.op(eng, lambda e: e.tensor_copy(out, in_), reads, writes)

    def memset(self, eng, out, val, writes=()):
        self.op(eng, lambda e: e.memset(out, val), (), writes)


class Ring:
    def __init__(self, bufs):
        self.bufs = bufs
        self.i = 0

    def next(self):
        b = self.bufs[self.i % len(self.bufs)]
        self.i += 1
        return b


D = 1024
SEQ = 8192
CTX = 256
NB = 2
DEPTH = 4
TL = 2048
T = TL + CTX
NK = SEQ + CTX
EPS = 1e-6
DFF = 3072
TILES = [(0, 512, 0), (512, 512, 0), (1024, 512, 0), (1536, 512, 0), (2048, 256, 1)]


def new_nc():
    return bass.Bass("TRN2", target_bir_lowering=False)


def din(nc, name, shape, dtype=F32):
    return nc.dram_tensor(name, list(shape), dtype, kind="ExternalInput").ap()


def dout(nc, name, shape, dtype=F32):
    return nc.dram_tensor(name, list(shape), dtype, kind="ExternalOutput").ap()


class Ctx:
    def __init__(self, nc, es, npsum=8):
        self.nc = nc
        self.es = es
        self.S = Sched(nc, es)
        S = self.S
        self.ps = Ring([S.psum(es, "ps%d" % i) for i in range(npsum)])
        self.ones = S.sbuf(es, "ones_f", [128, 128], F32)
        S.memset('dve', self.ones[:], 1.0, writes=[self.ones])
        self.ones_bf = S.sbuf(es, "ones_b", [128, 128], BF16)
        S.memset('pool', self.ones_bf[:], 1.0, writes=[self.ones_bf])
        self.eps = S.sbuf(es, "eps_t", [128, 1], F32)
        S.memset('dve', self.eps[:], EPS, writes=[self.eps])
        self.cast_i = 0

    def finish(self):
        self.S.barrier()
        self.S.flush()


def emit_mod(cx, es_outer, cvec_d, w_d, b_d, ns, ncol):
    S = cx.S
    mod = S.sbuf(es_outer, "mod", [128, ns, 8, ncol], F32)
    with ExitStack() as es:
        craw = S.sbuf(es, "craw", [128, 8, ncol], F32)
        sc = S.sbuf(es, "silu_c", [128, 8, ncol], F32)
        bada = S.sbuf(es, "bada", [128, ns * 8], F32)
        wb = Ring([S.sbuf(es, "wada%d" % i, [128, 8, 1024], F32) for i in range(2)])
        S.dma('sp', craw[:], cvec_d[:, :, :], writes=[craw])
        S.dma('sp', bada[:], b_d[:, :], writes=[bada])
        S.act(sc[:], craw[:], AF.Silu, reads=[craw], writes=[sc])
        for s in range(ns):
            w = wb.next()
            src = w_d[:, s * 1024:(s + 1) * 1024].rearrange("(c p) o -> p c o", p=128)
            S.dma('sp', w[:, 0:4, :], src[:, 0:4, :], writes=[w])
            S.dma('pool', w[:, 4:8, :], src[:, 4:8, :], writes=[w], join=True)
            ps = cx.ps.next()
            for o in range(8):
                for kc in range(8):
                    last = (o == 7 and kc == 7)
                    S.mm(ps[:, o * ncol:(o + 1) * ncol], w[:, kc, o * 128:(o + 1) * 128], sc[:, kc, :], kc == 0, kc == 7,
                         reads=[w, sc], writes=[ps], inc=last)
            S.tt('dve', mod[:, s, :, :], ps[:, 0:8 * ncol].rearrange("p (c j) -> p c j", j=ncol),
                 bada[:, s * 8:(s + 1) * 8].unsqueeze(2).broadcast_to([128, 8, ncol]), ALU.add,
                 reads=[ps, bada], writes=[mod])
        S.barrier()
        S.flush()
    return mod


def build_mod():
    nc = new_nc()
    cvec = din(nc, "cvec", [128, 8, 3])
    w_d = din(nc, "w", [D, 3 * D])
    b_d = din(nc, "b", [128, 24])
    lbin = din(nc, "lbin", [128, 8, 4])
    mod_o = dout(nc, "mod", [128, 3, 8, 3])
    lb_o = dout(nc, "lb", [128, 8])
    with ExitStack() as es:
        cx = Ctx(nc, es)
        S = cx.S
        mod = emit_mod(cx, es, cvec, w_d, b_d, 3, 3)
        S.dma('pool', mod_o[:, :, :, :], mod[:], reads=[mod])
        l0 = S.sbuf(es, "l0", [128, 8, 4], F32)
        l1 = S.sbuf(es, "l1", [128, 8, 4], F32)
        l2 = S.sbuf(es, "l2", [128, 8], F32)
        l3 = S.sbuf(es, "l3", [128, 8], F32)
        l4 = S.sbuf(es, "l4", [128, 8], F32)
        S.dma('sp', l0[:], lbin[:, :, :], writes=[l0])
        S.act(l1[:], l0[:], AF.Exp, reads=[l0], writes=[l1])
        S.op('dve', lambda e: e.tensor_reduce(l2[:], l1[:], AX.X, ALU.add), reads=[l1], writes=[l2])
        S.op('dve', lambda e: e.reciprocal(l3[:], l2[:]), reads=[l2], writes=[l3])
        S.tt('dve', l4[:], l1[:, :, 1], l3[:], ALU.mult, reads=[l1, l3], writes=[l4])
        S.dma('pool', lb_o[:, :], l4[:], reads=[l4])
        cx.finish()
    return nc


def launch_mod(inp):
    nc = build_mod()
    c, c_ctx = inp['c'], inp['c_ctx']
    cvec = np.ascontiguousarray(np.stack([fm(c[0]), fm(c[1]), fm(c_ctx)], axis=-1))
    lbin = np.ascontiguousarray(inp['hgrn_lower_bounds'].T.reshape(8, 128, 4).transpose(1, 0, 2))
    maps = []
    for r in range(NCORES):
        layer, hf = r // 2, r % 2
        w = np.ascontiguousarray(inp['w_ada'][layer][:, hf * 3 * D:(hf + 1) * 3 * D])
        b = fm(inp['b_ada'][layer][hf * 3 * D:(hf + 1) * 3 * D])
        maps.append(dict(cvec=cvec, w=w, b=b, lbin=lbin))
    res = run(nc, maps)
    mods = []
    for layer in range(DEPTH):
        full = np.concatenate([res[2 * layer]['mod'], res[2 * layer + 1]['mod']], axis=1)
        per_b = [np.ascontiguousarray(full[:, :, :, [b, 2]]) for b in range(NB)]
        mods.append([per_b[r // 4] for r in range(NCORES)])
    return mods, np.ascontiguousarray(res[0]['lb'])


def emit_affine(cx, es, mod, s_shift, s_scale, gain_d, name):
    S = cx.S
    gain = S.sbuf(es, name + "_g", [128, 8], F32)
    A = S.sbuf(es, name + "_A", [128, 8, 2], F32)
    S.dma('sp', gain[:], gain_d[:, :], writes=[gain])
    S.stt('dve', A[:], mod[:, s_scale, :, :], 1.0, gain[:, :].unsqueeze(2).broadcast_to([128, 8, 2]), ALU.add, ALU.mult,
          reads=[mod, gain], writes=[A])
    return A


def emit_norm_mod(cx, es, x_src, A, mod, s_shift, h, tiles=TILES, xkeep=None, tag="nm"):
    S = cx.S
    sq = Ring([S.sbuf(es, tag + "_sq%d" % i, [128, 8, 512], F32) for i in range(2)])
    rs = Ring([S.sbuf(es, tag + "_rs%d" % i, [128, 512], F32) for i in range(2)])
    rstd = Ring([S.sbuf(es, tag + "_rstd%d" % i, [128, 512], F32) for i in range(2)])
    for ti, (c0, n, j) in enumerate(tiles):
        xb, xap = x_src(ti)
        q = sq.next()
        S.tt('pool', q[:, :, 0:n], xap, xap, ALU.mult, reads=[xb], writes=[q])
        ps = cx.ps.next()
        for c in range(8):
            S.mm(ps[:, 0:n], cx.ones[:, :], q[:, c, 0:n], c == 0, c == 7, reads=[cx.ones, q], writes=[ps])
        r = rs.next()
        S.act(r[:, 0:n], ps[:, 0:n], AF.Sqrt, reads=[ps, cx.eps], writes=[r], bias=cx.eps[:, 0:1], scale=1.0 / D)
        rd = rstd.next()
        S.op('dve', lambda e, rd=rd, r=r, n=n: e.reciprocal(rd[:, 0:n], r[:, 0:n]), reads=[r], writes=[rd])
        S.tt('dve', q[:, :, 0:n], xap, rd[:, 0:n].unsqueeze(1).broadcast_to([128, 8, n]), ALU.mult,
             reads=[xb, rd], writes=[q])
        for c in range(8):
            S.act(h[:, c, c0:c0 + n], q[:, c, 0:n], AF.Identity, reads=[q, A, mod], writes=[h],
                  bias=mod[:, s_shift, c, j:j + 1], scale=A[:, c, j:j + 1])


def linear(cx, es, W_d, K, kp, ochunks, xb, tiles, evac, gcols=512, tag="lin", cast_engs=('dve', 'pool')):
    S = cx.S
    KC = K // kp
    stg = Ring([S.sbuf(es, tag + "_stg%d" % i, [kp, KC, gcols], F32) for i in range(2)])
    wbf = Ring([S.sbuf(es, tag + "_wbf%d" % i, [kp, KC, gcols], BF16) for i in range(2)])
    groups = []
    cur = []
    for oi, (c0, m) in enumerate(ochunks):
        if cur and (c0 + m - cur[0][1] > gcols or c0 != cur[-1][1] + cur[-1][2]):
            groups.append(cur)
            cur = []
        cur.append((oi, c0, m))
    if cur:
        groups.append(cur)
    Wv = W_d.rearrange("(c p) o -> p c o", p=kp)

    def load(g):
        ga = g[0][1]
        gb = g[-1][1] + g[-1][2]
        w = gb - ga
        st = stg.next()
        half = max(1, KC // 2)
        S.dma('sp', st[:, 0:half, 0:w], Wv[:, 0:half, ga:gb], writes=[st])
        if half < KC:
            S.dma('pool', st[:, half:KC, 0:w], Wv[:, half:KC, ga:gb], writes=[st], join=True)
        wb = wbf.next()
        ce = cast_engs[cx.cast_i % len(cast_engs)]
        cx.cast_i += 1
        S.copy(ce, wb[:, :, 0:w], st[:, :, 0:w], reads=[st], writes=[wb])
        return wb, ga

    nxt = load(groups[0])
    for gi, g in enumerate(groups):
        wb, ga = nxt
        if gi + 1 < len(groups):
            nxt = load(groups[gi + 1])
        for (oi, c0, m) in g:
            oc = c0 - ga
            for ti, (t0, n, j) in enumerate(tiles):
                ps = cx.ps.next()
                for kc in range(KC):
                    S.mm(ps[0:m, 0:n], wb[:, kc, oc:oc + m], xb[:, kc, t0:t0 + n], kc == 0, kc == KC - 1,
                         reads=[wb, xb], writes=[ps])
                evac(oi, ti, ps, m, t0, n, j)


def emit_headnorm_rope(cx, pools, ps, m, n, gain_ap, gain_buf, cs, sn, t0, out_ap, out_buf, hd, rope_lo, half, inv_dim):
    S = cx.S
    sqp, rsp, rdp, qnp, t1p, t2p = pools
    if gain_ap is not None:
        sq = sqp.next()
        S.act(sq[0:m, 0:n], ps[0:m, 0:n], AF.Square, reads=[ps], writes=[sq])
        ps2 = cx.ps.next()
        S.mm(ps2[0:m, 0:n], cx.ones[0:m, 0:m], sq[0:m, 0:n], True, True, reads=[cx.ones, sq], writes=[ps2])
        r = rsp.next()
        S.act(r[0:m, 0:n], ps2[0:m, 0:n], AF.Sqrt, reads=[ps2, cx.eps], writes=[r], bias=cx.eps[0:m, 0:1], scale=inv_dim)
        rd = rdp.next()
        S.op('dve', lambda e: e.reciprocal(rd[0:m, 0:n], r[0:m, 0:n]), reads=[r], writes=[rd])
        qn = qnp.next()
        S.stt('dve', qn[0:m, 0:n], ps[0:m, 0:n], gain_ap, rd[0:m, 0:n], ALU.mult, ALU.mult,
              reads=[ps, gain_buf, rd], writes=[qn])
    else:
        qn = qnp.next()
        S.copy('act', qn[0:m, 0:n], ps[0:m, 0:n], reads=[ps], writes=[qn])
    t1 = t1p.next()
    S.tt('pool', t1[0:m, 0:n], qn[0:m, 0:n], cs[0:m, t0:t0 + n], ALU.mult, reads=[qn, cs], writes=[t1])
    t2 = t2p.next()
    x1, x2 = rope_lo
    if 2 * half != m:
        S.memset('pool', t2[0:m, 0:n], 0.0, writes=[t2])
    S.tt('dve', t2[x1:x1 + half, 0:n], qn[x2:x2 + half, 0:n], sn[x2:x2 + half, t0:t0 + n], ALU.mult,
         reads=[qn, sn], writes=[t2])
    S.tt('dve', t2[x2:x2 + half, 0:n], qn[x1:x1 + half, 0:n], sn[x1:x1 + half, t0:t0 + n], ALU.mult,
         reads=[qn, sn], writes=[t2])
    S.tt('pool', out_ap, t1[0:m, 0:n], t2[0:m, 0:n], ALU.add, reads=[t1, t2], writes=[out_buf])


def fm(v):
    v = np.asarray(v)
    return np.ascontiguousarray(v.reshape(-1, 128).T)


def build_gqa_pre():
    nc = new_nc()
    xT = din(nc, "xT", [D, T])
    mod_d = din(nc, "mod", [128, 6, 8, 2])
    gain = din(nc, "gain", [128, 8])
    w_in = din(nc, "w_in", [D, 4096])
    qkg = din(nc, "qkg", [128, 2])
    cs_d = din(nc, "cs", [128, T])
    sn_d = din(nc, "sn", [128, T])
    qkv = dout(nc, "qkv", [4096, T], BF16)
    with ExitStack() as es:
        cx = Ctx(nc, es)
        S = cx.S
        mod = S.sbuf(es, "mod", [128, 6, 8, 2], F32)
        S.dma('sp', mod[:], mod_d[:, :, :, :], writes=[mod])
        h = S.sbuf(es, "h", [128, 8, T], BF16)
        cs = S.sbuf(es, "cs_s", [128, T], F32)
        sn = S.sbuf(es, "sn_s", [128, T], F32)
        g2 = S.sbuf(es, "qkg_s", [128, 2], F32)
        S.dma('sp', cs[:], cs_d[:, :], writes=[cs])
        S.dma('sp', sn[:], sn_d[:, :], writes=[sn])
        S.dma('sp', g2[:], qkg[:, :], writes=[g2])
        with ExitStack() as es2:
            A = emit_affine(cx, es2, mod, 0, 1, gain, "afa")
            xt = Ring([S.sbuf(es2, "xt%d" % i, [128, 8, 512], F32) for i in range(2)])
            xv = xT.rearrange("(c p) t -> p c t", p=128)

            def x_src(ti):
                c0, n, j = TILES[ti]
                b = xt.next()
                S.dma('sp', b[:, :, 0:n], xv[:, :, c0:c0 + n], writes=[b])
                return b, b[:, :, 0:n]
            emit_norm_mod(cx, es2, x_src, A, mod, 0, h)
            S.barrier()
            S.flush()
        with ExitStack() as es3:
            def mk(name, dt):
                return Ring([S.sbuf(es3, "%s%d" % (name, i), [128, 512], dt) for i in range(2)])
            pools = (mk("sq", F32), mk("rs", F32), mk("rd", F32), mk("qn", F32), mk("t1", F32), mk("t2", F32))
            ob = mk("ob", BF16)
            ochunks = [(o * 128, 128) for o in range(32)]

            def evac(oi, ti, ps, m, t0, n, j):
                o = ob.next()
                if oi < 24:
                    gi = 0 if oi < 16 else 1
                    emit_headnorm_rope(cx, pools, ps, 128, n, g2[:, gi:gi + 1], g2, cs, sn, t0, o[:, 0:n], o, 128, (0, 64), 64,
                                       1.0 / 128)
                else:
                    S.copy('act', o[:, 0:n], ps[:, 0:n], reads=[ps], writes=[o])
                S.dma('pool', qkv[oi * 128:(oi + 1) * 128, t0:t0 + n], o[:, 0:n], reads=[o])
            linear(cx, es3, w_in, D, 128, ochunks, h, TILES, evac)
            cx.finish()
    return nc


def rope_tables_axial(pos, rot_dim):
    GRID_W = 64
    row = (pos // GRID_W).astype(np.float32)
    col = (pos % GRID_W).astype(np.float32)
    axis_dim = rot_dim // 2
    inv_freq = np.power(np.float32(10000.0), -np.arange(0, axis_dim, 2, dtype=np.float32) / np.float32(axis_dim)).astype(np.float32)
    ang = np.concatenate([row[:, None] * inv_freq, col[:, None] * inv_freq], axis=-1).astype(np.float32)
    return np.cos(ang).astype(np.float32), np.sin(ang).astype(np.float32)


NCORES = 8


def core_tok(r):
    b = r // 4
    q = r % 4
    return b, q * TL, (q + 1) * TL


def host_xT(x, ctx):
    out = []
    for r in range(NCORES):
        b, t0, t1 = core_tok(r)
        out.append(np.ascontiguousarray(np.concatenate([x[b, t0:t1].T, ctx[b].T], axis=1)))
    return out


def host_cvec(c, c_ctx):
    return [np.ascontiguousarray(np.stack([fm(c[r // 4]), fm(c_ctx)], axis=-1)) for r in range(NCORES)]


def host_rope_fm(rot_dim, x1, x2, m):
    half = rot_dim // 2
    res = []
    for r in range(NCORES):
        b, t0, t1 = core_tok(r)
        cos, sin = rope_tables_axial(np.arange(t0, t1), rot_dim)
        cs = np.ones((m, T), np.float32)
        sn = np.zeros((m, T), np.float32)
        cs[x1:x1 + half, :TL] = cos.T
        cs[x2:x2 + half, :TL] = cos.T
        sn[x1:x1 + half, :TL] = sin.T
        sn[x2:x2 + half, :TL] = -sin.T
        res.append((cs, sn))
    return res


TRACE = False


def run(nc, in_maps):
    if TRACE:
        res = run_bass_kernel_spmd(nc, in_maps, core_ids=list(range(NCORES)), trace=True)
        print("EXEC_TIME_NS", res.exec_time_ns)
        return res.results
    res = run_bass_kernel_spmd(nc, in_maps, core_ids=list(range(NCORES)))
    t = getattr(res, 'exec_time_ns', None)
    if t is not None:
        print("EXEC_TIME_NS", t)
    return res.results


def launch_gqa_pre(xT, mods, inp, i, j):
    nc = build_gqa_pre()
    tabs = host_rope_fm(128, 0, 64, 128)
    gain = fm(inp['norm_mix'][i])
    w_in = np.ascontiguousarray(inp['gqa_w_in'][j])
    qkg = np.ascontiguousarray(np.stack([inp['gqa_q_norm'][j], inp['gqa_k_norm'][j]], axis=1))
    maps = [dict(xT=xT[r], mod=mods[i][r], gain=gain, w_in=w_in, qkg=qkg,
                 cs=tabs[r][0], sn=tabs[r][1]) for r in range(NCORES)]
    return [o['qkv'] for o in run(nc, maps)]


def emit_attention(cx, es_outer, q_d, kT_d, v_d, nheads, group, dk, dv, scale, nkc, AO, LOOK=3):
    S = cx.S
    with ExitStack() as es:
        kt = Ring([S.sbuf(es, "kt%d" % i, [dk, NK], BF16) for i in range(2)])
        vv = Ring([S.sbuf(es, "vv%d" % i, [128, nkc, dv], BF16) for i in range(2)])
        qt = Ring([S.sbuf(es, "qt%d" % i, [dk, T], BF16) for i in range(2)])
        pT = Ring([S.sbuf(es, "pT%d" % i, [128, 512], BF16) for i in range(LOOK + 2)])
        rdp = Ring([S.sbuf(es, "ard%d" % i, [128, 512], F32) for i in range(2)])
        psO = Ring([S.psum(es, "psO%d" % i) for i in range(2)])
        psD = Ring([S.psum(es, "psD%d" % i) for i in range(2)])
        nctx = CTX // 128
        pending = []

        def second_half(it):
            (v, p, po, pd, kc, n, first, last, h, c0) = it
            S.mm(po[0:dv, 0:n], v[:, kc, :], p[:, 0:n], first, last, reads=[v, p], writes=[po], inc=False)
            S.mm(pd[0:dv, 0:n], cx.ones_bf[:, 0:dv], p[:, 0:n], first, last, reads=[cx.ones_bf, p],
                 writes=[pd], inc=True)
            if last:
                rd = rdp.next()
                S.op('dve', lambda e, rd=rd, pd=pd, n=n: e.reciprocal(rd[0:dv, 0:n], pd[0:dv, 0:n]),
                     reads=[pd], writes=[rd])
                S.tt('dve', AO[:, h, c0:c0 + n], po[0:dv, 0:n], rd[0:dv, 0:n], ALU.mult, reads=[po, rd], writes=[AO])

        for g in range(nheads // group):
            k = kt.next()
            qn_ = NK // 4
            for part in range(4):
                S.dma('sp' if part % 2 == 0 else 'pool', k[:, part * qn_:(part + 1) * qn_],
                      kT_d[g * dk:(g + 1) * dk, part * qn_:(part + 1) * qn_], writes=[k], join=part > 0)
            v = vv.next()
            vsrc = v_d[g].rearrange("p (c d) -> p c d", d=dv)
            hc_ = nkc // 2
            S.dma('sp', v[:, 0:hc_, :], vsrc[:, 0:hc_, :], writes=[v])
            S.dma('pool', v[:, hc_:nkc, :], vsrc[:, hc_:nkc, :], writes=[v], join=True)
            for hh in range(group):
                h = g * group + hh
                q = qt.next()
                S.dma('sp', q[:, :], q_d[h * dk:(h + 1) * dk, :], writes=[q])
                for (c0, n, j) in TILES:
                    kcs = list(range(nkc)) if j == 0 else list(range(nkc - nctx, nkc))
                    po = psO.next()
                    pd = psD.next()
                    for idx, kc in enumerate(kcs):
                        ps = cx.ps.next()
                        S.mm(ps[:, 0:n], k[:, kc * 128:(kc + 1) * 128], q[:, c0:c0 + n], True, True,
                             reads=[k, q], writes=[ps])
                        p = pT.next()
                        S.act(p[:, 0:n], ps[:, 0:n], AF.Exp, reads=[ps], writes=[p], scale=scale)
                        pending.append((v, p, po, pd, kc, n, idx == 0, idx == len(kcs) - 1, h, c0))
                        if len(pending) > LOOK:
                            second_half(pending.pop(0))
        while pending:
            second_half(pending.pop(0))
        S.barrier()
        S.flush()


def emit_post(cx, es_outer, AO, kp, K, w_out_d, xT_d, x1_d, h2_d, mod, gain_d, hres=None):
    S = cx.S
    x1buf = Buf(None, "x1_dram")
    with ExitStack() as es:
        xt = Ring([S.sbuf(es, "pxt%d" % i, [128, 512], F32) for i in range(3)])
        xo = Ring([S.sbuf(es, "pxo%d" % i, [128, 512], F32) for i in range(3)])

        def evac(oi, ti, ps, m, t0, n, j):
            a = xt.next()
            S.dma('sp', a[:, 0:n], xT_d[oi * 128:(oi + 1) * 128, t0:t0 + n], writes=[a])
            o = xo.next()
            S.stt('dve', o[:, 0:n], ps[:, 0:n], mod[:, 2, oi, j:j + 1], a[:, 0:n], ALU.mult, ALU.add,
                  reads=[ps, mod, a], writes=[o])
            S.dma('pool', x1_d[oi * 128:(oi + 1) * 128, t0:t0 + n], o[:, 0:n], reads=[o], writes=[x1buf], join=True)
        linear(cx, es, w_out_d, K, kp, [(o * 128, 128) for o in range(8)], AO, TILES, evac, gcols=256, tag="wo")
        S.barrier()
        S.flush()
    with ExitStack() as es:
        h2 = S.sbuf(es, "h2", [128, 8, T], BF16)
        A = emit_affine(cx, es, mod, 3, 4, gain_d, "aff")
        xt = Ring([S.sbuf(es, "nxt%d" % i, [128, 8, 512], F32) for i in range(2)])
        xv = x1_d.rearrange("(c p) t -> p c t", p=128)

        def x_src(ti):
            c0, n, j = TILES[ti]
            b = xt.next()
            S.dma('sp', b[:, :, 0:n], xv[:, :, c0:c0 + n], reads=[x1buf], writes=[b])
            return b, b[:, :, 0:n]
        emit_norm_mod(cx, es, x_src, A, mod, 3, h2, tag="n2")
        S.dma('pool', h2_d.rearrange("(c p) t -> p c t", p=128), h2[:, :, :], reads=[h2])
        S.barrier()
        S.flush()


def build_attn_post(nheads, group, dk, dv, scale):
    nc = new_nc()
    nkv = nheads // group
    nkc = NK // 128
    q_d = din(nc, "q", [nheads * dk, T], BF16)
    kT_d = din(nc, "kT", [nkv * dk, NK], BF16)
    v_d = din(nc, "v", [nkv, 128, nkc * dv], BF16)
    xT_d = din(nc, "xT", [D, T])
    mod_d = din(nc, "mod", [128, 6, 8, 2])
    gain_d = din(nc, "gain", [128, 8])
    w_out_d = din(nc, "w_out", [nheads * dv, D])
    x1_d = dout(nc, "x1", [D, T])
    h2_d = dout(nc, "h2", [D, T], BF16)
    with ExitStack() as es:
        cx = Ctx(nc, es, npsum=4)
        S = cx.S
        mod = S.sbuf(es, "mod", [128, 6, 8, 2], F32)
        S.dma('sp', mod[:], mod_d[:, :, :, :], writes=[mod])
        with ExitStack() as es2:
            AO = S.sbuf(es2, "AO", [dv, nheads, T], BF16)
            emit_attention(cx, es2, q_d, kT_d, v_d, nheads, group, dk, dv, scale, nkc, AO)
            emit_post(cx, es2, AO, dv, nheads * dv, w_out_d, xT_d, x1_d, h2_d, mod, gain_d)
        cx.finish()
    return nc


def host_kv_gqa(qkv):
    kTs, vs = [], []
    for b in range(NB):
        k = np.concatenate([qkv[4 * b + q][2048:3072, :TL] for q in range(4)] + [qkv[4 * b][2048:3072, TL:]], axis=1)
        v = np.concatenate([qkv[4 * b + q][3072:4096, :TL] for q in range(4)] + [qkv[4 * b][3072:4096, TL:]], axis=1)
        kTs.append(np.ascontiguousarray(k))
        v4 = v.reshape(8, 128, NK // 128, 128)
        vs.append(np.ascontiguousarray(v4.transpose(0, 3, 2, 1)).reshape(8, 128, (NK // 128) * 128))
    return kTs, vs


def launch_gqa_attn(qkv, xT, mods, inp, i, j):
    nc = build_attn_post(16, 2, 128, 128, 128 ** -0.5)
    kTs, vs = host_kv_gqa(qkv)
    gain = fm(inp['norm_ffn'][i])
    w_out = np.ascontiguousarray(inp['gqa_w_out'][j])
    maps = [dict(q=np.ascontiguousarray(qkv[r][0:2048]), kT=kTs[r // 4], v=vs[r // 4], xT=xT[r], mod=mods[i][r],
                 gain=gain, w_out=w_out) for r in range(NCORES)]
    res = run(nc, maps)
    return [o['x1'] for o in res], [o['h2'] for o in res]


TP = TL + 2 + CTX + 2
UT = [(0, 512), (510, 512), (1020, 512), (1530, 512), (2040, 268)]
OT = [(1, 512, 0), (513, 512, 0), (1025, 512, 0), (1537, 512, 0), (2051, 256, 1)]


def tcol(p):
    return p - 1 if p < 2050 else p - 3


def build_ffn(final):
    nc = new_nc()
    h2_d = din(nc, "h2p", [D, TP], BF16)
    x1_d = din(nc, "x1", [D, T])
    mod_d = din(nc, "mod", [128, 6, 8, 2])
    w_in_d = din(nc, "w_in", [D, 2 * DFF])
    cw_d = din(nc, "cw", [128, 48, 3])
    cb_d = din(nc, "cb", [128, 48])
    w_out_d = din(nc, "w_out", [DFF, D])
    x2_d = dout(nc, "x2", [D, T])
    if final:
        fg_d = din(nc, "fgain", [128, 8])
        y_d = dout(nc, "y", [D, TL])
    with ExitStack() as es:
        cx = Ctx(nc, es)
        S = cx.S
        mod = S.sbuf(es, "mod", [128, 6, 8, 2], F32)
        S.dma('sp', mod[:], mod_d[:, :, :, :], writes=[mod])
        cw = S.sbuf(es, "cw", [128, 48, 3], F32)
        cb = S.sbuf(es, "cb", [128, 48], F32)
        S.dma('sp', cw[:], cw_d[:, :, :], writes=[cw])
        S.dma('sp', cb[:], cb_d[:, :], writes=[cb])
        x2buf = Buf(None, "x2_dram")
        with ExitStack() as esG:
            G = S.sbuf(esG, "G", [128, 24, TP], BF16)
            with ExitStack() as es1:
                h2 = S.sbuf(es1, "h2", [128, 8, TP], BF16)
                S.dma('sp', h2[:, 0:4, :], h2_d.rearrange("(c p) t -> p c t", p=128)[:, 0:4, :], writes=[h2])
                S.dma('pool', h2[:, 4:8, :], h2_d.rearrange("(c p) t -> p c t", p=128)[:, 4:8, :], writes=[h2], join=True)
                ta = Ring([S.sbuf(es1, "ta%d" % i, [128, 512], F32) for i in range(3)])
                sa = [S.sbuf(es1, "sa%d" % i, [128, 512], F32) for i in range(5)]
                ochunks = [(k * 128, 128) for k in range(48)]

                def evac(oi, ti, ps, m, u0, un, jj):
                    jch = oi // 2
                    isval = oi % 2
                    ch = jch + 24 * isval
                    on = un - 2
                    t = ta.next()
                    S.act(t[:, 0:on], ps[:, 1:1 + on], AF.Identity, reads=[ps, cw, cb], writes=[t],
                          bias=cb[:, ch:ch + 1], scale=cw[:, ch, 1:2])
                    S.stt('dve', t[:, 0:on], ps[:, 0:on], cw[:, ch, 0:1], t[:, 0:on], ALU.mult, ALU.add,
                          reads=[ps, cw, t], writes=[t])
                    S.stt('dve', t[:, 0:on], ps[:, 2:2 + on], cw[:, ch, 2:3], t[:, 0:on], ALU.mult, ALU.add,
                          reads=[ps, cw, t], writes=[t])
                    if not isval:
                        S.act(sa[ti][:, 0:on], t[:, 0:on], AF.Silu, reads=[t], writes=[sa[ti]])
                    else:
                        S.tt('pool', G[:, jch, u0 + 1:u0 + 1 + on], sa[ti][:, 0:on], t[:, 0:on], ALU.mult,
                             reads=[sa[ti], t], writes=[G])
                linear(cx, es1, w_in_d, D, 128, ochunks, h2, [(u0, un, 0) for (u0, un) in UT], evac, gcols=256, tag="fi",
                       cast_engs=('pool', 'dve'))
                S.barrier()
                S.flush()
            with ExitStack() as es2:
                xt = Ring([S.sbuf(es2, "fxt%d" % i, [128, 512], F32) for i in range(3)])
                xo = Ring([S.sbuf(es2, "fxo%d" % i, [128, 512], F32) for i in range(3)])

                def evac2(oi, ti, ps, m, p0, n, j):
                    t0 = tcol(p0)
                    a = xt.next()
                    S.dma('sp', a[:, 0:n], x1_d[oi * 128:(oi + 1) * 128, t0:t0 + n], writes=[a])
                    o = xo.next()
                    S.stt('dve', o[:, 0:n], ps[:, 0:n], mod[:, 5, oi, j:j + 1], a[:, 0:n], ALU.mult, ALU.add,
                          reads=[ps, mod, a], writes=[o])
                    S.dma('pool', x2_d[oi * 128:(oi + 1) * 128, t0:t0 + n], o[:, 0:n], reads=[o], writes=[x2buf], join=True)
                linear(cx, es2, w_out_d, DFF, 128, [(o * 128, 128) for o in range(8)], G, OT, evac2, gcols=256, tag="fo")
                S.barrier()
                S.flush()
        if final:
            with ExitStack() as es3:
                fg = S.sbuf(es3, "fg", [128, 8], F32)
                S.dma('sp', fg[:], fg_d[:, :], writes=[fg])
                xt = Ring([S.sbuf(es3, "yxt%d" % i, [128, 8, 512], F32) for i in range(2)])
                sq = Ring([S.sbuf(es3, "ysq%d" % i, [128, 8, 512], F32) for i in range(2)])
                rs = Ring([S.sbuf(es3, "yrs%d" % i, [128, 512], F32) for i in range(2)])
                rdp = Ring([S.sbuf(es3, "yrd%d" % i, [128, 512], F32) for i in range(2)])
                xv = x2_d.rearrange("(c p) t -> p c t", p=128)
                yv = y_d.rearrange("(c p) t -> p c t", p=128)
                for (c0, n, j) in TILES[0:4]:
                    b = xt.next()
                    S.dma('sp', b[:, :, 0:n], xv[:, :, c0:c0 + n], reads=[x2buf], writes=[b])
                    q = sq.next()
                    S.tt('pool', q[:, :, 0:n], b[:, :, 0:n], b[:, :, 0:n], ALU.mult, reads=[b], writes=[q])
                    ps = cx.ps.next()
                    for c in range(8):
                        S.mm(ps[:, 0:n], cx.ones[:, :], q[:, c, 0:n], c == 0, c == 7, reads=[cx.ones, q], writes=[ps])
                    r = rs.next()
                    S.act(r[:, 0:n], ps[:, 0:n], AF.Sqrt, reads=[ps, cx.eps], writes=[r], bias=cx.eps[:, 0:1], scale=1.0 / D)
                    rd = rdp.next()
                    S.op('dve', lambda e, rd=rd, r=r, n=n: e.reciprocal(rd[:, 0:n], r[:, 0:n]), reads=[r], writes=[rd])
                    S.tt('dve', q[:, :, 0:n], b[:, :, 0:n], rd[:, 0:n].unsqueeze(1).broadcast_to([128, 8, n]), ALU.mult,
                         reads=[b, rd], writes=[q])
                    S.tt('pool', b[:, :, 0:n], q[:, :, 0:n], fg[:, :].unsqueeze(2).broadcast_to([128, 8, n]), ALU.mult,
                         reads=[q, fg], writes=[b])
                    S.dma('pool', yv[:, :, c0:c0 + n], b[:, :, 0:n], reads=[b])
                S.barrier()
                S.flush()
        cx.finish()
    return nc


def host_h2p(h2):
    out = []
    for r in range(NCORES):
        q = r % 4
        a = np.asarray(h2[r])
        z1 = np.zeros((D, 1), a.dtype)
        left = np.asarray(h2[r - 1])[:, TL - 1:TL] if q > 0 else z1
        right = np.asarray(h2[r + 1])[:, 0:1] if q < 3 else z1
        out.append(np.ascontiguousarray(np.concatenate([left, a[:, :TL], right, z1, a[:, TL:], z1], axis=1)))
    return out


def launch_ffn(h2, x1, mods, inp, i, final):
    nc = build_ffn(final)
    h2p = host_h2p(h2)
    w_in = np.ascontiguousarray(inp['ffn_w_in'][i].reshape(D, 2, 24, 128).transpose(0, 2, 1, 3).reshape(D, 2 * DFF))
    w_out = np.ascontiguousarray(inp['ffn_w_out'][i])
    cw = np.ascontiguousarray(inp['ffn_conv_w'][i].T.reshape(48, 128, 3).transpose(1, 0, 2))
    cb = fm(inp['ffn_conv_b'][i])
    maps = []
    for r in range(NCORES):
        m = dict(h2p=h2p[r], x1=x1[r], mod=mods[i][r], w_in=w_in, cw=cw, cb=cb, w_out=w_out)
        if final:
            m['fgain'] = fm(inp['final_norm'])
        maps.append(m)
    res = run(nc, maps)
    if final:
        return [o['x2'] for o in res], [o['y'] for o in res]
    return [o['x2'] for o in res], None


def build_mla_pre():
    nc = new_nc()
    xT = din(nc, "xT", [D, T])
    mod_d = din(nc, "mod", [128, 6, 8, 2])
    gain = din(nc, "gain", [128, 8])
    w_in = din(nc, "w_in", [D, 1024 + 64])
    qg_d = din(nc, "qg", [128, 8])
    w_qb = din(nc, "w_qb", [768, 2048])
    w_kvb = din(nc, "w_kvb", [256, 2048])
    cs_d = din(nc, "cs", [128, T])
    sn_d = din(nc, "sn", [128, T])
    q_o = dout(nc, "q", [2048, T], BF16)
    kv_o = dout(nc, "kv", [2048, T], BF16)
    kr_o = dout(nc, "kr", [64, T], BF16)
    with ExitStack() as es:
        cx = Ctx(nc, es)
        S = cx.S
        mod = S.sbuf(es, "mod", [128, 6, 8, 2], F32)
        S.dma('sp', mod[:], mod_d[:, :, :, :], writes=[mod])
        cs = S.sbuf(es, "cs_s", [128, T], F32)
        sn = S.sbuf(es, "sn_s", [128, T], F32)
        qg = S.sbuf(es, "qg_s", [128, 8], F32)
        S.dma('sp', cs[:], cs_d[:, :], writes=[cs])
        S.dma('sp', sn[:], sn_d[:, :], writes=[sn])
        S.dma('sp', qg[:], qg_d[:, :], writes=[qg])
        cqn = S.sbuf(es, "cqn", [128, 8, T], BF16)
        cq_d = nc.dram_tensor("cq_scr", [D, T], F32, kind="Internal").ap()
        cqbuf = Buf(None, "cq_dram")
        with ExitStack() as es_h:
            h = S.sbuf(es_h, "h", [128, 8, T], BF16)
            with ExitStack() as es2:
                A = emit_affine(cx, es2, mod, 0, 1, gain, "afa")
                xt = Ring([S.sbuf(es2, "xt%d" % i, [128, 8, 512], F32) for i in range(2)])
                xv = xT.rearrange("(c p) t -> p c t", p=128)

                def x_src(ti):
                    c0, n, j = TILES[ti]
                    b = xt.next()
                    S.dma('sp', b[:, :, 0:n], xv[:, :, c0:c0 + n], writes=[b])
                    return b, b[:, :, 0:n]
                emit_norm_mod(cx, es2, x_src, A, mod, 0, h)
                S.barrier()
                S.flush()
            with ExitStack() as es3:
                def mk(name, dt):
                    return Ring([S.sbuf(es3, "%s%d" % (name, i), [128, 512], dt) for i in range(2)])
                pools = (mk("sq", F32), mk("rs", F32), mk("rd", F32), mk("qn", F32), mk("t1", F32), mk("t2", F32))
                ob = mk("ob", BF16)
                cqs = Ring([S.sbuf(es3, "cqs%d" % i, [128, 512], F32) for i in range(3)])
                ochunks = [(o * 128, 128) for o in range(8)] + [(1024, 64)]

                def evac1(oi, ti, ps, m, t0, n, j):
                    if oi < 8:
                        o = cqs.next()
                        S.copy('act', o[:, 0:n], ps[:, 0:n], reads=[ps], writes=[o])
                        S.dma('pool', cq_d[oi * 128:(oi + 1) * 128, t0:t0 + n], o[:, 0:n], reads=[o], writes=[cqbuf], join=True)
                    else:
                        o = ob.next()
                        emit_headnorm_rope(cx, pools, ps, 64, n, None, None, cs, sn, t0, o[0:64, 0:n], o, 64, (0, 32), 16, 0.0)
                        S.dma('pool', kr_o[:, t0:t0 + n], o[0:64, 0:n], reads=[o])
                linear(cx, es3, w_in, D, 128, ochunks, h, TILES, evac1, gcols=256, tag="l1")
                S.barrier()
                S.flush()
        with ExitStack() as es4:
            cqt = Ring([S.sbuf(es4, "cqt%d" % i, [128, 8, 512], F32) for i in range(2)])
            sq = Ring([S.sbuf(es4, "lsq%d" % i, [128, 8, 512], F32) for i in range(2)])
            rs = Ring([S.sbuf(es4, "lrs%d" % i, [128, 512], F32) for i in range(2)])
            rdp = Ring([S.sbuf(es4, "lrd%d" % i, [128, 512], F32) for i in range(4)])
            cqv = cq_d.rearrange("(c p) t -> p c t", p=128)
            for (c0, n, j) in TILES:
                cq = cqt.next()
                S.dma('sp', cq[:, :, 0:n], cqv[:, :, c0:c0 + n], reads=[cqbuf], writes=[cq])
                q = sq.next()
                S.tt('pool', q[:, :, 0:n], cq[:, :, 0:n], cq[:, :, 0:n], ALU.mult, reads=[cq], writes=[q])
                for (ca, cb_, inv) in ((0, 6, 1.0 / 768), (6, 8, 1.0 / 256)):
                    ps = cx.ps.next()
                    for c in range(ca, cb_):
                        S.mm(ps[:, 0:n], cx.ones[:, :], q[:, c, 0:n], c == ca, c == cb_ - 1, reads=[cx.ones, q], writes=[ps])
                    r = rs.next()
                    S.act(r[:, 0:n], ps[:, 0:n], AF.Sqrt, reads=[ps, cx.eps], writes=[r], bias=cx.eps[:, 0:1], scale=inv)
                    rd = rdp.next()
                    S.op('dve', lambda e, rd=rd, r=r, n=n: e.reciprocal(rd[:, 0:n], r[:, 0:n]), reads=[r], writes=[rd])
                    for c in range(ca, cb_):
                        S.stt('dve', cqn[:, c, c0:c0 + n], cq[:, c, 0:n], qg[:, c:c + 1], rd[:, 0:n], ALU.mult, ALU.mult,
                              reads=[cq, qg, rd], writes=[cqn])
            S.barrier()
            S.flush()
        with ExitStack() as es5:
            def mk(name, dt):
                return Ring([S.sbuf(es5, "%s%d" % (name, i), [128, 512], dt) for i in range(2)])
            pools = (mk("sq", F32), mk("rs", F32), mk("rd", F32), mk("qn", F32), mk("t1", F32), mk("t2", F32))
            ob = mk("ob", BF16)

            def evac2(oi, ti, ps, m, t0, n, j):
                o = ob.next()
                emit_headnorm_rope(cx, pools, ps, 128, n, None, None, cs, sn, t0, o[:, 0:n], o, 128, (0, 32), 16, 0.0)
                S.dma('pool', q_o[oi * 128:(oi + 1) * 128, t0:t0 + n], o[:, 0:n], reads=[o])
            linear(cx, es5, w_qb, 768, 128, [(o * 128, 128) for o in range(16)], cqn, TILES, evac2, tag="l2")
            kvsrc = Buf(cqn.t[:, 6:8, :], "kvsrc")
            kvsrc.w, kvsrc.r = dict(cqn.w), dict(cqn.r)

            def evac3(oi, ti, ps, m, t0, n, j):
                o = ob.next()
                S.copy('act', o[:, 0:n], ps[:, 0:n], reads=[ps], writes=[o])
                S.dma('pool', kv_o[oi * 128:(oi + 1) * 128, t0:t0 + n], o[:, 0:n], reads=[o])
            linear(cx, es5, w_kvb, 256, 128, [(o * 128, 128) for o in range(16)], kvsrc, TILES, evac3, tag="l3")
            cx.finish()
    return nc


def host_mla_weights(inp, j):
    w_in = inp['mla_w_in'][j]
    w_in2 = np.zeros((D, 1024 + 64), np.float32)
    w_in2[:, :1024] = w_in[:, :1024]
    w_in2[:, 1024:1040] = w_in[:, 1024:1040]
    w_in2[:, 1056:1072] = w_in[:, 1040:1056]
    w_qb = inp['mla_w_qb'][j].reshape(768, 16, 96)
    w_qb2 = np.zeros((768, 16, 128), np.float32)
    w_qb2[:, :, 0:16] = w_qb[:, :, 64:80]
    w_qb2[:, :, 32:48] = w_qb[:, :, 80:96]
    w_qb2[:, :, 64:128] = w_qb[:, :, 0:64]
    return w_in2, np.ascontiguousarray(w_qb2.reshape(768, 2048)), np.ascontiguousarray(inp['mla_w_kvb'][j])


def launch_mla_pre(xT, mods, inp, i, j):
    nc = build_mla_pre()
    tabs = host_rope_fm(32, 0, 32, 128)
    gain = fm(inp['norm_mix'][i])
    w_in2, w_qb2, w_kvb = host_mla_weights(inp, j)
    qg = np.ascontiguousarray(np.concatenate([fm(inp['mla_q_norm'][j]), fm(inp['mla_kv_norm'][j])], axis=1))
    maps = [dict(xT=xT[r], mod=mods[i][r], gain=gain, w_in=w_in2, qg=qg, w_qb=w_qb2, w_kvb=w_kvb,
                 cs=tabs[r][0], sn=tabs[r][1]) for r in range(NCORES)]
    res = run(nc, maps)
    return [o['q'] for o in res], [o['kv'] for o in res], [o['kr'] for o in res]


def host_kv_mla(kv, kr):
    kTs, vs = [], []
    for b in range(NB):
        kvb = np.concatenate([np.asarray(kv[4 * b + q])[:, :TL] for q in range(4)] + [np.asarray(kv[4 * b])[:, TL:]], axis=1)
        krb = np.concatenate([np.asarray(kr[4 * b + q])[:, :TL] for q in range(4)] + [np.asarray(kr[4 * b])[:, TL:]], axis=1)
        kvh = kvb.reshape(16, 128, NK)
        kT = np.zeros((16, 128, NK), kvb.dtype)
        kT[:, 0:64, :] = krb[None, :, :]
        kT[:, 64:128, :] = kvh[:, 0:64, :]
        kTs.append(np.ascontiguousarray(kT.reshape(16 * 128, NK)))
        v = kvh[:, 64:128, :]
        v4 = v.reshape(16, 64, NK // 128, 128)
        vs.append(np.ascontiguousarray(v4.transpose(0, 3, 2, 1)).reshape(16, 128, (NK // 128) * 64))
    return kTs, vs


def launch_mla_attn(q, kv, kr, xT, mods, inp, i, j):
    nc = build_attn_post(16, 1, 128, 64, 96 ** -0.5)
    kTs, vs = host_kv_mla(kv, kr)
    gain = fm(inp['norm_ffn'][i])
    w_out = np.ascontiguousarray(inp['mla_w_out'][j])
    maps = [dict(q=q[r], kT=kTs[r // 4], v=vs[r // 4], xT=xT[r], mod=mods[i][r],
                 gain=gain, w_out=w_out) for r in range(NCORES)]
    res = run(nc, maps)
    return [o['x1'] for o in res], [o['h2'] for o in res]


NS = CTX + SEQ
HC = 64
SUP = [(0, 256)] + [(256 + 512 * i, 512) for i in range(16)]


def build_hgrn_pre():
    nc = new_nc()
    xT = din(nc, "xT", [D, T])
    mod_d = din(nc, "mod", [128, 6, 8, 2])
    gain = din(nc, "gain", [128, 8])
    w_in = din(nc, "w_in", [D, 5120])
    p_o = dout(nc, "p", [5120, T])
    with ExitStack() as es:
        cx = Ctx(nc, es)
        S = cx.S
        mod = S.sbuf(es, "mod", [128, 6, 8, 2], F32)
        S.dma('sp', mod[:], mod_d[:, :, :, :], writes=[mod])
        h = S.sbuf(es, "h", [128, 8, T], BF16)
        with ExitStack() as es2:
            A = emit_affine(cx, es2, mod, 0, 1, gain, "afa")
            xt = Ring([S.sbuf(es2, "xt%d" % i, [128, 8, 512], F32) for i in range(2)])
            xv = xT.rearrange("(c p) t -> p c t", p=128)

            def x_src(ti):
                c0, n, j = TILES[ti]
                b = xt.next()
                S.dma('sp', b[:, :, 0:n], xv[:, :, c0:c0 + n], writes=[b])
                return b, b[:, :, 0:n]
            emit_norm_mod(cx, es2, x_src, A, mod, 0, h)
            S.barrier()
            S.flush()
        with ExitStack() as es3:
            ob = Ring([S.sbuf(es3, "ob%d" % i, [128, 512], F32) for i in range(3)])

            def evac(oi, ti, ps, m, t0, n, j):
                o = ob.next()
                if oi % 2 == 0:
                    S.copy('act', o[:, 0:n], ps[:, 0:n], reads=[ps], writes=[o])
                else:
                    S.copy('dve', o[:, 0:n], ps[:, 0:n], reads=[ps], writes=[o])
                S.dma('pool', p_o[oi * 128:(oi + 1) * 128, t0:t0 + n], o[:, 0:n], reads=[o])
            linear(cx, es3, w_in, D, 128, [(o * 128, 128) for o in range(40)], h, TILES, evac, tag="hl", cast_engs=('pool',))
            cx.finish()
    return nc


def build_hgrn_scan():
    nc = new_nc()
    q_d = din(nc, "q", [4, 128, NS])
    f_d = din(nc, "f", [4, 128, NS])
    v_d = din(nc, "v", [4, HC, (NS // HC) * 128])
    lb_d = din(nc, "lb", [128, 4])
    rm_d = din(nc, "rmask", [128, 512])
    ut_d = din(nc, "ut", [HC, HC])
    id_d = din(nc, "ident", [128, 128])
    o_d = dout(nc, "o", [4, 128, NS])
    scale = 128 ** -0.5
    with ExitStack() as es:
        cx = Ctx(nc, es)
        S = cx.S
        lb = S.sbuf(es, "lb", [128, 4], F32)
        oml = S.sbuf(es, "oml", [128, 4], F32)
        rm = S.sbuf(es, "rm", [128, 512], F32)
        ut = S.sbuf(es, "ut", [HC, HC], F32)
        idf = S.sbuf(es, "idf", [128, 128], F32)
        idb = S.sbuf(es, "idb", [128, 128], BF16)
        S.dma('sp', lb[:], lb_d[:, :], writes=[lb])
        S.dma('sp', rm[:], rm_d[:, :], writes=[rm])
        S.dma('sp', ut[:], ut_d[:, :], writes=[ut])
        S.dma('sp', idf[:], id_d[:, :], writes=[idf])
        S.copy('dve', idb[:], idf[:], reads=[idf], writes=[idb])
        S.ts('dve', oml[:], lb[:], -1.0, 1.0, ALU.mult, ALU.add, reads=[lb], writes=[oml])

        def mk(name, shape, dt, k=4):
            return Ring([S.sbuf(es, "%s%d" % (name, i), shape, dt) for i in range(k)])
        qin = mk("qin", [128, 512], F32)
        fin = mk("fin", [128, 512], F32)
        vin = mk("vin", [HC, 8, 128], F32)
        vbf = mk("vbf", [HC, 8, 128], BF16)
        sg = mk("sg", [128, 512], F32)
        ff = mk("ff", [128, 512], F32)
        lg = mk("lg", [128, 512], F32)
        k0 = mk("k0", [128, 512], F32)
        cum = mk("cum", [128, 512], F32)
        e1 = mk("e1", [128, 512], F32)
        e2 = mk("e2", [128, 512], F32)
        kkf = mk("kkf", [128, 512], F32)
        dcy = mk("dcy", [128, 8], F32)
        qq = mk("qq", [128, 512], BF16)
        kk = mk("kk", [128, 512], BF16)
        kk2 = mk("kk2", [128, 512], BF16)
        am = mk("am", [HC, HC], BF16, 8)
        k2t = mk("k2t", [HC, 128], BF16, 8)
        ost = mk("ost", [128, 512], F32)
        Sf = [mk("Sf%d_" % sc, [128, 128], F32, 2) for sc in range(4)]
        Sb = [mk("Sb%d_" % sc, [128, 128], BF16, 2) for sc in range(4)]
        s_cur = [None] * 4
        sb_cur = [None] * 4
        for sc in range(4):
            s_cur[sc] = Sf[sc].next()
            S.memset('dve', s_cur[sc][:], 0.0, writes=[s_cur[sc]])
            sb_cur[sc] = Sb[sc].next()
            S.memset('pool', sb_cur[sc][:], 0.0, writes=[sb_cur[sc]])
        for (s0, n) in SUP:
            ncnk = n // HC
            blk = []
            for sc in range(4):
                a_q = qin.next()
                S.dma('sp', a_q[:, 0:n], q_d[sc, :, s0:s0 + n], writes=[a_q])
                a_f = fin.next()
                S.dma('sp', a_f[:, 0:n], f_d[sc, :, s0:s0 + n], writes=[a_f])
                a_v = vin.next()
                c0 = s0 // HC
                S.dma('sp', a_v[:, 0:ncnk, :], v_d[sc, :, c0 * 128:(c0 + ncnk) * 128].rearrange("p (c d) -> p c d", d=128),
                      writes=[a_v])
                b_v = vbf.next()
                S.copy('pool', b_v[:, 0:ncnk, :], a_v[:, 0:ncnk, :], reads=[a_v], writes=[b_v])
                t_sg = sg.next()
                S.act(t_sg[:, 0:n], a_f[:, 0:n], AF.Sigmoid, reads=[a_f], writes=[t_sg])
                t_f = ff.next()
                S.ts('dve', t_f[:, 0:n], t_sg[:, 0:n], oml[:, sc:sc + 1], lb[:, sc:sc + 1], ALU.mult, ALU.add,
                     reads=[t_sg, oml, lb], writes=[t_f])
                t_lg = lg.next()
                S.act(t_lg[:, 0:n], t_f[:, 0:n], AF.Ln, reads=[t_f], writes=[t_lg])
                t_k0 = k0.next()
                S.ts('pool', t_k0[:, 0:n], t_f[:, 0:n], -1.0, 1.0, ALU.mult, ALU.add, reads=[t_f], writes=[t_k0])
                t_cum = cum.next()
                S.op('dve', lambda e, t_cum=t_cum, t_lg=t_lg, n=n: e.tensor_tensor_scan(
                    t_cum[:, 0:n], rm[:, 0:n], t_lg[:, 0:n], 0.0, ALU.mult, ALU.add), reads=[rm, t_lg], writes=[t_cum])
                t_e1 = e1.next()
                S.act(t_e1[:, 0:n], t_cum[:, 0:n], AF.Exp, reads=[t_cum], writes=[t_e1])
                t_e2 = e2.next()
                S.act(t_e2[:, 0:n], t_cum[:, 0:n], AF.Exp, reads=[t_cum], writes=[t_e2], scale=-1.0)
                t_d = dcy.next()
                S.copy('dve', t_d[:, 0:ncnk], t_e1[:, HC - 1:n:HC], reads=[t_e1], writes=[t_d])
                t_qq = qq.next()
                S.stt('dve', t_qq[:, 0:n], a_q[:, 0:n], scale, t_e1[:, 0:n], ALU.mult, ALU.mult, reads=[a_q, t_e1], writes=[t_qq])
                t_kkf = kkf.next()
                S.tt('pool', t_kkf[:, 0:n], t_k0[:, 0:n], t_e2[:, 0:n], ALU.mult, reads=[t_k0, t_e2], writes=[t_kkf])
                t_kk = kk.next()
                S.copy('pool', t_kk[:, 0:n], t_kkf[:, 0:n], reads=[t_kkf], writes=[t_kk])
                t_kk2 = kk2.next()
                S.tt('dve', t_kk2[:, 0:n].rearrange("p (c s) -> p c s", s=HC), t_kkf[:, 0:n].rearrange("p (c s) -> p c s", s=HC),
                     t_d[:, 0:ncnk].unsqueeze(2).broadcast_to([128, ncnk, HC]), ALU.mult, reads=[t_kkf, t_d], writes=[t_kk2])
                t_o = ost.next()
                blk.append((b_v, t_qq, t_kk, t_kk2, t_d, t_o))
            for c in range(ncnk):
                cs_ = slice(c * HC, (c + 1) * HC)
                for sc in range(4):
                    (b_v, t_qq, t_kk, t_kk2, t_d, t_o) = blk[sc]
                    psA = cx.ps.next()
                    S.mm(psA[0:HC, 0:HC], t_kk[:, cs_], t_qq[:, cs_], True, True, reads=[t_kk, t_qq], writes=[psA])
                    t_am = am.next()
                    S.tt('dve', t_am[:, :], psA[0:HC, 0:HC], ut[:, :], ALU.mult, reads=[psA, ut], writes=[t_am])
                    psT = cx.ps.next()
                    S.mm(psT[0:HC, 0:128], t_kk2[:, cs_], idb[:, :], True, True, reads=[t_kk2, idb], writes=[psT])
                    t_k2t = k2t.next()
                    S.copy('act', t_k2t[:, :], psT[0:HC, 0:128], reads=[psT], writes=[t_k2t])
                for sc in range(4):
                    (b_v, t_qq, t_kk, t_kk2, t_d, t_o) = blk[sc]
                    t_am = am.bufs[(am.i - 4 + sc) % len(am.bufs)]
                    t_k2t = k2t.bufs[(k2t.i - 4 + sc) % len(k2t.bufs)]
                    psO = cx.ps.next()
                    S.mm(psO[:, 0:HC], b_v[:, c, :], t_am[:, :], True, False, reads=[b_v, t_am], writes=[psO], inc=False)
                    S.mm(psO[:, 0:HC], sb_cur[sc][:, :], t_qq[:, cs_], False, True, reads=[sb_cur[sc], t_qq], writes=[psO], inc=True)
                    S.copy('act', t_o[:, cs_], psO[:, 0:HC], reads=[psO], writes=[t_o])
                    psS = cx.ps.next()
                    S.mm(psS[:, 0:128], t_k2t[:, :], b_v[:, c, :], True, True, reads=[t_k2t, b_v], writes=[psS])
                    s_new = Sf[sc].next()
                    S.stt('dve', s_new[:, :], s_cur[sc][:, :], t_d[:, c:c + 1], psS[:, 0:128], ALU.mult, ALU.add,
                          reads=[s_cur[sc], t_d, psS], writes=[s_new])
                    s_cur[sc] = s_new
                    sb_cur[sc] = Sb[sc].next()
                    S.copy('pool', sb_cur[sc][:, :], s_cur[sc][:, :], reads=[s_cur[sc]], writes=[sb_cur[sc]])
            for sc in range(4):
                S.dma('pool', o_d[sc, :, s0:s0 + n], blk[sc][5][:, 0:n], reads=[blk[sc][5]])
        cx.finish()
    return nc


def build_hgrn_post():
    nc = new_nc()
    of_d = din(nc, "of", [D, T])
    ob_d = din(nc, "ob", [D, T])
    g_d = din(nc, "gate", [D, T])
    xT_d = din(nc, "xT", [D, T])
    mod_d = din(nc, "mod", [128, 6, 8, 2])
    gain_d = din(nc, "gain", [128, 8])
    og_d = din(nc, "ogain", [128, 1])
    w_out_d = din(nc, "w_out", [D, D])
    x1_d = dout(nc, "x1", [D, T])
    h2_d = dout(nc, "h2", [D, T], BF16)
    with ExitStack() as es:
        cx = Ctx(nc, es)
        S = cx.S
        mod = S.sbuf(es, "mod", [128, 6, 8, 2], F32)
        S.dma('sp', mod[:], mod_d[:, :, :, :], writes=[mod])
        og = S.sbuf(es, "og", [128, 1], F32)
        S.dma('sp', og[:], og_d[:, :], writes=[og])
        with ExitStack() as es2:
            Z = S.sbuf(es2, "Z", [128, 8, T], BF16)
            with ExitStack() as es3:
                def mk(name, dt, k=2):
                    return Ring([S.sbuf(es3, "%s%d" % (name, i), [128, 512], dt) for i in range(k)])
                a_of, a_ob, a_g = mk("iof", F32), mk("iob", F32), mk("ig", F32)
                t_o, t_sq, t_rs, t_rd, t_zn, t_sg = mk("to", F32), mk("tsq", F32), mk("trs", F32), mk("trd", F32), mk("tzn", F32), mk("tsg", F32)
                for c in range(8):
                    for (c0, n, j) in TILES:
                        x_of = a_of.next()
                        S.dma('sp', x_of[:, 0:n], of_d[c * 128:(c + 1) * 128, c0:c0 + n], writes=[x_of])
                        x_ob = a_ob.next()
                        S.dma('sp', x_ob[:, 0:n], ob_d[c * 128:(c + 1) * 128, c0:c0 + n], writes=[x_ob])
                        x_g = a_g.next()
                        S.dma('sp', x_g[:, 0:n], g_d[c * 128:(c + 1) * 128, c0:c0 + n], writes=[x_g])
                        o = t_o.next()
                        S.tt('pool', o[:, 0:n], x_of[:, 0:n], x_ob[:, 0:n], ALU.add, reads=[x_of, x_ob], writes=[o])
                        sq = t_sq.next()
                        S.act(sq[:, 0:n], o[:, 0:n], AF.Square, reads=[o], writes=[sq])
                        ps = cx.ps.next()
                        S.mm(ps[:, 0:n], cx.ones[:, :], sq[:, 0:n], True, True, reads=[cx.ones, sq], writes=[ps])
                        r = t_rs.next()
                        S.act(r[:, 0:n], ps[:, 0:n], AF.Sqrt, reads=[ps, cx.eps], writes=[r], bias=cx.eps[:, 0:1], scale=1.0 / 128)
                        rd = t_rd.next()
                        S.op('dve', lambda e, rd=rd, r=r, n=n: e.reciprocal(rd[:, 0:n], r[:, 0:n]), reads=[r], writes=[rd])
                        zn = t_zn.next()
                        S.stt('dve', zn[:, 0:n], o[:, 0:n], og[:, 0:1], rd[:, 0:n], ALU.mult, ALU.mult, reads=[o, og, rd], writes=[zn])
                        sg = t_sg.next()
                        S.act(sg[:, 0:n], x_g[:, 0:n], AF.Silu, reads=[x_g], writes=[sg])
                        S.tt('pool', Z[:, c, c0:c0 + n], zn[:, 0:n], sg[:, 0:n], ALU.mult, reads=[zn, sg], writes=[Z])
                S.barrier()
                S.flush()
            emit_post(cx, es2, Z, 128, D, w_out_d, xT_d, x1_d, h2_d, mod, gain_d)
        cx.finish()
    return nc


def launch_hgrn(xT, mods, lbv, inp, i, j):
    nc = build_hgrn_pre()
    gain = fm(inp['norm_mix'][i])
    w_in = np.ascontiguousarray(inp['hgrn_w_in'][j])
    res = run(nc, [dict(xT=xT[r], mod=mods[i][r], gain=gain, w_in=w_in) for r in range(NCORES)])
    P = [np.asarray(o['p']) for o in res]
    Pfull = [np.concatenate([P[4 * b][:, TL:]] + [P[4 * b + q][:, :TL] for q in range(4)], axis=1) for b in range(NB)]
    idx_b = np.concatenate([np.arange(CTX - 1, -1, -1), CTX + np.arange(SEQ - 1, -1, -1)])
    rmask = np.ones((128, 512), np.float32)
    rmask[:, ::HC] = 0.0
    ut = np.triu(np.ones((HC, HC), np.float32))
    ident = np.eye(128, dtype=np.float32)
    maps = []
    for r in range(NCORES):
        qs, fs, vs, lbs = [], [], [], []
        for pp in range(2):
            p = 2 * r + pp
            b, hd = p // 8, p % 8
            rows = lambda k: slice(k * 1024 + hd * 128, k * 1024 + (hd + 1) * 128)
            qf = Pfull[b][rows(0)]
            vf = Pfull[b][rows(1)].T
            for d in range(2):
                fpre = Pfull[b][rows(2 + d)]
                if d == 1:
                    qd, fd, vd = qf[:, idx_b], fpre[:, idx_b], vf[idx_b]
                else:
                    qd, fd, vd = qf, fpre, vf
                qs.append(qd)
                fs.append(fd)
                vs.append(vd.reshape(NS // HC, HC, 128).transpose(1, 0, 2).reshape(HC, (NS // HC) * 128))
                lbs.append(lbv[:, hd])
        maps.append(dict(q=np.ascontiguousarray(np.stack(qs)), f=np.ascontiguousarray(np.stack(fs)),
                         v=np.ascontiguousarray(np.stack(vs)), lb=np.ascontiguousarray(np.stack(lbs, axis=1)),
                         rmask=rmask, ut=ut, ident=ident))
    nc = build_hgrn_scan()
    res = run(nc, maps)
    O = [np.asarray(o['o']) for o in res]
    of_full = [np.zeros((D, NS), np.float32) for _ in range(NB)]
    ob_full = [np.zeros((D, NS), np.float32) for _ in range(NB)]
    for r in range(NCORES):
        for pp in range(2):
            p = 2 * r + pp
            b, hd = p // 8, p % 8
            of_full[b][hd * 128:(hd + 1) * 128] = O[r][2 * pp]
            ob_full[b][hd * 128:(hd + 1) * 128] = O[r][2 * pp + 1][:, idx_b]
    nc = build_hgrn_post()
    gain2 = fm(inp['norm_ffn'][i])
    og = np.ascontiguousarray(inp['hgrn_out_norm'][j].reshape(128, 1))
    w_out = np.ascontiguousarray(inp['hgrn_w_out'][j])
    maps = []
    for r in range(NCORES):
        b, t0, t1 = core_tok(r)
        sel = lambda a: np.ascontiguousarray(np.concatenate([a[:, CTX + t0:CTX + t1], a[:, :CTX]], axis=1))
        maps.append(dict(of=sel(of_full[b]), ob=sel(ob_full[b]), gate=np.ascontiguousarray(P[r][4096:5120]), xT=xT[r],
                         mod=mods[i][r], gain=gain2, ogain=og, w_out=w_out))
    res = run(nc, maps)
    return [o['x1'] for o in res], [o['h2'] for o in res]


def kernel(**inputs):
    inp = {k: np.asarray(v) for k, v in inputs.items()}
    mods, lbv = launch_mod(inp)
    xT = host_xT(inp['x'], inp['ctx'])
    y = None
    for i in range(DEPTH):
        kind, j = i % 3, i // 3
        if kind == 0:
            qkv = launch_gqa_pre(xT, mods, inp, i, j)
            x1, h2 = launch_gqa_attn(qkv, xT, mods, inp, i, j)
        elif kind == 1:
            x1, h2 = launch_hgrn(xT, mods, lbv, inp, i, j)
        else:
            q, kv, kr = launch_mla_pre(xT, mods, inp, i, j)
            x1, h2 = launch_mla_attn(q, kv, kr, xT, mods, inp, i, j)
        xT, y = launch_ffn(h2, x1, mods, inp, i, i == DEPTH - 1)
    out = np.empty((NB, SEQ, D), np.float32)
    for r in range(NCORES):
        b, t0, t1 = core_tok(r)
        out[b, t0:t1, :] = np.asarray(y[r]).T
    return out
```
